# Optimizing a Trainium2 kernel written in Bass

```python
import jax, jax.numpy as jnp
from jax import lax
import numpy as np

D_MODEL = 1024
BATCH = 8
SEQ = 2048
DEPTH = 1
DEC_BATCH = 32
DEC_SEQ = 32
PAST_LEN = 4096

CHUNK = 64
Q_BLOCK = 128
A_HEADS = 8
A_KV_HEADS = 2
A_HEAD_DIM = 64
A_Q = A_HEADS * A_HEAD_DIM
A_KV = A_KV_HEADS * A_HEAD_DIM
IDX_HEADS = 16
IDX_DIM = 64
IDX_Q = IDX_HEADS * IDX_DIM
TOPK_MAX = 256
B_HEADS = 8
B_HEAD_DIM = 64
B_WIDTH = B_HEADS * B_HEAD_DIM
DECAY_LORA = 64
AAA_LORA = 64
GATE_LORA = 128
RWKV_SPLITS = (B_WIDTH, B_WIDTH, B_WIDTH, DECAY_LORA, AAA_LORA, GATE_LORA)
SHIFT_W = 3 * B_WIDTH + DECAY_LORA + AAA_LORA + GATE_LORA
IN_SPLITS = (A_Q, A_KV, A_KV, IDX_Q, IDX_DIM, IDX_HEADS, SHIFT_W, D_MODEL, D_MODEL)
IN_WIDTH = A_Q + 2 * A_KV + IDX_Q + IDX_DIM + IDX_HEADS + SHIFT_W + 2 * D_MODEL
D_FF = -(-8 * D_MODEL // (3 * 256)) * 256
RMS_EPS = 1e-6
LN_EPS = 1e-6
GN_EPS = 64e-5

kernel_name = 'chunk_stream_dsa_rwkv7_hybrid'


def _split(z, sizes):
    return jnp.split(z, np.cumsum(sizes)[:-1].tolist(), axis=-1)


def _rms_norm(x, g):
    xf = x.astype(jnp.float32)
    y = xf * lax.rsqrt(jnp.mean(xf * xf, axis=-1, keepdims=True) + RMS_EPS)
    return (y * g.astype(jnp.float32)).astype(x.dtype)


def _layer_norm(x, g, b):
    xf = x.astype(jnp.float32)
    mu = jnp.mean(xf, axis=-1, keepdims=True)
    var = jnp.mean(jnp.square(xf - mu), axis=-1, keepdims=True)
    y = (xf - mu) * lax.rsqrt(var + LN_EPS)
    return (y * g.astype(jnp.float32) + b.astype(jnp.float32)).astype(x.dtype)


def _dsa_attend(q, iq, iw, k_all, v_all, ik_all, adm, n_sel):
    f32 = jnp.float32
    B, T = q.shape[0], q.shape[1]
    iqh = iq.reshape(B, T, IDX_HEADS, IDX_DIM).astype(f32)
    logits = jnp.einsum('bthd,bsd->bths', iqh, ik_all.astype(f32)) * (IDX_DIM ** -0.5)
    score = jnp.einsum('bths,bth->bts', jax.nn.relu(logits), iw.astype(f32) * (IDX_HEADS ** -0.5))
    score = jnp.where(adm[None], score, -jnp.inf)
    top_val, top_idx = lax.top_k(score, n_sel)
    valid = jnp.isfinite(top_val)
    gather = jax.vmap(lambda rows, idx: rows[idx])
    k_sel = gather(k_all, top_idx).astype(f32)
    v_sel = gather(v_all, top_idx).astype(f32)
    qh = q.reshape(B, T, A_KV_HEADS, A_HEADS // A_KV_HEADS, A_HEAD_DIM).astype(f32)
    s = jnp.einsum('btkgd,btskd->btkgs', qh, k_sel) * (A_HEAD_DIM ** -0.5)
    s = jnp.where(valid[:, :, None, None, :], s, -jnp.inf)
    p = jax.nn.softmax(s, axis=-1)
    o = jnp.einsum('btkgs,btskd->btkgd', p, v_sel)
    return o.reshape(B, T, A_Q).astype(q.dtype)


def _prompt_attention(q, iq, iw, k, v, ik):
    B, S = q.shape[0], q.shape[1]
    n_sel = min(TOPK_MAX, S // 4)
    key_pos = jnp.arange(S)

    def one_block(i):
        t0 = i * Q_BLOCK
        sl = lambda a: lax.dynamic_slice_in_dim(a, t0, Q_BLOCK, axis=1)
        q_pos = t0 + jnp.arange(Q_BLOCK)
        adm = key_pos[None, :] < (q_pos[:, None] // CHUNK + 1) * CHUNK
        return _dsa_attend(sl(q), sl(iq), sl(iw), k, v, ik, adm, n_sel)

    o = lax.map(one_block, jnp.arange(S // Q_BLOCK))
    return jnp.moveaxis(o, 0, 1).reshape(B, S, A_Q)


def _sample_attention(cache_k, cache_v, cache_ik):
    def attend(q, iq, iw, k, v, ik):
        k_all = jnp.concatenate([cache_k.astype(k.dtype), k], axis=1)
        v_all = jnp.concatenate([cache_v.astype(v.dtype), v], axis=1)
        ik_all = jnp.concatenate([cache_ik.astype(ik.dtype), ik], axis=1)
        T, L = q.shape[1], k_all.shape[1]
        adm = jnp.ones((T, L), dtype=bool)
        return _dsa_attend(q, iq, iw, k_all, v_all, ik_all, adm, min(TOPK_MAX, L // 4))
    return attend


def _rwkv_time_mix(u, shift_prev, wkv0, p):
    f32 = jnp.float32
    B, T = u.shape[0], u.shape[1]
    uf = u.astype(f32)
    u_prev = jnp.concatenate([shift_prev.astype(f32), uf[:, :-1]], axis=1)
    m = uf + (u_prev - uf) * p['shift_mu'].astype(f32)
    r, k, v, wl, al, gl = _split(m, RWKV_SPLITS)
    w_log = -jax.nn.softplus(-(p['w0'].astype(f32) + jnp.tanh(wl) @ p['w2'].astype(f32))) - 0.5
    decay = jnp.exp(-jnp.exp(w_log))
    a = jax.nn.sigmoid(p['a0'].astype(f32) + al @ p['a2'].astype(f32))
    g = jax.nn.sigmoid(gl) @ p['g2'].astype(f32)
    heads = lambda z: z.reshape(B, T, B_HEADS, B_HEAD_DIM)
    kk = heads(k * p['k_k'].astype(f32))
    kk = kk / jnp.maximum(jnp.sqrt(jnp.sum(kk * kk, axis=-1, keepdims=True)), 1e-12)
    k = k * (1.0 + (a - 1.0) * p['k_a'].astype(f32))
    rh, wh, kh, vh, ah = heads(r), heads(decay), heads(k), heads(v), heads(a)
    a_vec, b_vec = -kk, kk * ah

    def step(S, inp):
        r_t, w_t, k_t, v_t, a_t, b_t = inp
        sa = jnp.einsum('bhij,bhj->bhi', S, a_t)
        S = S * w_t[:, :, None, :] + sa[..., None] * b_t[:, :, None, :] + v_t[..., None] * k_t[:, :, None, :]
        return S, jnp.einsum('bhij,bhj->bhi', S, r_t)

    tm = lambda z: jnp.moveaxis(z, 1, 0)
    s_fin, y = lax.scan(step, wkv0.astype(f32), (tm(rh), tm(wh), tm(kh), tm(vh), tm(a_vec), tm(b_vec)))
    y = jnp.moveaxis(y, 0, 1)
    mu = jnp.mean(y, axis=-1, keepdims=True)
    var = jnp.mean(jnp.square(y - mu), axis=-1, keepdims=True)
    y = ((y - mu) * lax.rsqrt(var + GN_EPS)).reshape(B, T, B_WIDTH)
    y = y * p['gn_w'].astype(f32) + p['gn_b'].astype(f32)
    bonus = jnp.sum(rh * kh * p['r_k'].astype(f32).reshape(B_HEADS, B_HEAD_DIM), axis=-1, keepdims=True) * vh
    y = (y + bonus.reshape(B, T, B_WIDTH)) * g
    return y.astype(u.dtype), u[:, -1:], s_fin.astype(u.dtype)


def _layer(x, attend, shift_prev, wkv0, p):
    B, T = x.shape[0], x.shape[1]
    xn = _rms_norm(x, p['norm1'])
    q, k, v, iq, ik, iw, u, g_a, g_b = _split(xn @ p['w_in'], IN_SPLITS)
    k = k.reshape(B, T, A_KV_HEADS, A_HEAD_DIM)
    v = v.reshape(B, T, A_KV_HEADS, A_HEAD_DIM)
    ik = _layer_norm(ik, p['idx_k_g'], p['idx_k_b'])
    o_a = attend(q, iq, iw, k, v, ik)
    y_b, shift_new, wkv_new = _rwkv_time_mix(u, shift_prev, wkv0, p)
    merged = jax.nn.sigmoid(g_a) * (o_a @ p['w_oa']) + jax.nn.sigmoid(g_b) * (y_b @ p['w_ob'])
    h = x + merged @ p['w_out']
    hn = _rms_norm(h, p['norm2'])
    h = h + (jax.nn.silu(hn @ p['w_gate']) * (hn @ p['w_up'])) @ p['w_down']
    return h, k, v, ik, wkv_new, shift_new


def setup_inputs(seed: int = 0) -> dict:
    key = jax.random.key(seed)
    ks = jax.random.split(key, 32)
    nrm = lambda kk, shape, scale: scale * jax.random.normal(kk, shape, jnp.float32)
    L = DEPTH
    return {
        'x_prompt': nrm(ks[0], (BATCH, SEQ, D_MODEL), 1.0),
        'x_sample': nrm(ks[1], (DEC_BATCH, DEC_SEQ, D_MODEL), 1.0),
        'cache_k': nrm(ks[2], (L, DEC_BATCH, PAST_LEN, A_KV_HEADS, A_HEAD_DIM), 1.0),
        'cache_v': nrm(ks[3], (L, DEC_BATCH, PAST_LEN, A_KV_HEADS, A_HEAD_DIM), 1.0),
        'cache_idx_k': nrm(ks[4], (L, DEC_BATCH, PAST_LEN, IDX_DIM), 1.0),
        'state_wkv': nrm(ks[5], (L, DEC_BATCH, B_HEADS, B_HEAD_DIM, B_HEAD_DIM), 0.3),
        'state_shift': nrm(ks[6], (L, DEC_BATCH, 1, SHIFT_W), 1.0),
        'norm1': 1.0 + nrm(ks[7], (L, D_MODEL), 0.02),
        'w_in': nrm(ks[8], (L, D_MODEL, IN_WIDTH), D_MODEL ** -0.5),
        'idx_k_g': 1.0 + nrm(ks[9], (L, IDX_DIM), 0.02),
        'idx_k_b': nrm(ks[10], (L, IDX_DIM), 0.02),
        'shift_mu': jax.random.uniform(ks[11], (L, SHIFT_W), jnp.float32),
        'w0': nrm(ks[12], (L, B_WIDTH), 0.5),
        'w2': nrm(ks[13], (L, DECAY_LORA, B_WIDTH), DECAY_LORA ** -0.5),
        'a0': nrm(ks[14], (L, B_WIDTH), 0.5),
        'a2': nrm(ks[15], (L, AAA_LORA, B_WIDTH), AAA_LORA ** -0.5),
        'g2': nrm(ks[16], (L, GATE_LORA, B_WIDTH), GATE_LORA ** -0.5),
        'k_k': 0.85 + nrm(ks[17], (L, B_WIDTH), 0.02),
        'k_a': 1.0 + nrm(ks[18], (L, B_WIDTH), 0.02),
        'r_k': nrm(ks[19], (L, B_WIDTH), 0.1),
        'gn_w': 1.0 + nrm(ks[20], (L, B_WIDTH), 0.02),
        'gn_b': nrm(ks[21], (L, B_WIDTH), 0.02),
        'w_oa': nrm(ks[22], (L, A_Q, D_MODEL), A_Q ** -0.5),
        'w_ob': nrm(ks[23], (L, B_WIDTH, D_MODEL), B_WIDTH ** -0.5),
        'w_out': nrm(ks[24], (L, D_MODEL, D_MODEL), D_MODEL ** -0.5),
        'norm2': 1.0 + nrm(ks[25], (L, D_MODEL), 0.02),
        'w_gate': nrm(ks[26], (L, D_MODEL, D_FF), D_MODEL ** -0.5),
        'w_up': nrm(ks[27], (L, D_MODEL, D_FF), D_MODEL ** -0.5),
        'w_down': nrm(ks[28], (L, D_FF, D_MODEL), D_FF ** -0.5),
        'norm_f': 1.0 + nrm(ks[29], (D_MODEL,), 0.02),
    }


def reference(x_prompt, x_sample, cache_k, cache_v, cache_idx_k, state_wkv, state_shift,
              norm1, w_in, idx_k_g, idx_k_b, shift_mu, w0, w2, a0, a2, g2, k_k, k_a, r_k,
              gn_w, gn_b, w_oa, w_ob, w_out, norm2, w_gate, w_up, w_down, norm_f):
    params = {'norm1': norm1, 'w_in': w_in, 'idx_k_g': idx_k_g, 'idx_k_b': idx_k_b,
              'shift_mu': shift_mu, 'w0': w0, 'w2': w2, 'a0': a0, 'a2': a2, 'g2': g2,
              'k_k': k_k, 'k_a': k_a, 'r_k': r_k, 'gn_w': gn_w, 'gn_b': gn_b,
              'w_oa': w_oa, 'w_ob': w_ob, 'w_out': w_out, 'norm2': norm2,
              'w_gate': w_gate, 'w_up': w_up, 'w_down': w_down}
    n_prompt = x_prompt.shape[0]
    hp, hs = x_prompt, x_sample
    outs_p, outs_s = [], []
    for l in range(DEPTH):
        p = {name: arr[l] for name, arr in params.items()}
        shift0 = jnp.zeros((n_prompt, 1, SHIFT_W), x_prompt.dtype)
        wkv_zero = jnp.zeros((n_prompt, B_HEADS, B_HEAD_DIM, B_HEAD_DIM), jnp.float32)
        hp, *st_p = _layer(hp, _prompt_attention, shift0, wkv_zero, p)
        hs, *st_s = _layer(hs, _sample_attention(cache_k[l], cache_v[l], cache_idx_k[l]),
                           state_shift[l], state_wkv[l], p)
        outs_p.append(st_p)
        outs_s.append(st_s)
    stk = lambda outs, i: jnp.stack([o[i] for o in outs], axis=0)
    y_prompt = _rms_norm(hp, norm_f)
    y_sample = _rms_norm(hs, norm_f)
    k_p, v_p, ik_p, wkv_p, shift_p = stk(outs_p, 0), stk(outs_p, 1), stk(outs_p, 2), stk(outs_p, 3), stk(outs_p, 4)
    k_s, v_s, ik_s, wkv_s, shift_s = stk(outs_s, 0), stk(outs_s, 1), stk(outs_s, 2), stk(outs_s, 3), stk(outs_s, 4)
    return (y_prompt, y_sample, k_p, v_p, ik_p, wkv_p, shift_p, k_s, v_s, ik_s, wkv_s, shift_s)
```

```python
import numpy as np
import ml_dtypes
from contextlib import ExitStack
import concourse.bass as bass
import concourse.mybir as mybir
from concourse.bass_utils import run_bass_kernel_spmd

F32 = mybir.dt.float32
BF16 = mybir.dt.bfloat16
AF = mybir.ActivationFunctionType
ALU = mybir.AluOpType
AX = mybir.AxisListType

D = 1024
NT = 17
TOK = NT * 128
NPT = 16
SB = 4
PAST = 4096
LS = PAST + 32
U_W = 1792
NEG = -1.0e30
DEBUG = {}
import os
CUT = float(os.environ.get('KCUT', '99'))
SKIP = os.environ.get('KSKIP', '')
DBG_TILE = int(os.environ.get('KDT', '5'))


class T:
    def __init__(self, t, name="", psum=False):
        self.t = t
        self.w = None
        self.r = []
        self.name = name
        self.psum = psum

    def __getitem__(self, k):
        return self.t[k]

    def sub(self):
        return T(self.t, self.name + "_sub")


class Prog:
    ENG = ("pe", "act", "dve", "pool", "sp")

    def __init__(self, nc, es, n_dma_sems=10):
        self.nc = nc
        self.q = {e: [] for e in self.ENG}
        self.cnt = {e: 0 for e in self.ENG}
        self.seen = {e: {} for e in self.ENG}
        self.sem = {e: es.enter_context(nc.semaphore("s_" + e)) for e in self.ENG}
        self.dq = {}
        for qn in ("sp", "act", "pool"):
            sems = [es.enter_context(nc.semaphore(f"d_{qn}_{i}")) for i in range(n_dma_sems)]
            self.dq[qn] = {"sems": sems, "n": [0] * n_dma_sems, "next": 0}
        self.semobj = dict(self.sem)
        for qn, d in self.dq.items():
            for i, s in enumerate(d["sems"]):
                self.semobj[(qn, i)] = s
        self.ninst = 0

    def _wait(self, eng, key, val):
        if val <= 0 or self.seen[eng].get(key, 0) >= val:
            return
        self.seen[eng][key] = val
        s = self.semobj[key]
        self.q[eng].append(lambda e, s=s, val=val: e.wait_ge(s, val))

    def _deps(self, eng, reads, writes):
        toks = []
        for r in reads:
            if r.w is not None:
                toks.append((r.w, True))
            if r.psum:
                for t in r.r:
                    toks.append((t, False))
        for w in writes:
            if w.w is not None:
                toks.append((w.w, True))
            for t in w.r:
                toks.append((t, False))
        for (key, val), raw in toks:
            if key == eng and (eng == "pe" or not raw):
                continue
            self._wait(eng, key, val)

    def _rec(self, tok, reads, writes):
        for r in reads:
            r.r.append(tok)
        for w in writes:
            w.w = tok
            w.r = []
        self.ninst += 1

    def op(self, eng, name, reads=(), writes=(), **kw):
        self._deps(eng, reads, writes)
        self.cnt[eng] += 1
        tok = (eng, self.cnt[eng])
        s = self.sem[eng]
        self.q[eng].append(lambda e, name=name, kw=kw, s=s: getattr(e, name)(**kw).then_inc(s, 1))
        self._rec(tok, reads, writes)
        return tok

    def dma(self, qn, out, in_, reads=(), writes=(), **kw):
        d = self.dq[qn]
        k = d["next"]
        d["next"] = (k + 1) % len(d["sems"])
        key = (qn, k)
        self._wait(qn, key, 16 * d["n"][k])
        self._deps(qn, reads, writes)
        d["n"][k] += 1
        tok = (key, 16 * d["n"][k])
        s = d["sems"][k]
        self.q[qn].append(lambda e, s=s, out=out, in_=in_, kw=kw: e.dma_start(out=out, in_=in_, **kw).then_inc(s, 16))
        self._rec(tok, reads, writes)
        return tok

    def barrier(self):
        for e in self.ENG:
            for qn, d in self.dq.items():
                for k in range(len(d["sems"])):
                    self._wait(e, (qn, k), 16 * d["n"][k])
            for e2 in self.ENG:
                if e2 != e:
                    self._wait(e, e2, self.cnt[e2])

    def finish(self):
        self.barrier()
        q = self.q
        with self.nc.Block() as block:
            @block.sync
            def _(e):
                for f in q["sp"]:
                    f(e)

            @block.tensor
            def _(e):
                for f in q["pe"]:
                    f(e)

            @block.scalar
            def _(e):
                for f in q["act"]:
                    f(e)

            @block.vector
            def _(e):
                for f in q["dve"]:
                    f(e)

            @block.gpsimd
            def _(e):
                for f in q["pool"]:
                    f(e)


IN_OFF = {}
_o = 0
for _n, _w in (("q", 512), ("k", 128), ("v", 128), ("iq", 1024), ("ik", 64), ("iw", 16), ("u", 1792), ("ga", 1024), ("gb", 1024)):
    IN_OFF[_n] = (_o, _w)
    _o += _w

IN_SPECS = [
    ("x", [TOK, D], F32), ("ck", [SB, PAST, 128], F32), ("cv", [SB, PAST, 128], F32), ("cik", [SB, PAST, 64], F32),
    ("swkv", [SB, 8, 64, 64], F32), ("sshift", [SB, U_W], F32),
    ("norm1", [D], F32), ("w_in", [D, 5712], F32), ("idx_k_g", [64], F32), ("idx_k_b", [64], F32),
    ("shift_mu", [U_W], F32), ("w0", [512], F32), ("w2", [64, 512], F32), ("a0", [512], F32), ("a2", [64, 512], F32),
    ("g2", [128, 512], F32), ("k_k", [512], F32), ("k_a", [512], F32), ("r_k", [512], F32), ("gn_w", [512], F32),
    ("gn_b", [512], F32), ("w_oa", [512, D], F32), ("w_ob", [512, D], F32), ("w_out", [D, D], F32), ("norm2", [D], F32),
    ("w_gate", [D, 2816], F32), ("w_up", [D, 2816], F32), ("w_down", [2816, D], F32), ("norm_f", [D], F32),
    ("ident_bf", [128, 128], BF16), ("ident_f", [128, 128], F32), ("tri", [128, 128], F32),
    ("m_su", [128, 128], BF16), ("m_ui", [128, 128], BF16), ("m_sl", [128, 128], BF16), ("elast", [128, 2], F32), ("bsel", [128, 4], F32), ("pow2", [128, 32], F32),
]
OUT_SPECS = [
    ("y", [TOK, D], F32), ("ko", [TOK, 128], F32), ("vo", [TOK, 128], F32), ("iko", [TOK, 64], F32),
    ("wkv_p", [8, 64, 64], F32), ("shift_p", [1, U_W], F32), ("wkv_s", [SB, 8, 64, 64], F32), ("shift_s", [SB, U_W], F32),
]


def build(stop_after="Z", dbg=()):
    nc = bass.Bass("TRN2", target_bir_lowering=False)
    I = {n: nc.dram_tensor(n, s, d, kind="ExternalInput").ap() for n, s, d in IN_SPECS}
    O = {n: nc.dram_tensor(n, s, d, kind="ExternalOutput").ap() for n, s, d in OUT_SPECS}
    DBG = {}
    for n, s in dbg:
        DBG[n] = nc.dram_tensor("dbg_" + n, s, F32, kind="ExternalOutput").ap()
    Od = {n: T(None, n) for n in O}
    with ExitStack() as es:
        P = Prog(nc, es)

        def sb(ctx, name, shape, dt=F32):
            return T(ctx.enter_context(nc.sbuf_tensor("sb_" + name, shape, dt)), name)

        def ps(ctx, name, shape, dt=F32):
            return T(ctx.enter_context(nc.psum_tensor("ps_" + name, shape, dt)), name, psum=True)

        def dbg_out(name, src, rows, cols, t):
            if name in DBG:
                P.dma("sp", DBG[name][0:rows, 0:cols], src, reads=[t])

        xnT = sb(es, "xnT", [128, 8, TOK], BF16)
        xnT_t = [xnT.sub() for _ in range(NT)]
        idb = sb(es, "idb", [128, 128], BF16)
        idf = sb(es, "idf", [128, 128], F32)
        P.dma("sp", idb[:], I["ident_bf"], writes=[idb])
        P.dma("sp", idf[:], I["ident_f"], writes=[idf])

        def load_w(qn, dst, dst_ap, src_ap):
            P.dma(qn, dst_ap, src_ap, writes=[dst])

        def wsl(name, c0, c1):
            return I["w_in"][:, c0:c1].rearrange("(c p) n -> p c n", p=128)

        with ExitStack() as e1:
            yBT = sb(e1, "yBT", [128, 4, TOK], BF16)
            kTs = sb(e1, "kTs", [128, 128], BF16)
            ikTs = sb(e1, "ikTs", [128, 128], BF16)
            vTs = sb(e1, "vTs", [128, 128], BF16)
            kT = sb(e1, "kT", [128, 2048], BF16)
            ikT = sb(e1, "ikT", [128, 2048], BF16)
            vaug = sb(e1, "vaug", [128, 16, 2, 65], BF16)
            iws = sb(e1, "iws", [128, NT, 16], F32)
            ewu = ExitStack()
            wu = sb(ewu, "wu", [128, 8, U_W], BF16)
            o_u = IN_OFF["u"][0]
            for c4 in range(4):
                load_w("pool", wu, wu[:, :, c4 * 448:(c4 + 1) * 448], wsl("u", o_u + c4 * 448, o_u + (c4 + 1) * 448))
            with ExitStack() as ea:
                n1T = sb(ea, "n1T", [128, 8], F32)
                P.dma("sp", n1T[:], I["norm1"].rearrange("(c p) -> p c", p=128), writes=[n1T], allow_slow_non_contiguous=True)
                wkvi = sb(ea, "wkvi", [128, 8, 336], BF16)
                o_k = IN_OFF["k"][0]
                if 'w' not in SKIP:
                    load_w("pool", wkvi, wkvi[:, :, 0:256], wsl("kv", o_k, o_k + 256))
                o_ik = IN_OFF["ik"][0]
                if 'v' not in SKIP:
                    load_w("pool", wkvi, wkvi[:, :, 256:336], wsl("iki", o_ik, o_ik + 80))
                ikg = sb(ea, "ikg", [128, 64], F32)
                ikb = sb(ea, "ikb", [128, 64], F32)
                if 'b' not in SKIP:
                    P.dma("sp", ikg[:], I["idx_k_g"].partition_broadcast(128), writes=[ikg])
                    P.dma("sp", ikb[:], I["idx_k_b"].partition_broadcast(128), writes=[ikb])
                if 'm' not in SKIP:
                    P.op("pool", "memset", ap=vaug[:, :, :, 64:65], constant=1.0, writes=[vaug])
                xt = [sb(ea, f"xt{i}", [128, D], F32) for i in range(2)]
                junk = sb(ea, "junk", [128, D], F32)
                ss = [sb(ea, f"ss{i}", [128, 2], F32) for i in range(2)]
                xs = [sb(ea, f"xs{i}", [128, D], BF16) for i in range(2)]
                kvf = [sb(ea, f"kvf{i}", [128, 256], F32) for i in range(2)]
                ikf = [sb(ea, f"ikf{i}", [128, 64], F32) for i in range(2)]
                st6 = sb(ea, "st6", [128, 6], F32)
                mv = sb(ea, "mv", [128, 4], F32)
                kvb = [sb(ea, f"kvb{i}", [128, 384], BF16) for i in range(2)]
                pT = [ps(ea, f"pT{i}", [128, 8, 128], BF16) for i in range(2)]
                pkv = [ps(ea, f"pkv{i}", [128, 512], F32) for i in range(2)]
                pk2 = [ps(ea, f"pk2{i}", [128, 8, 128], BF16) for i in range(2)]
                def a1(i):
                    b = i % 2
                    P.dma("sp", xt[b][:], I["x"][i * 128:(i + 1) * 128, :], writes=[xt[b]])
                    P.op("act", "activation", out=junk[:], in_=xt[b][:], func=AF.Square, accum_out=ss[b][:, 0:1], reads=[xt[b]], writes=[junk, ss[b]])
                    P.op("act", "activation", out=ss[b][:, 1:2], in_=ss[b][:, 0:1], func=AF.Sqrt, scale=1.0 / D, bias=1e-6, reads=[ss[b]], writes=[ss[b]])
                    P.op("dve", "reciprocal", out=ss[b][:, 1:2], in_=ss[b][:, 1:2], reads=[ss[b]], writes=[ss[b]])
                    P.op("dve", "tensor_scalar", out=xs[b][:], in0=xt[b][:], scalar1=ss[b][:, 1:2], scalar2=None, op0=ALU.mult, reads=[xt[b], ss[b]], writes=[xs[b]])
                    for c in range(8):
                        P.op("pe", "transpose", out=pT[b][:, c, :], in_=xs[b][:, c * 128:(c + 1) * 128], identity=idb[:], reads=[xs[b], idb], writes=[pT[b]])
                    P.op("dve", "tensor_tensor", out=xnT[:, :, i * 128:(i + 1) * 128], in0=pT[b][:], in1=n1T[:].unsqueeze(2).to_broadcast([128, 8, 128]), op=ALU.mult,
                         reads=[pT[b], n1T], writes=[xnT_t[i]])

                def a2(i):
                    b = i % 2
                    for c in range(8):
                        P.op("pe", "matmul", out=pkv[b][:, 0:336], lhsT=xnT[:, c, i * 128:(i + 1) * 128], rhs=wkvi[:, c, :], start=(c == 0), stop=(c == 7),
                             reads=[xnT_t[i], wkvi], writes=[pkv[b]])
                    P.op("act", "copy", out=kvf[b][:], in_=pkv[b][:, 0:256], reads=[pkv[b]], writes=[kvf[b]])
                    P.dma("sp", O["ko"][i * 128:(i + 1) * 128, :], kvf[b][:, 0:128], reads=[kvf[b]])
                    P.dma("sp", O["vo"][i * 128:(i + 1) * 128, :], kvf[b][:, 128:256], reads=[kvf[b]])
                    P.op("dve", "tensor_scalar", out=iws[:, i, :], in0=pkv[b][:, 320:336], scalar1=1.0 / 32, scalar2=None, op0=ALU.mult, reads=[pkv[b]], writes=[iws])
                    P.op("dve", "tensor_reduce", out=mv[:, 0:1], in_=pkv[b][:, 256:320], axis=AX.X, op=ALU.add, reads=[pkv[b]], writes=[mv])
                    P.op("dve", "tensor_scalar", out=mv[:, 0:1], in0=mv[:, 0:1], scalar1=1.0 / 64, scalar2=None, op0=ALU.mult, reads=[mv], writes=[mv])
                    P.op("dve", "tensor_scalar", out=ikf[b][:], in0=pkv[b][:, 256:320], scalar1=mv[:, 0:1], scalar2=None, op0=ALU.subtract, reads=[pkv[b], mv], writes=[ikf[b]])
                    P.op("act", "activation", out=junk[:, 0:64], in_=ikf[b][:], func=AF.Square, accum_out=mv[:, 1:2], reads=[ikf[b]], writes=[junk, mv])
                    P.op("act", "activation", out=mv[:, 2:3], in_=mv[:, 1:2], func=AF.Sqrt, scale=1.0 / 64, bias=1e-6, reads=[mv], writes=[mv])
                    P.op("dve", "reciprocal", out=mv[:, 3:4], in_=mv[:, 2:3], reads=[mv], writes=[mv])
                    P.op("dve", "tensor_scalar", out=ikf[b][:], in0=ikf[b][:], scalar1=mv[:, 3:4], scalar2=None, op0=ALU.mult, reads=[ikf[b], mv], writes=[ikf[b]])
                    P.op("dve", "tensor_tensor", out=ikf[b][:], in0=ikf[b][:], in1=ikg[:], op=ALU.mult, reads=[ikf[b], ikg], writes=[ikf[b]])
                    P.op("dve", "tensor_tensor", out=ikf[b][:], in0=ikf[b][:], in1=ikb[:], op=ALU.add, reads=[ikf[b], ikb], writes=[ikf[b]])
                    P.dma("sp", O["iko"][i * 128:(i + 1) * 128, :], ikf[b][:], reads=[ikf[b]])

                def a3(i):
                    b = i % 2
                    P.op("act", "copy", out=kvb[b][:, 0:128], in_=kvf[b][:, 0:128], reads=[kvf[b]], writes=[kvb[b]])
                    P.op("pool", "tensor_copy", out=kvb[b][:, 128:192], in_=ikf[b][:], reads=[ikf[b]], writes=[kvb[b]])
                    P.op("pool", "tensor_copy", out=kvb[b][:, 192:256], in_=ikf[b][:], reads=[ikf[b]], writes=[kvb[b]])
                    P.op("act", "copy", out=kvb[b][:, 256:384], in_=kvf[b][:, 128:256], reads=[kvf[b]], writes=[kvb[b]])
                    nT = 2 if i < NPT else 3
                    for c in range(nT):
                        P.op("pe", "transpose", out=pk2[b][:, c, :], in_=kvb[b][:, c * 128:(c + 1) * 128], identity=idb[:], reads=[kvb[b], idb], writes=[pk2[b]])
                    if i < NPT:
                        P.op("act", "copy", out=kT[:, i * 128:(i + 1) * 128], in_=pk2[b][:, 0, :], reads=[pk2[b]], writes=[kT])
                        P.op("act", "copy", out=ikT[:, i * 128:(i + 1) * 128], in_=pk2[b][:, 1, :], reads=[pk2[b]], writes=[ikT])
                        P.op("pool", "tensor_copy", out=vaug[:, i, :, 0:64], in_=kvb[b][:, 256:384].rearrange("p (k d) -> p k d", k=2), reads=[kvb[b]], writes=[vaug])
                    else:
                        P.op("act", "copy", out=kTs[:], in_=pk2[b][:, 0, :], reads=[pk2[b]], writes=[kTs])
                        P.op("act", "copy", out=ikTs[:], in_=pk2[b][:, 1, :], reads=[pk2[b]], writes=[ikTs])
                        P.op("act", "copy", out=vTs[:], in_=pk2[b][:, 2, :], reads=[pk2[b]], writes=[vTs])

                for step in range(NT + 2):
                    if step < NT:
                        a1(step)
                    if 1 <= step < NT + 1:
                        a2(step - 1)
                    if step >= 2:
                        a3(step - 2)
                P.barrier()
            if stop_after == "A":
                ewu.close()
                P.finish()
                return nc

            with ExitStack() as eb:
                mu_bc = sb(eb, "mu_bc", [128, U_W], F32)
                P.dma("sp", mu_bc[:], I["shift_mu"].partition_broadcast(128), writes=[mu_bc])
                pbc = {}
                for n in ("w0", "a0", "k_k", "k_a", "r_k", "gn_w", "gn_b"):
                    pbc[n] = sb(eb, "bc_" + n, [128, 512], F32)
                    P.dma("act", pbc[n][:], I[n].partition_broadcast(128), writes=[pbc[n]])
                w2a2 = sb(eb, "w2a2", [128, 512], BF16)
                g2b = sb(eb, "g2b", [128, 512], BF16)
                P.dma("pool", w2a2[0:64, :], I["w2"], writes=[w2a2])
                P.dma("pool", w2a2[64:128, :], I["a2"], writes=[w2a2])
                P.dma("pool", g2b[:], I["g2"], writes=[g2b])
                tri = sb(eb, "tri_s", [128, 128], F32)
                P.dma("sp", tri[:], I["tri"], writes=[tri])
                msui = sb(eb, "msui", [128, 2, 128], BF16)
                msl = sb(eb, "msl", [128, 128], BF16)
                P.dma("sp", msui[:, 0, :], I["m_su"], writes=[msui])
                P.dma("sp", msui[:, 1, :], I["m_ui"], writes=[msui])
                P.dma("sp", msl[:], I["m_sl"], writes=[msl])
                elast = sb(eb, "elast_s", [128, 2], F32)
                P.dma("sp", elast[:], I["elast"], writes=[elast])

                uf = [sb(eb, f"uf{i}", [128, U_W], F32) for i in range(2)]
                mmb = [sb(eb, f"mmb{i}", [128, U_W], F32) for i in range(2)]
                lo = sb(eb, "lo", [128, 256], BF16)
                lts = sb(eb, "lts", [128, 64], F32)
                loTb = [sb(eb, f"loT{i}", [128, 2, 128], BF16) for i in range(2)]
                S0 = ps(eb, "S0", [128, 512], F32)
                S1 = ps(eb, "S1", [128, 512], F32)
                S0b = S0.t[:].bitcast(BF16)

                class Lane:
                    pass

                lanes = []
                for hg in range(2):
                    Ln = Lane()
                    nm = lambda n: f"{n}_{hg}"
                    for n in ("f_a", "f_w", "f_kk", "f_k", "f_t", "f_g", "f_ep", "f_em", "f_ex", "UlT"):
                        setattr(Ln, n, sb(eb, nm(n), [128, 256], F32))
                    Ln.sm = sb(eb, nm("sm"), [128, 40], F32)
                    for n in ("Rt", "Kt", "Bt", "At", "Vb", "WlT", "UT", "yb16"):
                        setattr(Ln, n, sb(eb, nm(n), [128, 256], BF16))
                    Ln.ARF = sb(eb, nm("ARF"), [64, 4, 2, 128], BF16)
                    Ln.BFm = sb(eb, nm("BFm"), [64, 4, 128], BF16)
                    Ln.KFm = sb(eb, nm("KFm"), [64, 4, 128], BF16)
                    Ln.Mb = sb(eb, nm("Mb"), [128, 4, 2, 128], BF16)
                    Ln.Mk = sb(eb, nm("Mk"), [128, 4, 2, 128], BF16)
                    for n in ("MbaT", "A2", "A2T", "Tm"):
                        setattr(Ln, n, sb(eb, nm(n), [128, 4, 128], BF16))
                    Ln.AFp = sb(eb, nm("AFp"), [64, 4, 128], BF16)
                    Ln.ST = sb(eb, nm("ST"), [64, 4, 64], F32)
                    Ln.Sb = sb(eb, nm("Sb"), [64, 4, 64], BF16)
                    Ln.gC = sb(eb, nm("gC"), [64, 4], F32)
                    Ln.sto = sb(eb, nm("sto"), [64, 4, 64], F32)
                    Ln.Q = [ps(eb, nm(f"Q{j}"), [128, 512], F32) for j in range(3)]
                    Ln.Qb = [q.t[:].bitcast(BF16) for q in Ln.Q]
                    lanes.append(Ln)

                def v3(ap, a):
                    return ap.rearrange("p (a b) -> p a b", a=a)

                def shared(ti, C, col0, ub, prev_row_src, out_shift_ap):
                    u = uf[ub]
                    mm = mmb[ub]
                    loT = loTb[ub]
                    for g4 in range(4):
                        pz = (S0, S1)[g4 % 2]
                        for c in range(8):
                            P.op("pe", "matmul", out=pz[0:C, 0:448], lhsT=xnT[:, c, col0:col0 + C], rhs=wu[:, c, g4 * 448:(g4 + 1) * 448],
                                 start=(c == 0), stop=(c == 7), reads=[xnT_t[ti], wu], writes=[pz])
                        P.op("act", "copy", out=u[0:C, g4 * 448:(g4 + 1) * 448], in_=pz[0:C, 0:448], reads=[pz], writes=[u])
                    if out_shift_ap is not None:
                        P.dma("sp", out_shift_ap, u[C - 1:C, :], reads=[u])
                    r16 = ((C - 1) // 16) * 16
                    qsh = ("sp", "act")
                    if r16 > 0:
                        P.dma(qsh[0], mm[1:1 + r16, :], u[0:r16, :], reads=[u], writes=[mm])
                    if C - 1 - r16 > 0:
                        P.dma(qsh[1], mm[1 + r16:C, :], u[r16:C - 1, :], reads=[u], writes=[mm])
                    if prev_row_src is None:
                        P.op("dve", "memset", ap=mm[0:1, :], constant=0.0, writes=[mm])
                    else:
                        src, rd = prev_row_src
                        P.dma("sp", mm[0:1, :], src, reads=rd, writes=[mm])
                    P.op("dve", "tensor_tensor", out=mm[0:C, :], in0=mm[0:C, :], in1=u[0:C, :], op=ALU.subtract, reads=[mm, u], writes=[mm])
                    P.op("dve", "tensor_tensor", out=mm[0:C, :], in0=mm[0:C, :], in1=mu_bc[0:C, :], op=ALU.mult, reads=[mm, mu_bc], writes=[mm])
                    P.op("dve", "tensor_tensor", out=mm[0:C, :], in0=mm[0:C, :], in1=u[0:C, :], op=ALU.add, reads=[mm, u], writes=[mm])

                def shared_b(ti, C, col0, ub):
                    mm = mmb[ub]
                    loT = loTb[ub]
                    P.op("act", "activation", out=lts[0:C, 0:64], in_=mm[0:C, 1536:1600], func=AF.Sigmoid, scale=2.0, reads=[mm], writes=[lts])
                    P.op("dve", "tensor_scalar", out=lo[0:C, 0:64], in0=lts[0:C, 0:64], scalar1=2.0, scalar2=-1.0, op0=ALU.mult, op1=ALU.add, reads=[lts], writes=[lo])
                    P.op("dve", "tensor_copy", out=lo[0:C, 64:128], in_=mm[0:C, 1600:1664], reads=[mm], writes=[lo])
                    P.op("act", "activation", out=lo[0:C, 128:256], in_=mm[0:C, 1664:1792], func=AF.Sigmoid, reads=[mm], writes=[lo])
                    for c in range(2):
                        P.op("pe", "transpose", out=S0b[:, c * 128:c * 128 + C], in_=lo[0:C, c * 128:(c + 1) * 128], identity=idb[0:C, 0:C], reads=[lo, idb], writes=[S0])
                    P.op("act", "copy", out=loT[:, :, 0:C], in_=v3(S0b[:, 0:256], 2)[:, :, 0:C], reads=[S0], writes=[loT])

                def lane(ti, C, col0, ub, hg, first, state_src, out_state):
                    Ln = lanes[hg]
                    Q0, Q1, Q2 = Ln.Q
                    Q0b, Q1b, Q2b = Ln.Qb
                    mm = mmb[ub]
                    loT = loTb[ub]
                    cs = slice(hg * 256, (hg + 1) * 256)
                    r_ = mm[0:C, hg * 256:(hg + 1) * 256]
                    kr_ = mm[0:C, 512 + hg * 256:512 + (hg + 1) * 256]
                    v_ = mm[0:C, 1024 + hg * 256:1024 + (hg + 1) * 256]
                    f_a, f_w, f_kk, f_k, f_t, f_g, f_ep, f_em, f_ex, sm = Ln.f_a, Ln.f_w, Ln.f_kk, Ln.f_k, Ln.f_t, Ln.f_g, Ln.f_ep, Ln.f_em, Ln.f_ex, Ln.sm
                    Rt, Kt, Bt, At, Vb = Ln.Rt, Ln.Kt, Ln.Bt, Ln.At, Ln.Vb
                    ARF, BFm, KFm, Mb, Mk, MbaT, Tm = Ln.ARF, Ln.BFm, Ln.KFm, Ln.Mb, Ln.Mk, Ln.MbaT, Ln.Tm
                    WlT, UlT, AFp, UT, ST, Sb, gC, sto, yb16 = Ln.WlT, Ln.UlT, Ln.AFp, Ln.UT, Ln.ST, Ln.Sb, Ln.gC, Ln.sto, Ln.yb16
                    yv = f_w
                    pb = lambda n: pbc[n][0:C, cs]
                    P.op("pe", "matmul", out=Q0[0:C, 0:256], lhsT=loT[0:64, 0, 0:C], rhs=w2a2[0:64, cs], start=True, stop=True, reads=[loT, w2a2], writes=[Q0])
                    P.op("pe", "matmul", out=Q1[0:C, 0:256], lhsT=loT[64:128, 0, 0:C], rhs=w2a2[64:128, cs], start=True, stop=True, reads=[loT, w2a2], writes=[Q1])
                    P.op("pe", "matmul", out=Q2[0:C, 0:256], lhsT=loT[:, 1, 0:C], rhs=g2b[:, cs], start=True, stop=True, reads=[loT, g2b], writes=[Q2])
                    P.op("dve", "tensor_tensor", out=f_w[0:C, :], in0=Q0[0:C, 0:256], in1=pb("w0"), op=ALU.add, reads=[Q0, pbc["w0"]], writes=[f_w])
                    P.op("dve", "tensor_tensor", out=f_a[0:C, :], in0=Q1[0:C, 0:256], in1=pb("a0"), op=ALU.add, reads=[Q1, pbc["a0"]], writes=[f_a])
                    P.op("act", "activation", out=f_w[0:C, :], in_=f_w[0:C, :], func=AF.Sigmoid, reads=[f_w], writes=[f_w])
                    P.op("act", "activation", out=f_a[0:C, :], in_=f_a[0:C, :], func=AF.Sigmoid, reads=[f_a], writes=[f_a])
                    P.op("act", "copy", out=f_g[0:C, :], in_=Q2[0:C, 0:256], reads=[Q2], writes=[f_g])
                    P.op("dve", "tensor_scalar", out=f_w[0:C, :], in0=f_w[0:C, :], scalar1=-0.6065306597126334, scalar2=None, op0=ALU.mult, reads=[f_w], writes=[f_w])
                    yield
                    P.op("pe", "matmul", out=Q0[0:C, 256:512], lhsT=tri[0:C, 0:C], rhs=f_w[0:C, :], start=True, stop=True, reads=[tri, f_w], writes=[Q0])
                    P.op("dve", "tensor_tensor", out=f_kk[0:C, :], in0=kr_, in1=pb("k_k"), op=ALU.mult, reads=[mm, pbc["k_k"]], writes=[f_kk])
                    P.op("dve", "tensor_tensor", out=f_t[0:C, :], in0=f_kk[0:C, :], in1=f_kk[0:C, :], op=ALU.mult, reads=[f_kk], writes=[f_t])
                    P.op("act", "activation", out=f_ep[0:C, :], in_=Q0[0:C, 256:512], func=AF.Exp, reads=[Q0], writes=[f_ep])
                    P.op("act", "activation", out=f_em[0:C, :], in_=Q0[0:C, 256:512], func=AF.Exp, scale=-1.0, reads=[Q0], writes=[f_em])
                    P.op("dve", "tensor_tensor", out=f_ex[0:C, :], in0=Q0[0:C, 256:512], in1=f_w[0:C, :], op=ALU.subtract, reads=[Q0, f_w], writes=[f_ex])
                    P.op("act", "activation", out=f_ex[0:C, :], in_=f_ex[0:C, :], func=AF.Exp, reads=[f_ex], writes=[f_ex])
                    yield
                    P.op("dve", "tensor_reduce", out=sm[0:C, 0:4], in_=v3(f_t[0:C, :], 4), axis=AX.X, op=ALU.add, reads=[f_t], writes=[sm])
                    P.op("act", "activation", out=sm[0:C, 0:4], in_=sm[0:C, 0:4], func=AF.Sqrt, reads=[sm], writes=[sm])
                    P.op("dve", "tensor_scalar", out=sm[0:C, 0:4], in0=sm[0:C, 0:4], scalar1=1e-12, scalar2=None, op0=ALU.max, reads=[sm], writes=[sm])
                    P.op("dve", "reciprocal", out=sm[0:C, 0:4], in_=sm[0:C, 0:4], reads=[sm], writes=[sm])
                    P.op("dve", "tensor_tensor", out=v3(f_kk[0:C, :], 4), in0=v3(f_kk[0:C, :], 4), in1=sm[0:C, 0:4].unsqueeze(2).to_broadcast([C, 4, 64]), op=ALU.mult,
                         reads=[f_kk, sm], writes=[f_kk])
                    P.op("dve", "scalar_tensor_tensor", out=f_k[0:C, :], in0=f_a[0:C, :], scalar=-1.0, in1=pb("k_a"), op0=ALU.add, op1=ALU.mult,
                         reads=[f_a, pbc["k_a"]], writes=[f_k])
                    P.op("dve", "scalar_tensor_tensor", out=f_k[0:C, :], in0=f_k[0:C, :], scalar=1.0, in1=kr_, op0=ALU.add, op1=ALU.mult, reads=[f_k, mm], writes=[f_k])
                    yield
                    P.op("dve", "tensor_tensor", out=f_t[0:C, :], in0=r_, in1=f_k[0:C, :], op=ALU.mult, reads=[mm, f_k], writes=[f_t])
                    P.op("dve", "tensor_tensor", out=f_t[0:C, :], in0=f_t[0:C, :], in1=pb("r_k"), op=ALU.mult, reads=[f_t, pbc["r_k"]], writes=[f_t])
                    P.op("dve", "tensor_reduce", out=sm[0:C, 4:8], in_=v3(f_t[0:C, :], 4), axis=AX.X, op=ALU.add, reads=[f_t], writes=[sm])
                    P.op("dve", "tensor_tensor", out=Rt[0:C, :], in0=r_, in1=f_ep[0:C, :], op=ALU.mult, reads=[mm, f_ep], writes=[Rt])
                    P.op("dve", "tensor_tensor", out=Kt[0:C, :], in0=f_k[0:C, :], in1=f_em[0:C, :], op=ALU.mult, reads=[f_k, f_em], writes=[Kt])
                    P.op("dve", "tensor_tensor", out=f_t[0:C, :], in0=f_kk[0:C, :], in1=f_a[0:C, :], op=ALU.mult, reads=[f_kk, f_a], writes=[f_t])
                    P.op("dve", "tensor_tensor", out=Bt[0:C, :], in0=f_t[0:C, :], in1=f_em[0:C, :], op=ALU.mult, reads=[f_t, f_em], writes=[Bt])
                    P.op("dve", "scalar_tensor_tensor", out=At[0:C, :], in0=f_kk[0:C, :], scalar=-1.0, in1=f_ex[0:C, :], op0=ALU.mult, op1=ALU.mult, reads=[f_kk, f_ex], writes=[At])
                    P.op("act", "copy", out=Vb[0:C, :], in_=v_, reads=[mm], writes=[Vb])
                    yield
                    for h in range(4):
                        hc = slice(h * 64, (h + 1) * 64)
                        P.op("pe", "transpose", out=Q2b[0:64, h * 256:h * 256 + C], in_=At[0:C, hc], identity=idb[0:C, 0:C], reads=[At, idb], writes=[Q2])
                        P.op("pe", "transpose", out=Q2b[0:64, h * 256 + 128:h * 256 + 128 + C], in_=Rt[0:C, hc], identity=idb[0:C, 0:C], reads=[Rt, idb], writes=[Q2])
                        P.op("pe", "transpose", out=Q0b[0:64, h * 128:h * 128 + C], in_=Bt[0:C, hc], identity=idb[0:C, 0:C], reads=[Bt, idb], writes=[Q0])
                        P.op("pe", "transpose", out=Q0b[0:64, 512 + h * 128:512 + h * 128 + C], in_=Kt[0:C, hc], identity=idb[0:C, 0:C], reads=[Kt, idb], writes=[Q0])
                    P.op("act", "copy", out=ARF[:, :, :, 0:C], in_=Q2b[0:64, :].rearrange("p (h a t) -> p h a t", h=4, a=2)[:, :, :, 0:C], reads=[Q2], writes=[ARF])
                    P.op("dve", "tensor_copy", out=BFm[:, :, 0:C], in_=v3(Q0b[0:64, 0:512], 4)[:, :, 0:C], reads=[Q0], writes=[BFm])
                    P.op("dve", "tensor_copy", out=KFm[:, :, 0:C], in_=v3(Q0b[0:64, 512:1024], 4)[:, :, 0:C], reads=[Q0], writes=[KFm])
                    yield
                    bq = (Q1, Q2)
                    for h in range(4):
                        q = bq[h // 2]
                        P.op("pe", "matmul", out=q[0:C, (h % 2) * 256:(h % 2 + 1) * 256].rearrange("p (a t) -> p a t", a=2)[:, :, 0:C], lhsT=BFm[:, h, 0:C], rhs=ARF[:, h, :, 0:C],
                             start=True, stop=True, reads=[BFm, ARF], writes=[q])
                    mbc = msui[0:C, :, 0:C].unsqueeze(1).to_broadcast([C, 2, 2, C])
                    for j in range(2):
                        P.op("dve", "tensor_tensor", out=Mb[0:C, 2 * j:2 * j + 2, :, 0:C], in0=bq[j][0:C, :].rearrange("p (h a t) -> p h a t", h=2, a=2)[:, :, :, 0:C], in1=mbc, op=ALU.mult,
                             reads=[bq[j], msui], writes=[Mb])
                    yield
                    kq = (Q0, Q1)
                    for h in range(4):
                        q = kq[h // 2]
                        P.op("pe", "matmul", out=q[0:C, (h % 2) * 256:(h % 2 + 1) * 256].rearrange("p (a t) -> p a t", a=2)[:, :, 0:C], lhsT=KFm[:, h, 0:C], rhs=ARF[:, h, :, 0:C],
                             start=True, stop=True, reads=[KFm, ARF], writes=[q])
                    for h in range(4):
                        P.op("pe", "matmul", out=Q2[0:C, h * 128:h * 128 + C], lhsT=ARF[:, h, 0, 0:C], rhs=BFm[:, h, 0:C], start=True, stop=True, reads=[ARF, BFm], writes=[Q2])
                    for j in range(2):
                        P.op("dve", "tensor_tensor", out=Mk[0:C, 2 * j:2 * j + 2, :, 0:C], in0=kq[j][0:C, :].rearrange("p (h a t) -> p h a t", h=2, a=2)[:, :, :, 0:C], in1=mbc, op=ALU.mult,
                             reads=[kq[j], msui], writes=[Mk])
                    P.op("dve", "tensor_tensor", out=MbaT[0:C, :, 0:C], in0=v3(Q2[0:C, :], 4)[:, :, 0:C], in1=msl[0:C, 0:C].unsqueeze(1).to_broadcast([C, 4, C]), op=ALU.mult,
                         reads=[Q2, msl], writes=[MbaT])
                    P.op("dve", "tensor_tensor", out=Tm[0:C, :, 0:C], in0=Mb[0:C, :, 0, 0:C], in1=idb[0:C, 0:C].unsqueeze(1).to_broadcast([C, 4, C]), op=ALU.add,
                         reads=[Mb, idb], writes=[Tm])
                    yield
                    nlev = {128: 6, 32: 4}[C]

                    class View:
                        def __init__(self, t, f):
                            self.t, self.f = t, f

                    cur = (View(Mb, lambda h: Mb[0:C, h, 0, 0:C]), View(MbaT, lambda h: MbaT[0:C, h, 0:C]))
                    nxt = (View(Ln.A2, lambda h: Ln.A2[0:C, h, 0:C]), View(Ln.A2T, lambda h: Ln.A2T[0:C, h, 0:C]))
                    for lv in range(nlev):
                        Ac, ATc = cur
                        An, ATn = nxt
                        last = (lv == nlev - 1)
                        if not last:
                            for h in range(4):
                                P.op("pe", "matmul", out=Q0[0:C, h * 128:h * 128 + C], lhsT=ATc.f(h), rhs=Ac.f(h), start=True, stop=True, reads=[ATc.t, Ac.t], writes=[Q0])
                        for h in range(4):
                            P.op("pe", "matmul", out=Q1[0:C, h * 128:h * 128 + C], lhsT=Ac.f(h), rhs=ATc.f(h), start=True, stop=True, reads=[ATc.t, Ac.t], writes=[Q1])
                        if last:
                            pass
                        elif An.t is Mb:
                            P.op("act", "copy", out=Mb[0:C, :, 0, 0:C], in_=v3(Q0[0:C, :], 4)[:, :, 0:C], reads=[Q0], writes=[Mb])
                        else:
                            P.op("act", "copy", out=An.t[0:C, :, 0:C], in_=v3(Q0[0:C, :], 4)[:, :, 0:C], reads=[Q0], writes=[An.t])
                        P.op("dve", "tensor_copy", out=ATn.t[0:C, :, 0:C], in_=v3(Q1[0:C, :], 4)[:, :, 0:C], reads=[Q1], writes=[ATn.t])
                        yield
                        for h in range(4):
                            P.op("pe", "matmul", out=Q2[0:C, h * 128:h * 128 + C], lhsT=ATn.f(h), rhs=Tm[0:C, h, 0:C], start=True, stop=True, reads=[ATn.t, Tm], writes=[Q2])
                        P.op("dve", "tensor_tensor", out=Tm[0:C, :, 0:C], in0=v3(Q2[0:C, :], 4)[:, :, 0:C], in1=Tm[0:C, :, 0:C], op=ALU.add, reads=[Q2, Tm], writes=[Tm])
                        cur, nxt = nxt, cur
                        yield
                    for h in range(4):
                        hc = slice(h * 64, (h + 1) * 64)
                        P.op("pe", "matmul", out=Q0[0:C, hc], lhsT=Mk[0:C, h, 0, 0:C], rhs=Vb[0:C, hc], start=True, stop=True, reads=[Mk, Vb], writes=[Q0])
                    P.op("act", "copy", out=WlT[0:C, :], in_=Q0[0:C, 0:256], reads=[Q0], writes=[WlT])
                    for h in range(4):
                        P.op("pe", "matmul", out=Q2[0:64, h * 128:h * 128 + C], lhsT=At[0:C, h * 64:(h + 1) * 64], rhs=Tm[0:C, h, 0:C], start=True, stop=True, reads=[At, Tm], writes=[Q2])
                    P.op("act", "copy", out=AFp[:, :, 0:C], in_=v3(Q2[0:64, :], 4)[:, :, 0:C], reads=[Q2], writes=[AFp])
                    ecol = 0 if C == 128 else 1
                    for h in range(4):
                        P.op("pe", "matmul", out=Q1[0:64, 256 + h * 2:256 + h * 2 + 2], lhsT=f_ep[0:C, h * 64:(h + 1) * 64], rhs=elast[0:C, 0:2], start=True, stop=True,
                             reads=[f_ep, elast], writes=[Q1])
                    yield
                    for h in range(4):
                        hc = slice(h * 64, (h + 1) * 64)
                        P.op("pe", "matmul", out=Q1[0:C, hc], lhsT=Tm[0:C, h, 0:C], rhs=WlT[0:C, hc], start=True, stop=True, reads=[Tm, WlT], writes=[Q1])
                    P.op("dve", "tensor_copy", out=gC[:, :], in_=Q1[0:64, 256:264].rearrange("p (h a) -> p h a", a=2)[:, :, ecol], reads=[Q1], writes=[gC])
                    P.op("act", "copy", out=UlT[0:C, :], in_=Q1[0:C, 0:256], reads=[Q1], writes=[UlT])
                    yield
                    if first:
                        if state_src is None:
                            P.op("dve", "memset", ap=ST[:], constant=0.0, writes=[ST])
                        else:
                            P.dma("sp", sto[:], state_src[hg * 4:(hg + 1) * 4].rearrange("h i j -> i h j"), writes=[sto])
                            for h in range(4):
                                P.op("pe", "transpose", out=Q0[0:64, h * 64:(h + 1) * 64], in_=sto[:, h, :], identity=idf[0:64, 0:64], reads=[sto, idf], writes=[Q0])
                            P.op("act", "copy", out=ST[:], in_=v3(Q0[0:64, 0:256], 4), reads=[Q0], writes=[ST])
                        P.op("act", "copy", out=Sb[:], in_=ST[:], reads=[ST], writes=[Sb])
                    for h in range(4):
                        P.op("pe", "matmul", out=Q0[0:C, h * 64:(h + 1) * 64], lhsT=AFp[:, h, 0:C], rhs=Sb[:, h, :], start=True, stop=True, reads=[AFp, Sb], writes=[Q0])
                    P.op("dve", "tensor_tensor", out=UT[0:C, :], in0=Q0[0:C, 0:256], in1=UlT[0:C, :], op=ALU.add, reads=[Q0, UlT], writes=[UT])
                    for h in range(4):
                        hc = slice(h * 64, (h + 1) * 64)
                        P.op("pe", "matmul", out=Q1[0:C, hc], lhsT=ARF[:, h, 1, 0:C], rhs=Sb[:, h, :], start=True, stop=False, reads=[ARF, Sb], writes=[Q1])
                        P.op("pe", "matmul", out=Q1[0:C, hc], lhsT=Mb[0:C, h, 1, 0:C], rhs=UT[0:C, hc], start=False, stop=False, reads=[Mb, UT], writes=[Q1])
                        P.op("pe", "matmul", out=Q1[0:C, hc], lhsT=Mk[0:C, h, 1, 0:C], rhs=Vb[0:C, hc], start=False, stop=True, reads=[Mk, Vb], writes=[Q1])
                    for h in range(4):
                        hc = slice(h * 64, (h + 1) * 64)
                        P.op("pe", "matmul", out=Q2[0:64, hc], lhsT=Bt[0:C, hc], rhs=UT[0:C, hc], start=True, stop=False, reads=[Bt, UT], writes=[Q2])
                        P.op("pe", "matmul", out=Q2[0:64, hc], lhsT=Kt[0:C, hc], rhs=Vb[0:C, hc], start=False, stop=True, reads=[Kt, Vb], writes=[Q2])
                    P.op("dve", "tensor_tensor", out=ST[:], in0=v3(Q2[0:64, 0:256], 4), in1=ST[:], op=ALU.add, reads=[Q2, ST], writes=[ST])
                    P.op("dve", "tensor_tensor", out=ST[:], in0=ST[:], in1=gC[:, :].unsqueeze(2).to_broadcast([64, 4, 64]), op=ALU.mult, reads=[ST, gC], writes=[ST])
                    P.op("act", "copy", out=Sb[:], in_=ST[:], reads=[ST], writes=[Sb])
                    P.op("act", "copy", out=yv[0:C, :], in_=Q1[0:C, 0:256], reads=[Q1], writes=[yv])
                    yield
                    if out_state is not None:
                        for h in range(4):
                            P.op("pe", "transpose", out=Q0[0:64, h * 64:(h + 1) * 64], in_=ST[:, h, :], identity=idf[0:64, 0:64], reads=[ST, idf], writes=[Q0])
                        P.op("act", "copy", out=sto[:], in_=v3(Q0[0:64, 0:256], 4), reads=[Q0], writes=[sto])
                        P.dma("sp", out_state[hg * 4:(hg + 1) * 4].rearrange("h i j -> i h j"), sto[:], reads=[sto])
                    y3 = v3(yv[0:C, :], 4)
                    P.op("dve", "tensor_reduce", out=sm[0:C, 8:12], in_=y3, axis=AX.X, op=ALU.add, reads=[yv], writes=[sm])
                    P.op("dve", "tensor_tensor", out=f_t[0:C, :], in0=yv[0:C, :], in1=yv[0:C, :], op=ALU.mult, reads=[yv], writes=[f_t])
                    P.op("dve", "tensor_reduce", out=sm[0:C, 12:16], in_=v3(f_t[0:C, :], 4), axis=AX.X, op=ALU.add, reads=[f_t], writes=[sm])
                    P.op("dve", "tensor_scalar", out=sm[0:C, 8:12], in0=sm[0:C, 8:12], scalar1=1.0 / 64, scalar2=None, op0=ALU.mult, reads=[sm], writes=[sm])
                    P.op("dve", "tensor_tensor", out=sm[0:C, 16:20], in0=sm[0:C, 8:12], in1=sm[0:C, 8:12], op=ALU.mult, reads=[sm], writes=[sm])
                    P.op("dve", "scalar_tensor_tensor", out=sm[0:C, 12:16], in0=sm[0:C, 12:16], scalar=1.0 / 64, in1=sm[0:C, 16:20], op0=ALU.mult, op1=ALU.subtract,
                         reads=[sm], writes=[sm])
                    P.op("act", "activation", out=sm[0:C, 12:16], in_=sm[0:C, 12:16], func=AF.Sqrt, scale=1.0, bias=64e-5, reads=[sm], writes=[sm])
                    P.op("dve", "reciprocal", out=sm[0:C, 12:16], in_=sm[0:C, 12:16], reads=[sm], writes=[sm])
                    yield
                    bc4 = lambda a: a.unsqueeze(2).to_broadcast([C, 4, 64])
                    P.op("dve", "tensor_tensor", out=y3, in0=y3, in1=bc4(sm[0:C, 8:12]), op=ALU.subtract, reads=[yv, sm], writes=[yv])
                    P.op("dve", "tensor_tensor", out=y3, in0=y3, in1=bc4(sm[0:C, 12:16]), op=ALU.mult, reads=[yv, sm], writes=[yv])
                    P.op("pool", "tensor_tensor", out=v3(f_t[0:C, :], 4), in0=v3(v_, 4), in1=bc4(sm[0:C, 4:8]), op=ALU.mult, reads=[mm, sm], writes=[f_t])
                    P.op("dve", "tensor_tensor", out=yv[0:C, :], in0=yv[0:C, :], in1=pb("gn_w"), op=ALU.mult, reads=[yv, pbc["gn_w"]], writes=[yv])
                    P.op("pool", "tensor_tensor", out=f_t[0:C, :], in0=f_t[0:C, :], in1=pb("gn_b"), op=ALU.add, reads=[f_t, pbc["gn_b"]], writes=[f_t])
                    P.op("dve", "tensor_tensor", out=yv[0:C, :], in0=yv[0:C, :], in1=f_t[0:C, :], op=ALU.add, reads=[yv, f_t], writes=[yv])
                    P.op("dve", "tensor_tensor", out=yb16[0:C, :], in0=yv[0:C, :], in1=f_g[0:C, :], op=ALU.mult, reads=[yv, f_g], writes=[yb16])
                    if "yB" in DBG:
                        P.op("dve", "tensor_tensor", out=yv[0:C, :], in0=yv[0:C, :], in1=f_g[0:C, :], op=ALU.mult, reads=[yv, f_g], writes=[yv])
                        P.dma("sp", DBG["yB"][col0:col0 + C, cs], yv[0:C, :], reads=[yv])
                    for c in range(2):
                        P.op("pe", "transpose", out=Q2b[:, c * 128:c * 128 + C], in_=yb16[0:C, c * 128:(c + 1) * 128], identity=idb[0:C, 0:C], reads=[yb16, idb], writes=[Q2])
                    P.op("act", "copy", out=yBT[:, 2 * hg:2 * hg + 2, col0:col0 + C], in_=v3(Q2b[:, 0:256], 2)[:, :, 0:C], reads=[Q2], writes=[yBT])

                seq = []
                for i in range(NPT):
                    seq.append(dict(ti=i, C=128, col0=i * 128, first=(i == 0), state_src=None, out_state=(O["wkv_p"] if i == NPT - 1 else None),
                                    out_shift=(O["shift_p"][0:1, :] if i == NPT - 1 else None), prev=("tile" if i > 0 else None)))
                for b in range(SB):
                    seq.append(dict(ti=16, C=32, col0=2048 + 32 * b, first=True, state_src=I["swkv"][b], out_state=O["wkv_s"][b],
                                    out_shift=O["shift_s"][b:b + 1, :], prev=("dram", b)))

                def do_shared_b(k):
                    d = seq[k]
                    shared_b(d["ti"], d["C"], d["col0"], k % 2)

                def do_shared(k):
                    d = seq[k]
                    if d["prev"] is None:
                        prev = None
                    elif d["prev"] == "tile":
                        prev = (uf[(k - 1) % 2][127:128, :], [uf[(k - 1) % 2]])
                    else:
                        prev = (I["sshift"][d["prev"][1]:d["prev"][1] + 1, :], [])
                    shared(d["ti"], d["C"], d["col0"], k % 2, prev, d["out_shift"])

                seq = seq[:int(os.environ.get("KSEQ", "99"))]
                do_shared(0)
                do_shared_b(0)
                RA = int(os.environ.get("KRA", "4"))
                RB = int(os.environ.get("KRB", "4"))
                for k in range(len(seq)):
                    d = seq[k]
                    gens = [lane(d["ti"], d["C"], d["col0"], k % 2, hg, d["first"], d["state_src"], d["out_state"]) for hg in range(2)]
                    rounds = 0
                    active = list(gens)
                    while active:
                        if rounds >= int(os.environ.get("KLST", "999")):
                            break
                        for g in list(active):
                            try:
                                next(g)
                            except StopIteration:
                                active.remove(g)
                        rounds += 1
                        if rounds == RA and k + 1 < len(seq):
                            do_shared(k + 1)
                        if rounds == RB and k + 1 < len(seq):
                            do_shared_b(k + 1)
                    if k + 1 < len(seq):
                        if rounds < RA:
                            do_shared(k + 1)
                        if rounds < RB:
                            do_shared_b(k + 1)
                P.barrier()
            ewu.close()
            if stop_after == "B":
                P.finish()
                return nc

            oAT = sb(e1, "oAT", [128, 4, TOK], BF16)
            with ExitStack() as ec:
                LP = 4224
                wq = sb(ec, "wq", [128, 8, 512], BF16)
                wiq = sb(ec, "wiq", [128, 8, 1024], BF16)
                o_q = IN_OFF["q"][0]
                o_iq = IN_OFF["iq"][0]
                load_w("pool", wiq, wiq[:, :, 0:512], wsl("iq", o_iq, o_iq + 512))
                load_w("pool", wiq, wiq[:, :, 512:1024], wsl("iq", o_iq + 512, o_iq + 1024))
                load_w("pool", wq, wq[:], wsl("q", o_q, o_q + 512))
                bsel = sb(ec, "bsel", [128, 4], F32)
                wtmp = sb(ec, "wtmp", [128, 16], F32)
                P.dma("sp", bsel[:], I["bsel"], writes=[bsel])
                qT4 = [sb(ec, f"qT4{i}", [128, 4, 512], BF16) for i in range(2)]
                iqT4 = sb(ec, "iqT4", [128, 8, 512], BF16)
                Dw = sb(ec, "Dw", [128, 16, 128], BF16)
                score = sb(ec, "score", [128, LP], F32)
                cx = {"qT": (qT4[0], 0), "iqT": (iqT4, 0), "Dw": Dw, "score": score}
                work = sb(ec, "work", [128, 512], F32)
                bz = sb(ec, "bz", [128, 4], F32)
                bzi = sb(ec, "bzi", [128, 2], mybir.dt.uint32)
                NBIS = int(os.environ.get("KNBIS", "18"))
                pw2 = sb(ec, "pw2", [128, 32], F32)
                hw = sb(ec, "hw", [128, 32], F32)
                P.dma("sp", pw2[:], I["pow2"], writes=[pw2])

                def score_bounds(L):
                    score = cx["score"]
                    P.op("dve", "tensor_reduce", out=bz[:, 0:1], in_=score[:, 0:L], axis=AX.X, op=ALU.min, reads=[score], writes=[bz])
                    P.op("dve", "tensor_reduce", out=bz[:, 1:2], in_=score[:, 0:L], axis=AX.X, op=ALU.max, reads=[score], writes=[bz])
                mask = sb(ec, "mask", [128, LP], BF16)
                maskT = sb(ec, "maskT", [128, 33, 128], BF16)
                rb = [sb(ec, f"rb{i}", [128, 512], BF16) for i in range(4)]
                Eb = [sb(ec, f"Eb{i}", [128, 8, 128], BF16) for i in range(3)]
                PTb = [sb(ec, f"PTb{i}", [128, 8, 128], BF16) for i in range(3)]
                rden = sb(ec, "rden", [128, 8], F32)
                o16 = sb(ec, "o16", [128, 512], BF16)
                X2 = [ps(ec, f"X2{i}", [128, 1024], F32) for i in range(2)]
                Y1 = [ps(ec, f"Y1{i}", [128, 512], F32) for i in range(4)]
                Y1b16 = [Y1[i].t[:].bitcast(BF16) for i in range(4)]

                def project_q(ti, col0, Pq):
                    qT, qo = cx["qT"]
                    for h in range(8):
                        for c in range(8):
                            P.op("pe", "matmul", out=Y1[0][64 * (h // 4):64 * (h // 4) + 64, (h % 4) * 128:(h % 4) * 128 + Pq], lhsT=wq[:, c, h * 64:(h + 1) * 64],
                                 rhs=xnT[:, c, col0:col0 + Pq], start=(c == 0), stop=(c == 7), reads=[wq, xnT_t[ti]], writes=[Y1[0]])
                    P.op("act", "copy", out=qT[:, :, qo:qo + Pq], in_=Y1[0][:, :].rearrange("p (h t) -> p h t", h=4)[:, :, 0:Pq], reads=[Y1[0]], writes=[qT])

                def project_iq(ti, col0, Pq):
                    iqT, io = cx["iqT"]
                    for h in range(16):
                        for c in range(8):
                            P.op("pe", "matmul", out=X2[0][64 * (h % 2):64 * (h % 2) + 64, (h // 2) * 128:(h // 2) * 128 + Pq], lhsT=wiq[:, c, h * 64:(h + 1) * 64],
                                 rhs=xnT[:, c, col0:col0 + Pq], start=(c == 0), stop=(c == 7), reads=[wiq, xnT_t[ti]], writes=[X2[0]])
                    P.op("act", "copy", out=iqT[:, :, io:io + Pq], in_=X2[0][:, :].rearrange("p (h t) -> p h t", h=8)[:, :, 0:Pq], reads=[X2[0]], writes=[iqT])

                def project_group(g, qbuf):
                    c0 = g * 512
                    rg = [xnT_t[4 * g + j] for j in range(4)]
                    slots = [(X2[0], 0), (X2[0], 512), (X2[1], 0), (X2[1], 512), (Y1[0], 0), (Y1[1], 0), (Y1[2], 0), (Y1[3], 0)]
                    for j in range(8):
                        t, off = slots[j]
                        for hh in range(2):
                            h = 2 * j + hh
                            for c in range(8):
                                P.op("pe", "matmul", out=t[64 * hh:64 * hh + 64, off:off + 512], lhsT=wiq[:, c, h * 64:(h + 1) * 64], rhs=xnT[:, c, c0:c0 + 512],
                                     start=(c == 0), stop=(c == 7), reads=[wiq] + rg, writes=[t])
                        P.op("act", "copy", out=iqT4[:, j, :], in_=t[:, off:off + 512], reads=[t], writes=[iqT4])
                    for j in range(4):
                        t = Y1[j]
                        for hh in range(2):
                            h = 4 * hh + j
                            for c in range(8):
                                P.op("pe", "matmul", out=t[64 * hh:64 * hh + 64, 0:512], lhsT=wq[:, c, h * 64:(h + 1) * 64], rhs=xnT[:, c, c0:c0 + 512],
                                     start=(c == 0), stop=(c == 7), reads=[wq] + rg, writes=[t])
                        P.op("act", "copy", out=qbuf[:, j, :], in_=t[:, 0:512], reads=[t], writes=[qbuf])

                def make_dw(ti, bcol):
                    Dw = cx["Dw"]
                    if bcol is None:
                        wsrc = iws[:, ti, :]
                        rd = [idb, iws]
                    else:
                        P.op("dve", "tensor_scalar", out=wtmp[:, :], in0=iws[:, ti, :], scalar1=bsel[:, bcol:bcol + 1], scalar2=None, op0=ALU.mult, reads=[iws, bsel], writes=[wtmp])
                        wsrc = wtmp[:, :]
                        rd = [idb, wtmp]
                    P.op("dve", "tensor_tensor", out=Dw[:, :, :], in0=idb[:, :].unsqueeze(1).to_broadcast([128, 16, 128]), in1=wsrc.unsqueeze(2).to_broadcast([128, 16, 128]), op=ALU.mult,
                         reads=rd, writes=[Dw])

                rbi = [0]

                def indexer(ikT_src, ik_off, L, accumulate):
                    nkb = (L + 511) // 512
                    pls = [Y1[0], Y1[1], Y1[2], X2[1]]
                    steps = [(kb, h) for kb in range(nkb) for h in range(16)]
                    iqT, io = cx["iqT"]
                    Dw = cx["Dw"]
                    score = cx["score"]

                    def mm1(si):
                        kb, h = steps[si]
                        n = min(512, L - kb * 512)
                        pl = pls[si % 4]
                        hp = slice(64 * (h % 2), 64 * (h % 2) + 64)
                        P.op("pe", "matmul", out=pl[:, 0:n], lhsT=iqT[hp, h // 2, io:io + 128], rhs=ikT_src[hp, ik_off + kb * 512:ik_off + kb * 512 + n], start=True, stop=True,
                             reads=[iqT, ikT_src], writes=[pl])
                        r = rb[si % 4]
                        if si % 2 == 0 or cx.get("act_only"):
                            P.op("act", "activation", out=r[:, 0:n], in_=pl[:, 0:n], func=AF.Relu, reads=[pl], writes=[r])
                        else:
                            P.op("dve", "tensor_scalar", out=r[:, 0:n], in0=pl[:, 0:n], scalar1=0.0, scalar2=None, op0=ALU.max, reads=[pl], writes=[r])

                    def mm2(si):
                        kb, h = steps[si]
                        n = min(512, L - kb * 512)
                        r = rb[si % 4]
                        P.op("pe", "matmul", out=Y1[3][:, 0:n], lhsT=Dw[:, h, :], rhs=r[:, 0:n], start=(h == 0), stop=(h == 15), reads=[Dw, r], writes=[Y1[3]])
                        if h == 15:
                            cs = slice(kb * 512, kb * 512 + n)
                            if accumulate:
                                P.op("dve", "tensor_tensor", out=score[:, cs], in0=Y1[3][:, 0:n], in1=score[:, cs], op=ALU.add, reads=[Y1[3], score], writes=[score])
                            else:
                                P.op("act", "copy", out=score[:, cs], in_=Y1[3][:, 0:n], reads=[Y1[3]], writes=[score])

                    LA = int(os.environ.get("KLA", "3"))
                    for si in range(len(steps) + LA):
                        if si < len(steps):
                            mm1(si)
                        if si >= LA:
                            mm2(si - LA)

                def topk_mask(L, do_topk):
                    score = cx["score"]
                    if do_topk:
                        P.op("dve", "tensor_tensor", out=bz[:, 1:2], in0=bz[:, 1:2], in1=bz[:, 0:1], op=ALU.subtract, reads=[bz], writes=[bz])
                        P.op("dve", "tensor_scalar", out=hw[:, :], in0=pw2[:, :], scalar1=bz[:, 1:2], scalar2=None, op0=ALU.mult, reads=[pw2, bz], writes=[hw])
                        P.op("dve", "tensor_tensor", out=bz[:, 2:3], in0=bz[:, 0:1], in1=hw[:, 0:1], op=ALU.add, reads=[bz, hw], writes=[bz])
                        for st in range(NBIS):
                            P.op("dve", "tensor_scalar", out=mask[:, 0:L], in0=score[:, 0:L], scalar1=bz[:, 2:3], scalar2=None, op0=ALU.is_ge, op1=ALU.add, accum_out=bz[:, 3:4],
                                 reads=[score, bz], writes=[mask, bz])
                            P.op("dve", "tensor_scalar", out=bz[:, 3:4], in0=bz[:, 3:4], scalar1=255.5, scalar2=hw[:, st:st + 1], op0=ALU.is_ge, op1=ALU.mult, reads=[bz, hw], writes=[bz])
                            P.op("dve", "scalar_tensor_tensor", out=bz[:, 2:3], in0=bz[:, 3:4], scalar=hw[:, st + 1:st + 2], in1=bz[:, 2:3], op0=ALU.subtract, op1=ALU.add,
                                 reads=[bz, hw], writes=[bz])
                        P.op("dve", "tensor_tensor", out=bz[:, 0:1], in0=bz[:, 2:3], in1=hw[:, NBIS:NBIS + 1], op=ALU.subtract, reads=[bz, hw], writes=[bz])
                        P.op("dve", "tensor_scalar", out=mask[:, 0:L], in0=score[:, 0:L], scalar1=bz[:, 0:1], scalar2=None, op0=ALU.is_ge, reads=[score, bz], writes=[mask])
                    else:
                        P.op("dve", "tensor_scalar", out=mask[:, 0:L], in0=score[:, 0:L], scalar1=-1.0e29, scalar2=None, op0=ALU.is_ge, reads=[score], writes=[mask])
                    nblk = (L + 127) // 128
                    for b0 in range(0, nblk, 8):
                        nb = min(8, nblk - b0)
                        for j in range(nb):
                            blk = b0 + j
                            ks = min(128, L - blk * 128)
                            P.op("pe", "transpose", out=Y1b16[0][0:ks, j * 128:(j + 1) * 128], in_=mask[:, blk * 128:blk * 128 + ks], identity=idb[:, :], reads=[mask, idb], writes=[Y1[0]])
                        P.op("act", "copy", out=maskT[:, b0:b0 + nb, :], in_=Y1b16[0][:, 0:nb * 128].rearrange("p (b t) -> p b t", b=nb), reads=[Y1[0]], writes=[maskT])

                ebi = [0]

                def attend(Pq, tq0, kT_src, va_src, L, ocol0):
                    nblk = (L + 127) // 128
                    base = ebi[0]
                    ebi[0] += nblk
                    qT, qo = cx["qT"]
                    tq0 = tq0 + qo
                    mq0 = tq0 - qo

                    def ksz(blk):
                        return min(128, L - blk * 128)

                    def pS_of(blk, kv):
                        bf = (base + blk) % 3
                        if bf < 2:
                            return X2[bf], X2[bf][0:ksz(blk), kv * 512:kv * 512 + 4 * Pq]
                        t = (Y1[0], Y1[3])[kv]
                        return t, t[0:ksz(blk), 0:4 * Pq]

                    def S(blk):
                        ks = ksz(blk)
                        bf = (base + blk) % 3
                        E = Eb[bf]
                        PT = PTb[bf]
                        for kv in range(2):
                            t, ap = pS_of(blk, kv)
                            P.op("pe", "matmul", out=ap.rearrange("p (g t) -> p g t", g=4), lhsT=kT_src[64 * kv:64 * kv + 64, blk * 128:blk * 128 + ks],
                                 rhs=qT[64 * kv:64 * kv + 64, :, tq0:tq0 + Pq], start=True, stop=True, reads=[kT_src, qT], writes=[t])
                        for kv in range(2):
                            t, ap = pS_of(blk, kv)
                            P.op("act", "activation", out=E[0:ks, 4 * kv:4 * kv + 4, 0:Pq], in_=ap.rearrange("p (g t) -> p g t", g=4), func=AF.Exp, scale=0.125, reads=[t], writes=[E])
                        eng = "pool" if blk % 3 == 0 else "dve"
                        P.op(eng, "tensor_tensor", out=PT[0:ks, :, 0:Pq], in0=E[0:ks, :, 0:Pq], in1=maskT[0:ks, blk, mq0:mq0 + Pq].unsqueeze(1).to_broadcast([ks, 8, Pq]), op=ALU.mult,
                             reads=[E, maskT], writes=[PT])

                    def PV(blk):
                        ks = ksz(blk)
                        PT = PTb[(base + blk) % 3]
                        for h in range(8):
                            kv, g = h // 4, h % 4
                            po = Y1[1 + kv]
                            P.op("pe", "matmul", out=po[0:Pq, 65 * g:65 * g + 65], lhsT=PT[0:ks, h, 0:Pq], rhs=va_src[0:ks, blk, kv, :], start=(blk == 0 and g == 0),
                                 stop=(blk == nblk - 1 and g == 3), skip_group_check=True, reads=[PT, va_src], writes=[po])

                    LA = 2
                    for bi in range(nblk + LA):
                        if bi < nblk:
                            S(bi)
                        if bi >= LA:
                            PV(bi - LA)
                    for kv in range(2):
                        po = Y1[1 + kv]
                        pov = po[0:Pq, 0:260].rearrange("p (g d) -> p g d", g=4)
                        P.op("dve", "reciprocal", out=rden[0:Pq, 4 * kv:4 * kv + 4], in_=pov[:, :, 64], reads=[po], writes=[rden])
                        P.op("dve", "tensor_tensor", out=o16[0:Pq, kv * 256:(kv + 1) * 256].rearrange("p (g d) -> p g d", g=4), in0=pov[:, :, 0:64],
                             in1=rden[0:Pq, 4 * kv:4 * kv + 4].unsqueeze(2).to_broadcast([Pq, 4, 64]), op=ALU.mult, reads=[po, rden], writes=[o16])
                    if "oA" in DBG:
                        P.op("act", "copy", out=work[0:Pq, 0:512], in_=o16[0:Pq, :], reads=[o16], writes=[work])
                        P.dma("sp", DBG["oA"][ocol0:ocol0 + Pq, :], work[0:Pq, 0:512], reads=[work])
                    for c in range(4):
                        P.op("pe", "transpose", out=Y1b16[3][:, c * 128:c * 128 + Pq], in_=o16[0:Pq, c * 128:(c + 1) * 128], identity=idb[0:Pq, 0:Pq], reads=[o16, idb], writes=[Y1[3]])
                    P.op("act", "copy", out=oAT[:, :, ocol0:ocol0 + Pq], in_=Y1b16[3][:, 0:512].rearrange("p (c t) -> p c t", c=4)[:, :, 0:Pq], reads=[Y1[3]], writes=[oAT])

                n_pt = int(os.environ.get("KNPT", NPT))
                with ExitStack() as ecp:
                    score2 = sb(ecp, "score2", [128, 2048], F32)
                    Dw2 = sb(ecp, "Dw2", [128, 16, 128], BF16)
                    scoreb = [score, score2]
                    Dwb = [Dw, Dw2]

                    def front(i):
                        g = i // 4
                        if i % 4 == 0:
                            project_group(g, qT4[g % 2])
                        cx.update({"qT": (qT4[g % 2], (i % 4) * 128), "iqT": (iqT4, (i % 4) * 128), "Dw": Dwb[i % 2], "score": scoreb[i % 2], "act_only": True})
                        make_dw(i, None)
                        indexer(ikT, 0, 128 * (i + 1), False)

                    def back(i):
                        g = i // 4
                        L = 128 * (i + 1)
                        cx.update({"qT": (qT4[g % 2], (i % 4) * 128), "score": scoreb[i % 2]})
                        sc = scoreb[i % 2]
                        if L > 256:
                            score_bounds(L)
                        P.op("dve", "memset", ap=sc[0:64, L - 64:L], constant=NEG, writes=[sc])
                        if "sc" in DBG and i == DBG_TILE:
                            P.dma("sp", DBG["sc"][0:128, 0:L], sc[:, 0:L], reads=[sc])
                        topk_mask(L, L > 256)
                        attend(128, 0, kT, vaug, L, i * 128)

                    front(0)
                    for i in range(n_pt):
                        if i + 1 < n_pt:
                            front(i + 1)
                        back(i)
                    P.barrier()
                cx.update({"qT": (qT4[0], 0), "iqT": (iqT4, 0), "Dw": Dw, "score": score, "act_only": False})

                if "s" not in SKIP:
                    ikTb = sb(ec, "ikTb", [128, LP], BF16)
                    kTb = sb(ec, "kTb", [128, LP], BF16)
                    vab = sb(ec, "vab", [128, 33, 2, 65], BF16)
                    stg = [sb(ec, "stg0", [128, 8, 128], BF16)] * 2
                    P.op("pool", "memset", ap=vab[:, :, :, 64:65], constant=1.0, writes=[vab])
                    project_q(16, 2048, 128)
                    project_iq(16, 2048, 128)
                    sgi = 0
                    sTm = wiq.t[:].rearrange("p c n -> p (c n)").bitcast(F32).rearrange("p (b t) -> p b t", t=128)
                    sTl = wq.t[:].rearrange("p c n -> p (c n)").bitcast(F32)
                    tAB = qT4[1].t[:].rearrange("p c n -> p (c n)").bitcast(F32)
                    wbc_ap = Dw.t[:].rearrange("p h t -> p (h t)").bitcast(F32)[:, 0:512].rearrange("p (h t) -> p h t", h=16)
                    P.op("dve", "memset", ap=work[:, 0:128], constant=1.0, writes=[work])

                    def sample_indexer_T(b):
                        P.op("dve", "tensor_tensor", out=tAB[:, 0:512].rearrange("p (h t) -> p h t", h=16), in0=idf[:, 32 * b:32 * b + 32].unsqueeze(1).to_broadcast([128, 16, 32]),
                             in1=iws[:, 16, :].unsqueeze(2).to_broadcast([128, 16, 32]), op=ALU.mult, reads=[idf, iws], writes=[qT4[1]])
                        P.op("pe", "matmul", out=Y1[3][:, 0:512], lhsT=work[:, 0:128], rhs=tAB[:, 0:512], start=True, stop=True, reads=[work, qT4[1]], writes=[Y1[3]])
                        P.op("act", "copy", out=wbc_ap, in_=Y1[3][:, 0:512].rearrange("p (h t) -> p h t", h=16), reads=[Y1[3]], writes=[Dw])
                        wv = wbc_ap.rearrange("p (j two) t -> p j two t", two=2)
                        npair = 17
                        banks = [(Y1[0], Y1[1]), (Y1[2], X2[1])]
                        for kp in range(npair):
                            nq = 2 if kp < 16 else 1
                            ks = 128 if kp < 16 else 32
                            bA, bB = banks[kp % 2]
                            rA = rb[(2 * kp) % 4]
                            rB = rb[(2 * kp + 1) % 4]
                            for par, bank in ((0, bA), (1, bB)):
                                for q in range(nq):
                                    blk = 2 * kp + q
                                    P.op("pe", "matmul", out=bank[0:ks, q * 256:(q + 1) * 256].rearrange("p (j t) -> p j t", j=8), lhsT=ikTb[64 * par:64 * par + 64, blk * 128:blk * 128 + ks],
                                         rhs=iqT4[64 * par:64 * par + 64, :, 32 * b:32 * b + 32], start=True, stop=True, reads=[ikTb, iqT4], writes=[bank])
                            n = nq * 256
                            P.op("act", "activation", out=rA[0:ks, 0:n], in_=bA[0:ks, 0:n], func=AF.Relu, reads=[bA], writes=[rA])
                            P.op("act", "activation", out=rB[0:ks, 0:n], in_=bB[0:ks, 0:n], func=AF.Relu, reads=[bB], writes=[rB])
                            v4 = lambda ap: ap.rearrange("p (q j t) -> p q j t", q=nq, j=8)
                            wA = wv[0:ks, :, 0, :].unsqueeze(1).to_broadcast([ks, nq, 8, 32])
                            wB = wv[0:ks, :, 1, :].unsqueeze(1).to_broadcast([ks, nq, 8, 32])
                            P.op("dve", "tensor_tensor", out=v4(tAB[0:ks, 0:n]), in0=v4(rA[0:ks, 0:n]), in1=wA, op=ALU.mult, reads=[rA, Dw], writes=[qT4[1]])
                            P.op("dve", "tensor_tensor", out=v4(tAB[0:ks, 512:512 + n]), in0=v4(rB[0:ks, 0:n]), in1=wB, op=ALU.mult, reads=[rB, Dw], writes=[qT4[1]])
                            P.op("dve", "tensor_tensor", out=tAB[0:ks, 0:n], in0=tAB[0:ks, 0:n], in1=tAB[0:ks, 512:512 + n], op=ALU.add, reads=[qT4[1]], writes=[qT4[1]])
                            src = tAB[0:ks, 0:n].rearrange("p (q j t) -> p q t j", q=nq, j=8)
                            if kp < 16:
                                P.op("dve", "tensor_reduce", out=sTm[:, 2 * kp:2 * kp + 2, 32 * b:32 * b + 32], in_=src, axis=AX.X, op=ALU.add, reads=[qT4[1]], writes=[wiq])
                            else:
                                P.op("dve", "tensor_reduce", out=sTl[0:32, 32 * b:32 * b + 32].unsqueeze(1), in_=src, axis=AX.X, op=ALU.add, reads=[qT4[1]], writes=[wq])

                    def sample_score_transpose():
                        for b0 in range(0, 33, 4):
                            nb = min(4, 33 - b0)
                            bank = Y1[(b0 // 4) % 2]
                            for j in range(nb):
                                blk = b0 + j
                                if blk < 32:
                                    P.op("pe", "transpose", out=bank[:, j * 128:(j + 1) * 128], in_=sTm[:, blk, :], identity=idf[:, :], reads=[wiq, idf], writes=[bank])
                                else:
                                    P.op("pe", "transpose", out=bank[:, j * 128:j * 128 + 32], in_=sTl[0:32, 0:128], identity=idf[0:32, 0:32], reads=[wq, idf], writes=[bank])
                            w_ = (nb - 1) * 128 + (128 if b0 + nb - 1 < 32 else 32)
                            P.op("act", "copy", out=score[:, b0 * 128:b0 * 128 + w_], in_=bank[:, 0:w_], reads=[bank], writes=[score])
                    for b in range(SB):
                        for pc in range(4):
                            sg = stg[sgi % 2]
                            sgi += 1
                            src = I["cik"][b, pc * 1024:(pc + 1) * 1024, :].rearrange("(blk s) d -> s blk d", s=128)
                            P.dma("pool", sg[:, :, 0:64], src, writes=[sg])
                            P.dma("pool", sg[:, :, 64:128], src, writes=[sg])
                            for j in range(8):
                                P.op("pe", "transpose", out=Y1b16[0][:, j * 128:(j + 1) * 128], in_=sg[:, j, :], identity=idb[:, :], reads=[sg, idb], writes=[Y1[0]])
                            P.op("act", "copy", out=ikTb[:, pc * 1024:(pc + 1) * 1024], in_=Y1b16[0][:, :], reads=[Y1[0]], writes=[ikTb])
                        P.op("act", "copy", out=ikTb[:, 4096:4128], in_=ikTs[:, 32 * b:32 * b + 32], reads=[ikTs], writes=[ikTb])
                        if "t" in SKIP:
                            make_dw(16, b)
                            indexer(ikTb, 0, LS, b > 0)
                        else:
                            sample_indexer_T(b)
                    if "t" not in SKIP:
                        sample_score_transpose()
                    if "sc" in DBG and DBG_TILE == 16:
                        P.dma("sp", DBG["sc"][0:128, 0:LS], score[:, 0:LS], reads=[score])
                    def build_kside(b):
                        nonlocal sgi
                        for pc in range(4):
                            sg = stg[sgi % 2]
                            sgi += 1
                            P.dma("pool", sg[:], I["ck"][b, pc * 1024:(pc + 1) * 1024, :].rearrange("(blk s) d -> s blk d", s=128), writes=[sg])
                            for j in range(8):
                                P.op("pe", "transpose", out=Y1b16[0][:, j * 128:(j + 1) * 128], in_=sg[:, j, :], identity=idb[:, :], reads=[sg, idb], writes=[Y1[0]])
                            P.op("act", "copy", out=kTb[:, pc * 1024:(pc + 1) * 1024], in_=Y1b16[0][:, :], reads=[Y1[0]], writes=[kTb])
                            for kv in range(2):
                                P.dma("pool", vab[:, pc * 8:(pc + 1) * 8, kv, 0:64], I["cv"][b, pc * 1024:(pc + 1) * 1024, kv * 64:(kv + 1) * 64].rearrange("(blk s) d -> s blk d", s=128), writes=[vab])
                        P.op("act", "copy", out=kTb[:, 4096:4128], in_=kTs[:, 32 * b:32 * b + 32], reads=[kTs], writes=[kTb])
                        P.op("pe", "transpose", out=Y1b16[0][0:32, 0:128], in_=vTs[:, 32 * b:32 * b + 32], identity=idb[:, :], reads=[vTs, idb], writes=[Y1[0]])
                        P.op("act", "copy", out=vab[0:32, 32, :, 0:64], in_=Y1b16[0][0:32, 0:128].rearrange("p (k d) -> p k d", k=2), reads=[Y1[0]], writes=[vab])

                    build_kside(0)
                    score_bounds(LS)
                    topk_mask(LS, True)
                    for b in range(SB):
                        if b > 0:
                            build_kside(b)
                        attend(32, 32 * b, kTb, vab, LS, 2048 + 32 * b)
                P.barrier()
            if stop_after == "C":
                P.finish()
                return nc

            Oy = [T(None, f"oy{i}") for i in range(NT)]
            with ExitStack() as ed:
                wga = sb(ed, "wga", [128, 8, 1024], BF16)
                wgb = sb(ed, "wgb", [128, 8, 1024], BF16)
                woa = sb(ed, "woa", [128, 4, 1024], BF16)
                wob = sb(ed, "wob", [128, 4, 1024], BF16)
                wout = sb(ed, "wout", [128, 8, 1024], BF16)
                o_ga = IN_OFF["ga"][0]
                o_gb = IN_OFF["gb"][0]
                for hf in range(2):
                    cs = slice(hf * 512, (hf + 1) * 512)
                    load_w("pool", wga, wga[:, :, cs], wsl("ga", o_ga + hf * 512, o_ga + (hf + 1) * 512))
                    load_w("pool", wgb, wgb[:, :, cs], wsl("gb", o_gb + hf * 512, o_gb + (hf + 1) * 512))
                    load_w("pool", woa, woa[:, :, cs], I["w_oa"][:, cs].rearrange("(c p) n -> p c n", p=128))
                    load_w("pool", wob, wob[:, :, cs], I["w_ob"][:, cs].rearrange("(c p) n -> p c n", p=128))
                for hf in range(2):
                    cs = slice(hf * 512, (hf + 1) * 512)
                    load_w("pool", wout, wout[:, :, cs], I["w_out"][:, cs].rearrange("(c p) n -> p c n", p=128))
                n2T = sb(ed, "n2T", [128, 8], F32)
                P.dma("sp", n2T[:], I["norm2"].rearrange("(c p) -> p c", p=128), writes=[n2T], allow_slow_non_contiguous=True)
                mTg = sb(ed, "mTg", [128, 8, 512], BF16)
                sga = sb(ed, "sga", [128, 512], F32)
                sgb = sb(ed, "sgb", [128, 512], F32)
                t1 = sb(ed, "t1", [128, 512], F32)
                t2 = sb(ed, "t2", [128, 512], F32)
                xt2 = [sb(ed, f"xt2{i}", [128, D], F32) for i in range(2)]
                ht = [sb(ed, f"ht{i}", [128, D], F32) for i in range(2)]
                hs = sb(ed, "hs", [128, D], BF16)
                jk = sb(ed, "jk", [128, D], BF16)
                ss2 = sb(ed, "ss2", [128, 2], F32)
                pG = [ps(ed, f"pG{i}", [128, 512], F32) for i in range(4)]
                pH = [ps(ed, f"pH{i}", [128, 512], F32) for i in range(2)]
                pT2 = ps(ed, "pT2", [128, 8, 128], BF16)
                groups = [(0, 4), (4, 4), (8, 4), (12, 4), (16, 1)]
                for (t0, nt) in groups:
                    c0 = t0 * 128
                    n = nt * 128
                    rg = [xnT_t[t0 + j] for j in range(nt)]
                    for fc in range(8):
                        fs = slice(fc * 128, (fc + 1) * 128)
                        for c in range(8):
                            P.op("pe", "matmul", out=pG[0][:, 0:n], lhsT=wga[:, c, fs], rhs=xnT[:, c, c0:c0 + n], start=(c == 0), stop=(c == 7), reads=[wga] + rg, writes=[pG[0]])
                        for c in range(8):
                            P.op("pe", "matmul", out=pG[1][:, 0:n], lhsT=wgb[:, c, fs], rhs=xnT[:, c, c0:c0 + n], start=(c == 0), stop=(c == 7), reads=[wgb] + rg, writes=[pG[1]])
                        for c in range(4):
                            P.op("pe", "matmul", out=pG[2][:, 0:n], lhsT=woa[:, c, fs], rhs=oAT[:, c, c0:c0 + n], start=(c == 0), stop=(c == 3), reads=[woa, oAT], writes=[pG[2]])
                        for c in range(4):
                            P.op("pe", "matmul", out=pG[3][:, 0:n], lhsT=wob[:, c, fs], rhs=yBT[:, c, c0:c0 + n], start=(c == 0), stop=(c == 3), reads=[wob, yBT], writes=[pG[3]])
                        P.op("act", "activation", out=sga[:, 0:n], in_=pG[0][:, 0:n], func=AF.Sigmoid, reads=[pG[0]], writes=[sga])
                        P.op("act", "activation", out=sgb[:, 0:n], in_=pG[1][:, 0:n], func=AF.Sigmoid, reads=[pG[1]], writes=[sgb])
                        P.op("dve", "tensor_tensor", out=t1[:, 0:n], in0=pG[2][:, 0:n], in1=sga[:, 0:n], op=ALU.mult, reads=[pG[2], sga], writes=[t1])
                        P.op("dve", "tensor_tensor", out=t2[:, 0:n], in0=pG[3][:, 0:n], in1=sgb[:, 0:n], op=ALU.mult, reads=[pG[3], sgb], writes=[t2])
                        P.op("pool", "tensor_tensor", out=mTg[:, fc, 0:n], in0=t1[:, 0:n], in1=t2[:, 0:n], op=ALU.add, reads=[t1, t2], writes=[mTg])
                    for j in range(nt):
                        ti = t0 + j
                        b = ti % 2
                        P.dma("sp", xt2[b][:], I["x"][ti * 128:(ti + 1) * 128, :], writes=[xt2[b]])
                        for hf in range(2):
                            for fc in range(8):
                                P.op("pe", "matmul", out=pH[hf][:, :], lhsT=mTg[:, fc, j * 128:(j + 1) * 128], rhs=wout[:, fc, hf * 512:(hf + 1) * 512], start=(fc == 0), stop=(fc == 7),
                                     reads=[mTg, wout], writes=[pH[hf]])
                            P.op("dve", "tensor_tensor", out=ht[b][:, hf * 512:(hf + 1) * 512], in0=pH[hf][:, :], in1=xt2[b][:, hf * 512:(hf + 1) * 512], op=ALU.add,
                                 reads=[pH[hf], xt2[b]], writes=[ht[b]])
                        P.dma("sp", O["y"][ti * 128:(ti + 1) * 128, :], ht[b][:], reads=[ht[b]], writes=[Oy[ti]])
                        if "h1" in DBG:
                            P.dma("sp", DBG["h1"][ti * 128:(ti + 1) * 128, :], ht[b][:], reads=[ht[b]])
                        P.op("act", "activation", out=jk[:], in_=ht[b][:], func=AF.Square, accum_out=ss2[:, 0:1], reads=[ht[b]], writes=[jk, ss2])
                        P.op("act", "activation", out=ss2[:, 1:2], in_=ss2[:, 0:1], func=AF.Sqrt, scale=1.0 / D, bias=1e-6, reads=[ss2], writes=[ss2])
                        P.op("dve", "reciprocal", out=ss2[:, 1:2], in_=ss2[:, 1:2], reads=[ss2], writes=[ss2])
                        P.op("dve", "tensor_scalar", out=hs[:], in0=ht[b][:], scalar1=ss2[:, 1:2], scalar2=None, op0=ALU.mult, reads=[ht[b], ss2], writes=[hs])
                        for c in range(8):
                            P.op("pe", "transpose", out=pT2[:, c, :], in_=hs[:, c * 128:(c + 1) * 128], identity=idb[:], reads=[hs, idb], writes=[pT2])
                        P.op("dve", "tensor_tensor", out=xnT[:, :, ti * 128:(ti + 1) * 128], in0=pT2[:], in1=n2T[:].unsqueeze(2).to_broadcast([128, 8, 128]), op=ALU.mult,
                             reads=[pT2, n2T], writes=[xnT_t[ti]])
                P.barrier()
            if stop_after == "D":
                P.finish()
                return nc
        with ExitStack() as ef:
            NFF = 22
            wg = sb(ef, "wg", [128, 8, 2816], BF16)
            wup = sb(ef, "wup", [128, 8, 2816], BF16)
            wd = sb(ef, "wd", [128, NFF, 1024], BF16)
            for q4 in range(4):
                cs = slice(q4 * 704, (q4 + 1) * 704)
                load_w("pool", wg, wg[:, :, cs], I["w_gate"][:, cs].rearrange("(c p) n -> p c n", p=128))
                load_w("pool", wup, wup[:, :, cs], I["w_up"][:, cs].rearrange("(c p) n -> p c n", p=128))
            for hf in range(2):
                cs = slice(hf * 512, (hf + 1) * 512)
                load_w("pool", wd, wd[:, :, cs], I["w_down"][:, cs].rearrange("(c p) n -> p c n", p=128))
            nfb = sb(ef, "nfb", [128, D], F32)
            P.dma("sp", nfb[:], I["norm_f"].partition_broadcast(128), writes=[nfb])
            actT = sb(ef, "actT", [128, NFF, 256], BF16)
            sg = [sb(ef, f"sg{i}", [128, 256], F32) for i in range(2)]
            hb = sb(ef, "hb", [128, D], F32)
            jk2 = sb(ef, "jk2", [128, D], BF16)
            ss3 = sb(ef, "ss3", [128, 2], F32)
            pF = [ps(ef, f"pF{i}", [128, 512], F32) for i in range(3)]
            pO = [ps(ef, f"pO{i}", [128, 512], F32) for i in range(2)]
            fgroups = [(t, 2) for t in range(0, 16, 2)] + [(16, 1)]
            k = 0
            for (t0, nt) in fgroups:
                c0 = t0 * 128
                n = nt * 128
                rg = [xnT_t[t0 + j] for j in range(nt)]
                for f in range(NFF):
                    fs = slice(f * 128, (f + 1) * 128)
                    pf = pF[k % 3]
                    s_ = sg[k % 2]
                    k += 1
                    for c in range(8):
                        P.op("pe", "matmul", out=pf[:, 0:n], lhsT=wg[:, c, fs], rhs=xnT[:, c, c0:c0 + n], start=(c == 0), stop=(c == 7), reads=[wg] + rg, writes=[pf])
                    for c in range(8):
                        P.op("pe", "matmul", out=pf[:, 256:256 + n], lhsT=wup[:, c, fs], rhs=xnT[:, c, c0:c0 + n], start=(c == 0), stop=(c == 7), reads=[wup] + rg, writes=[pf])
                    P.op("act", "activation", out=s_[:, 0:n], in_=pf[:, 0:n], func=AF.Silu, reads=[pf], writes=[s_])
                    P.op("dve", "tensor_tensor", out=actT[:, f, 0:n], in0=pf[:, 256:256 + n], in1=s_[:, 0:n], op=ALU.mult, reads=[pf, s_], writes=[actT])
                for j in range(nt):
                    ti = t0 + j
                    P.dma("sp", hb[:], O["y"][ti * 128:(ti + 1) * 128, :], reads=[Oy[ti]], writes=[hb])
                    for hf in range(2):
                        for f in range(NFF):
                            P.op("pe", "matmul", out=pO[hf][:, :], lhsT=actT[:, f, j * 128:(j + 1) * 128], rhs=wd[:, f, hf * 512:(hf + 1) * 512], start=(f == 0), stop=(f == NFF - 1),
                                 reads=[actT, wd], writes=[pO[hf]])
                        P.op("dve", "tensor_tensor", out=hb[:, hf * 512:(hf + 1) * 512], in0=pO[hf][:, :], in1=hb[:, hf * 512:(hf + 1) * 512], op=ALU.add, reads=[pO[hf], hb], writes=[hb])
                    P.op("act", "activation", out=jk2[:], in_=hb[:], func=AF.Square, accum_out=ss3[:, 0:1], reads=[hb], writes=[jk2, ss3])
                    P.op("act", "activation", out=ss3[:, 1:2], in_=ss3[:, 0:1], func=AF.Sqrt, scale=1.0 / D, bias=1e-6, reads=[ss3], writes=[ss3])
                    P.op("dve", "reciprocal", out=ss3[:, 1:2], in_=ss3[:, 1:2], reads=[ss3], writes=[ss3])
                    P.op("dve", "scalar_tensor_tensor", out=hb[:], in0=hb[:], scalar=ss3[:, 1:2], in1=nfb[:], op0=ALU.mult, op1=ALU.mult, reads=[hb, ss3, nfb], writes=[hb])
                    P.dma("sp", O["y"][ti * 128:(ti + 1) * 128, :], hb[:], reads=[hb], writes=[Oy[ti]])
            P.barrier()
        P.finish()
    return nc


def make_consts():
    c = {}
    c["ident_bf"] = np.eye(128, dtype=np.float32).astype(ml_dtypes.bfloat16)
    c["ident_f"] = np.eye(128, dtype=np.float32)
    s = np.arange(128)[:, None]
    t = np.arange(128)[None, :]
    c["tri"] = (s <= t).astype(np.float32)
    c["m_su"] = (s < t).astype(np.float32).astype(ml_dtypes.bfloat16)
    c["m_ui"] = (s <= t).astype(np.float32).astype(ml_dtypes.bfloat16)
    c["m_sl"] = (s > t).astype(np.float32).astype(ml_dtypes.bfloat16)
    el = np.zeros((128, 2), np.float32)
    el[127, 0] = 1.0
    el[31, 1] = 1.0
    c["elast"] = el
    bs = np.zeros((128, 4), np.float32)
    for b in range(4):
        bs[32 * b:32 * b + 32, b] = 1.0
    c["bsel"] = bs
    c["pow2"] = np.tile((0.5 ** np.arange(1, 33)).astype(np.float32)[None, :], (128, 1))
    return c


def core_inputs(inp, c):
    f = lambda a: np.ascontiguousarray(np.asarray(a, dtype=np.float32))
    m = {}
    m["x"] = np.concatenate([f(inp["x_prompt"][c]), f(inp["x_sample"][4 * c:4 * c + 4]).reshape(128, D)], axis=0)
    m["ck"] = f(inp["cache_k"][0, 4 * c:4 * c + 4]).reshape(SB, PAST, 128)
    m["cv"] = f(inp["cache_v"][0, 4 * c:4 * c + 4]).reshape(SB, PAST, 128)
    m["cik"] = f(inp["cache_idx_k"][0, 4 * c:4 * c + 4])
    m["swkv"] = f(inp["state_wkv"][0, 4 * c:4 * c + 4])
    m["sshift"] = f(inp["state_shift"][0, 4 * c:4 * c + 4]).reshape(SB, U_W)
    for n in ("norm1", "w_in", "idx_k_g", "idx_k_b", "shift_mu", "w0", "w2", "a0", "a2", "g2", "k_k", "k_a", "r_k", "gn_w", "gn_b",
              "w_oa", "w_ob", "w_out", "norm2", "w_gate", "w_up", "w_down"):
        m[n] = f(inp[n][0])
    m["norm_f"] = f(inp["norm_f"])
    m.update(make_consts())
    return m


def assemble(res):
    n = len(res)
    B = 8
    y_p = np.stack([r["y"][0:2048] for r in res])
    y_s = np.concatenate([r["y"][2048:].reshape(4, 32, D) for r in res])
    k_p = np.stack([r["ko"][0:2048].reshape(2048, 2, 64) for r in res])[None]
    v_p = np.stack([r["vo"][0:2048].reshape(2048, 2, 64) for r in res])[None]
    ik_p = np.stack([r["iko"][0:2048] for r in res])[None]
    wkv_p = np.stack([r["wkv_p"] for r in res])[None]
    sh_p = np.stack([r["shift_p"] for r in res])[None]
    k_s = np.concatenate([r["ko"][2048:].reshape(4, 32, 2, 64) for r in res])[None]
    v_s = np.concatenate([r["vo"][2048:].reshape(4, 32, 2, 64) for r in res])[None]
    ik_s = np.concatenate([r["iko"][2048:].reshape(4, 32, 64) for r in res])[None]
    wkv_s = np.concatenate([r["wkv_s"] for r in res])[None]
    sh_s = np.concatenate([r["shift_s"].reshape(4, 1, U_W) for r in res])[None]
    outs = (y_p, y_s, k_p, v_p, ik_p, wkv_p, sh_p, k_s, v_s, ik_s, wkv_s, sh_s)
    return tuple(np.ascontiguousarray(o.astype(np.float32)) for o in outs)


def kernel(**inputs):
    nc = build()
    in_maps = [core_inputs(inputs, c) for c in range(8)]
    res = run_bass_kernel_spmd(nc, in_maps, core_ids=list(range(8)))
    return assemble(res.results)
```

```python
import numpy as np
import ml_dtypes
from contextlib import ExitStack
import concourse.bass as bass
import concourse.mybir as mybir
from concourse.bass_utils import run_bass_kernel_spmd

F32 = mybir.dt.float32
BF16 = mybir.dt.bfloat16
AF = mybir.ActivationFunctionType
ALU = mybir.AluOpType
AX = mybir.AxisListType

D = 1024
NT = 17
TOK = NT * 128
NPT = 16
SB = 4
PAST = 4096
LS = PAST + 32
U_W = 1792
NEG = -1.0e30
DEBUG = {}
import os
CUT = float(os.environ.get('KCUT', '99'))
SKIP = os.environ.get('KSKIP', '')
DBG_TILE = int(os.environ.get('KDT', '5'))


class T:
    def __init__(self, t, name="", psum=False):
        self.t = t
        self.w = None
        self.r = []
        self.name = name
        self.psum = psum

    def __getitem__(self, k):
        return self.t[k]

    def sub(self):
        return T(self.t, self.name + "_sub")


class Prog:
    ENG = ("pe", "act", "dve", "pool", "sp")

    def __init__(self, nc, es, n_dma_sems=10):
        self.nc = nc
        self.q = {e: [] for e in self.ENG}
        self.cnt = {e: 0 for e in self.ENG}
        self.seen = {e: {} for e in self.ENG}
        self.sem = {e: es.enter_context(nc.semaphore("s_" + e)) for e in self.ENG}
        self.dq = {}
        for qn in ("sp", "act", "pool"):
            sems = [es.enter_context(nc.semaphore(f"d_{qn}_{i}")) for i in range(n_dma_sems)]
            self.dq[qn] = {"sems": sems, "n": [0] * n_dma_sems, "next": 0}
        self.semobj = dict(self.sem)
        for qn, d in self.dq.items():
            for i, s in enumerate(d["sems"]):
                self.semobj[(qn, i)] = s
        self.ninst = 0

    def _wait(self, eng, key, val):
        if val <= 0 or self.seen[eng].get(key, 0) >= val:
            return
        self.seen[eng][key] = val
        s = self.semobj[key]
        self.q[eng].append(lambda e, s=s, val=val: e.wait_ge(s, val))

    def _deps(self, eng, reads, writes):
        toks = []
        for r in reads:
            if r.w is not None:
                toks.append((r.w, True))
            if r.psum:
                for t in r.r:
                    toks.append((t, False))
        for w in writes:
            if w.w is not None:
                toks.append((w.w, True))
            for t in w.r:
                toks.append((t, False))
        for (key, val), raw in toks:
            if key == eng and (eng == "pe" or not raw):
                continue
            self._wait(eng, key, val)

    def _rec(self, tok, reads, writes):
        for r in reads:
            r.r.append(tok)
        for w in writes:
            w.w = tok
            w.r = []
        self.ninst += 1

    def op(self, eng, name, reads=(), writes=(), **kw):
        self._deps(eng, reads, writes)
        self.cnt[eng] += 1
        tok = (eng, self.cnt[eng])
        s = self.sem[eng]
        self.q[eng].append(lambda e, name=name, kw=kw, s=s: getattr(e, name)(**kw).then_inc(s, 1))
        self._rec(tok, reads, writes)
        return tok

    def dma(self, qn, out, in_, reads=(), writes=(), **kw):
        d = self.dq[qn]
        k = d["next"]
        d["next"] = (k + 1) % len(d["sems"])
        key = (qn, k)
        self._wait(qn, key, 16 * d["n"][k])
        self._deps(qn, reads, writes)
        d["n"][k] += 1
        tok = (key, 16 * d["n"][k])
        s = d["sems"][k]
        self.q[qn].append(lambda e, s=s, out=out, in_=in_, kw=kw: e.dma_start(out=out, in_=in_, **kw).then_inc(s, 16))
        self._rec(tok, reads, writes)
        return tok

    def barrier(self):
        for e in self.ENG:
            for qn, d in self.dq.items():
                for k in range(len(d["sems"])):
                    self._wait(e, (qn, k), 16 * d["n"][k])
            for e2 in self.ENG:
                if e2 != e:
                    self._wait(e, e2, self.cnt[e2])

    def finish(self):
        self.barrier()
        q = self.q
        with self.nc.Block() as block:
            @block.sync
            def _(e):
                for f in q["sp"]:
                    f(e)

            @block.tensor
            def _(e):
                for f in q["pe"]:
                    f(e)

            @block.scalar
            def _(e):
                for f in q["act"]:
                    f(e)

            @block.vector
            def _(e):
                for f in q["dve"]:
                    f(e)

            @block.gpsimd
            def _(e):
                for f in q["pool"]:
                    f(e)


IN_OFF = {}
_o = 0
for _n, _w in (("q", 512), ("k", 128), ("v", 128), ("iq", 1024), ("ik", 64), ("iw", 16), ("u", 1792), ("ga", 1024), ("gb", 1024)):
    IN_OFF[_n] = (_o, _w)
    _o += _w

IN_SPECS = [
    ("x", [TOK, D], F32), ("ck", [SB, PAST, 128], F32), ("cv", [SB, PAST, 128], F32), ("cik", [SB, PAST, 64], F32),
    ("swkv", [SB, 8, 64, 64], F32), ("sshift", [SB, U_W], F32),
    ("norm1", [D], F32), ("w_in", [D, 5712], F32), ("idx_k_g", [64], F32), ("idx_k_b", [64], F32),
    ("shift_mu", [U_W], F32), ("w0", [512], F32), ("w2", [64, 512], F32), ("a0", [512], F32), ("a2", [64, 512], F32),
    ("g2", [128, 512], F32), ("k_k", [512], F32), ("k_a", [512], F32), ("r_k", [512], F32), ("gn_w", [512], F32),
    ("gn_b", [512], F32), ("w_oa", [512, D], F32), ("w_ob", [512, D], F32), ("w_out", [D, D], F32), ("norm2", [D], F32),
    ("w_gate", [D, 2816], F32), ("w_up", [D, 2816], F32), ("w_down", [2816, D], F32), ("norm_f", [D], F32),
    ("ident_bf", [128, 128], BF16), ("ident_f", [128, 128], F32), ("tri", [128, 128], F32),
    ("m_su", [128, 128], BF16), ("m_ui", [128, 128], BF16), ("m_sl", [128, 128], BF16), ("elast", [128, 2], F32), ("bsel", [128, 4], F32), ("pow2", [128, 32], F32),
]
OUT_SPECS = [
    ("y", [TOK, D], F32), ("ko", [TOK, 128], F32), ("vo", [TOK, 128], F32), ("iko", [TOK, 64], F32),
    ("wkv_p", [8, 64, 64], F32), ("shift_p", [1, U_W], F32), ("wkv_s", [SB, 8, 64, 64], F32), ("shift_s", [SB, U_W], F32),
]


def build(stop_after="Z", dbg=()):
    nc = bass.Bass("TRN2", target_bir_lowering=False)
    I = {n: nc.dram_tensor(n, s, d, kind="ExternalInput").ap() for n, s, d in IN_SPECS}
    O = {n: nc.dram_tensor(n, s, d, kind="ExternalOutput").ap() for n, s, d in OUT_SPECS}
    DBG = {}
    for n, s in dbg:
        DBG[n] = nc.dram_tensor("dbg_" + n, s, F32, kind="ExternalOutput").ap()
    Od = {n: T(None, n) for n in O}
    with ExitStack() as es:
        P = Prog(nc, es)

        def sb(ctx, name, shape, dt=F32):
            return T(ctx.enter_context(nc.sbuf_tensor("sb_" + name, shape, dt)), name)

        def ps(ctx, name, shape, dt=F32):
            return T(ctx.enter_context(nc.psum_tensor("ps_" + name, shape, dt)), name, psum=True)

        def dbg_out(name, src, rows, cols, t):
            if name in DBG:
                P.dma("sp", DBG[name][0:rows, 0:cols], src, reads=[t])

        xnT = sb(es, "xnT", [128, 8, TOK], BF16)
        xnT_t = [xnT.sub() for _ in range(NT)]
        idb = sb(es, "idb", [128, 128], BF16)
        idf = sb(es, "idf", [128, 128], F32)
        P.dma("sp", idb[:], I["ident_bf"], writes=[idb])
        P.dma("sp", idf[:], I["ident_f"], writes=[idf])

        def load_w(qn, dst, dst_ap, src_ap):
            P.dma(qn, dst_ap, src_ap, writes=[dst])

        def wsl(name, c0, c1):
            return I["w_in"][:, c0:c1].rearrange("(c p) n -> p c n", p=128)

        with ExitStack() as e1:
            yBT = sb(e1, "yBT", [128, 4, TOK], BF16)
            kTs = sb(e1, "kTs", [128, 128], BF16)
            ikTs = sb(e1, "ikTs", [128, 128], BF16)
            vTs = sb(e1, "vTs", [128, 128], BF16)
            kT = sb(e1, "kT", [128, 2048], BF16)
            ikT = sb(e1, "ikT", [128, 2048], BF16)
            vaug = sb(e1, "vaug", [128, 16, 2, 65], BF16)
            iws = sb(e1, "iws", [128, NT, 16], F32)
            ewu = ExitStack()
            wu = sb(ewu, "wu", [128, 8, U_W], BF16)
            o_u = IN_OFF["u"][0]
            for c4 in range(4):
                load_w("pool", wu, wu[:, :, c4 * 448:(c4 + 1) * 448], wsl("u", o_u + c4 * 448, o_u + (c4 + 1) * 448))
            with ExitStack() as ea:
                n1T = sb(ea, "n1T", [128, 8], F32)
                P.dma("sp", n1T[:], I["norm1"].rearrange("(c p) -> p c", p=128), writes=[n1T], allow_slow_non_contiguous=True)
                wkvi = sb(ea, "wkvi", [128, 8, 336], BF16)
                o_k = IN_OFF["k"][0]
                if 'w' not in SKIP:
                    load_w("pool", wkvi, wkvi[:, :, 0:256], wsl("kv", o_k, o_k + 256))
                o_ik = IN_OFF["ik"][0]
                if 'v' not in SKIP:
                    load_w("pool", wkvi, wkvi[:, :, 256:336], wsl("iki", o_ik, o_ik + 80))
                ikg = sb(ea, "ikg", [128, 64], F32)
                ikb = sb(ea, "ikb", [128, 64], F32)
                if 'b' not in SKIP:
                    P.dma("sp", ikg[:], I["idx_k_g"].partition_broadcast(128), writes=[ikg])
                    P.dma("sp", ikb[:], I["idx_k_b"].partition_broadcast(128), writes=[ikb])
                if 'm' not in SKIP:
                    P.op("pool", "memset", ap=vaug[:, :, :, 64:65], constant=1.0, writes=[vaug])
                xt = [sb(ea, f"xt{i}", [128, D], F32) for i in range(2)]
                junk = sb(ea, "junk", [128, D], F32)
                ss = [sb(ea, f"ss{i}", [128, 2], F32) for i in range(2)]
                xs = [sb(ea, f"xs{i}", [128, D], BF16) for i in range(2)]
                kvf = [sb(ea, f"kvf{i}", [128, 256], F32) for i in range(2)]
                ikf = [sb(ea, f"ikf{i}", [128, 64], F32) for i in range(2)]
                st6 = sb(ea, "st6", [128, 6], F32)
                mv = sb(ea, "mv", [128, 4], F32)
                kvb = [sb(ea, f"kvb{i}", [128, 384], BF16) for i in range(2)]
                pT = [ps(ea, f"pT{i}", [128, 8, 128], BF16) for i in range(2)]
                pkv = [ps(ea, f"pkv{i}", [128, 512], F32) for i in range(2)]
                pk2 = [ps(ea, f"pk2{i}", [128, 8, 128], BF16) for i in range(2)]
                def a1(i):
                    b = i % 2
                    P.dma("sp", xt[b][:], I["x"][i * 128:(i + 1) * 128, :], writes=[xt[b]])
                    P.op("act", "activation", out=junk[:], in_=xt[b][:], func=AF.Square, accum_out=ss[b][:, 0:1], reads=[xt[b]], writes=[junk, ss[b]])
                    P.op("act", "activation", out=ss[b][:, 1:2], in_=ss[b][:, 0:1], func=AF.Sqrt, scale=1.0 / D, bias=1e-6, reads=[ss[b]], writes=[ss[b]])
                    P.op("dve", "reciprocal", out=ss[b][:, 1:2], in_=ss[b][:, 1:2], reads=[ss[b]], writes=[ss[b]])
                    P.op("dve", "tensor_scalar", out=xs[b][:], in0=xt[b][:], scalar1=ss[b][:, 1:2], scalar2=None, op0=ALU.mult, reads=[xt[b], ss[b]], writes=[xs[b]])
                    for c in range(8):
                        P.op("pe", "transpose", out=pT[b][:, c, :], in_=xs[b][:, c * 128:(c + 1) * 128], identity=idb[:], reads=[xs[b], idb], writes=[pT[b]])
                    P.op("dve", "tensor_tensor", out=xnT[:, :, i * 128:(i + 1) * 128], in0=pT[b][:], in1=n1T[:].unsqueeze(2).to_broadcast([128, 8, 128]), op=ALU.mult,
                         reads=[pT[b], n1T], writes=[xnT_t[i]])

                def a2(i):
                    b = i % 2
                    for c in range(8):
                        P.op("pe", "matmul", out=pkv[b][:, 0:336], lhsT=xnT[:, c, i * 128:(i + 1) * 128], rhs=wkvi[:, c, :], start=(c == 0), stop=(c == 7),
                             reads=[xnT_t[i], wkvi], writes=[pkv[b]])
                    P.op("act", "copy", out=kvf[b][:], in_=pkv[b][:, 0:256], reads=[pkv[b]], writes=[kvf[b]])
                    P.dma("sp", O["ko"][i * 128:(i + 1) * 128, :], kvf[b][:, 0:128], reads=[kvf[b]])
                    P.dma("sp", O["vo"][i * 128:(i + 1) * 128, :], kvf[b][:, 128:256], reads=[kvf[b]])
                    P.op("dve", "tensor_scalar", out=iws[:, i, :], in0=pkv[b][:, 320:336], scalar1=1.0 / 32, scalar2=None, op0=ALU.mult, reads=[pkv[b]], writes=[iws])
                    P.op("dve", "tensor_reduce", out=mv[:, 0:1], in_=pkv[b][:, 256:320], axis=AX.X, op=ALU.add, reads=[pkv[b]], writes=[mv])
                    P.op("dve", "tensor_scalar", out=mv[:, 0:1], in0=mv[:, 0:1], scalar1=1.0 / 64, scalar2=None, op0=ALU.mult, reads=[mv], writes=[mv])
                    P.op("dve", "tensor_scalar", out=ikf[b][:], in0=pkv[b][:, 256:320], scalar1=mv[:, 0:1], scalar2=None, op0=ALU.subtract, reads=[pkv[b], mv], writes=[ikf[b]])
                    P.op("act", "activation", out=junk[:, 0:64], in_=ikf[b][:], func=AF.Square, accum_out=mv[:, 1:2], reads=[ikf[b]], writes=[junk, mv])
                    P.op("act", "activation", out=mv[:, 2:3], in_=mv[:, 1:2], func=AF.Sqrt, scale=1.0 / 64, bias=1e-6, reads=[mv], writes=[mv])
                    P.op("dve", "reciprocal", out=mv[:, 3:4], in_=mv[:, 2:3], reads=[mv], writes=[mv])
                    P.op("dve", "tensor_scalar", out=ikf[b][:], in0=ikf[b][:], scalar1=mv[:, 3:4], scalar2=None, op0=ALU.mult, reads=[ikf[b], mv], writes=[ikf[b]])
                    P.op("dve", "tensor_tensor", out=ikf[b][:], in0=ikf[b][:], in1=ikg[:], op=ALU.mult, reads=[ikf[b], ikg], writes=[ikf[b]])
                    P.op("dve", "tensor_tensor", out=ikf[b][:], in0=ikf[b][:], in1=ikb[:], op=ALU.add, reads=[ikf[b], ikb], writes=[ikf[b]])
                    P.dma("sp", O["iko"][i * 128:(i + 1) * 128, :], ikf[b][:], reads=[ikf[b]])

                def a3(i):
                    b = i % 2
                    P.op("act", "copy", out=kvb[b][:, 0:128], in_=kvf[b][:, 0:128], reads=[kvf[b]], writes=[kvb[b]])
                    P.op("pool", "tensor_copy", out=kvb[b][:, 128:192], in_=ikf[b][:], reads=[ikf[b]], writes=[kvb[b]])
                    P.op("pool", "tensor_copy", out=kvb[b][:, 192:256], in_=ikf[b][:], reads=[ikf[b]], writes=[kvb[b]])
                    P.op("act", "copy", out=kvb[b][:, 256:384], in_=kvf[b][:, 128:256], reads=[kvf[b]], writes=[kvb[b]])
                    nT = 2 if i < NPT else 3
                    for c in range(nT):
                        P.op("pe", "transpose", out=pk2[b][:, c, :], in_=kvb[b][:, c * 128:(c + 1) * 128], identity=idb[:], reads=[kvb[b], idb], writes=[pk2[b]])
                    if i < NPT:
                        P.op("act", "copy", out=kT[:, i * 128:(i + 1) * 128], in_=pk2[b][:, 0, :], reads=[pk2[b]], writes=[kT])
                        P.op("act", "copy", out=ikT[:, i * 128:(i + 1) * 128], in_=pk2[b][:, 1, :], reads=[pk2[b]], writes=[ikT])
                        P.op("pool", "tensor_copy", out=vaug[:, i, :, 0:64], in_=kvb[b][:, 256:384].rearrange("p (k d) -> p k d", k=2), reads=[kvb[b]], writes=[vaug])
                    else:
                        P.op("act", "copy", out=kTs[:], in_=pk2[b][:, 0, :], reads=[pk2[b]], writes=[kTs])
                        P.op("act", "copy", out=ikTs[:], in_=pk2[b][:, 1, :], reads=[pk2[b]], writes=[ikTs])
                        P.op("act", "copy", out=vTs[:], in_=pk2[b][:, 2, :], reads=[pk2[b]], writes=[vTs])

                for step in range(NT + 2):
                    if step < NT:
                        a1(step)
                    if 1 <= step < NT + 1:
                        a2(step - 1)
                    if step >= 2:
                        a3(step - 2)
                P.barrier()
            if stop_after == "A":
                ewu.close()
                P.finish()
                return nc

            with ExitStack() as eb:
                mu_bc = sb(eb, "mu_bc", [128, U_W], F32)
                P.dma("sp", mu_bc[:], I["shift_mu"].partition_broadcast(128), writes=[mu_bc])
                pbc = {}
                for n in ("w0", "a0", "k_k", "k_a", "r_k", "gn_w", "gn_b"):
                    pbc[n] = sb(eb, "bc_" + n, [128, 512], F32)
                    P.dma("act", pbc[n][:], I[n].partition_broadcast(128), writes=[pbc[n]])
                w2a2 = sb(eb, "w2a2", [128, 512], BF16)
                g2b = sb(eb, "g2b", [128, 512], BF16)
                P.dma("pool", w2a2[0:64, :], I["w2"], writes=[w2a2])
                P.dma("pool", w2a2[64:128, :], I["a2"], writes=[w2a2])
                P.dma("pool", g2b[:], I["g2"], writes=[g2b])
                tri = sb(eb, "tri_s", [128, 128], F32)
                P.dma("sp", tri[:], I["tri"], writes=[tri])
                msui = sb(eb, "msui", [128, 2, 128], BF16)
                msl = sb(eb, "msl", [128, 128], BF16)
                P.dma("sp", msui[:, 0, :], I["m_su"], writes=[msui])
                P.dma("sp", msui[:, 1, :], I["m_ui"], writes=[msui])
                P.dma("sp", msl[:], I["m_sl"], writes=[msl])
                elast = sb(eb, "elast_s", [128, 2], F32)
                P.dma("sp", elast[:], I["elast"], writes=[elast])

                uf = [sb(eb, f"uf{i}", [128, U_W], F32) for i in range(2)]
                mmb = [sb(eb, f"mmb{i}", [128, U_W], F32) for i in range(2)]
                lo = sb(eb, "lo", [128, 256], BF16)
                lts = sb(eb, "lts", [128, 64], F32)
                loTb = [sb(eb, f"loT{i}", [128, 2, 128], BF16) for i in range(2)]
                S0 = ps(eb, "S0", [128, 512], F32)
                S1 = ps(eb, "S1", [128, 512], F32)
                S0b = S0.t[:].bitcast(BF16)

                class Lane:
                    pass

                lanes = []
                for hg in range(2):
                    Ln = Lane()
                    nm = lambda n: f"{n}_{hg}"
                    for n in ("f_a", "f_w", "f_kk", "f_k", "f_t", "f_g", "f_ep", "f_em", "f_ex", "UlT"):
                        setattr(Ln, n, sb(eb, nm(n), [128, 256], F32))
                    Ln.sm = sb(eb, nm("sm"), [128, 40], F32)
                    for n in ("Rt", "Kt", "Bt", "At", "Vb", "WlT", "UT", "yb16"):
                        setattr(Ln, n, sb(eb, nm(n), [128, 256], BF16))
                    Ln.ARF = sb(eb, nm("ARF"), [64, 4, 2, 128], BF16)
                    Ln.BFm = sb(eb, nm("BFm"), [64, 4, 128], BF16)
                    Ln.KFm = sb(eb, nm("KFm"), [64, 4, 128], BF16)
                    Ln.Mb = sb(eb, nm("Mb"), [128, 4, 2, 128], BF16)
                    Ln.Mk = sb(eb, nm("Mk"), [128, 4, 2, 128], BF16)
                    for n in ("MbaT", "A2", "A2T", "Tm"):
                        setattr(Ln, n, sb(eb, nm(n), [128, 4, 128], BF16))
                    Ln.AFp = sb(eb, nm("AFp"), [64, 4, 128], BF16)
                    Ln.ST = sb(eb, nm("ST"), [64, 4, 64], F32)
                    Ln.Sb = sb(eb, nm("Sb"), [64, 4, 64], BF16)
                    Ln.gC = sb(eb, nm("gC"), [64, 4], F32)
                    Ln.sto = sb(eb, nm("sto"), [64, 4, 64], F32)
                    Ln.Q = [ps(eb, nm(f"Q{j}"), [128, 512], F32) for j in range(3)]
                    Ln.Qb = [q.t[:].bitcast(BF16) for q in Ln.Q]
                    lanes.append(Ln)

                def v3(ap, a):
                    return ap.rearrange("p (a b) -> p a b", a=a)

                def shared(ti, C, col0, ub, prev_row_src, out_shift_ap):
                    u = uf[ub]
                    mm = mmb[ub]
                    loT = loTb[ub]
                    for g4 in range(4):
                        pz = (S0, S1)[g4 % 2]
                        for c in range(8):
                            P.op("pe", "matmul", out=pz[0:C, 0:448], lhsT=xnT[:, c, col0:col0 + C], rhs=wu[:, c, g4 * 448:(g4 + 1) * 448],
                                 start=(c == 0), stop=(c == 7), reads=[xnT_t[ti], wu], writes=[pz])
                        P.op("act", "copy", out=u[0:C, g4 * 448:(g4 + 1) * 448], in_=pz[0:C, 0:448], reads=[pz], writes=[u])
                    if out_shift_ap is not None:
                        P.dma("sp", out_shift_ap, u[C - 1:C, :], reads=[u])
                    r16 = ((C - 1) // 16) * 16
                    qsh = ("sp", "act")
                    if r16 > 0:
                        P.dma(qsh[0], mm[1:1 + r16, :], u[0:r16, :], reads=[u], writes=[mm])
                    if C - 1 - r16 > 0:
                        P.dma(qsh[1], mm[1 + r16:C, :], u[r16:C - 1, :], reads=[u], writes=[mm])
                    if prev_row_src is None:
                        P.op("dve", "memset", ap=mm[0:1, :], constant=0.0, writes=[mm])
                    else:
                        src, rd = prev_row_src
                        P.dma("sp", mm[0:1, :], src, reads=rd, writes=[mm])
                    P.op("dve", "tensor_tensor", out=mm[0:C, :], in0=mm[0:C, :], in1=u[0:C, :], op=ALU.subtract, reads=[mm, u], writes=[mm])
                    P.op("dve", "tensor_tensor", out=mm[0:C, :], in0=mm[0:C, :], in1=mu_bc[0:C, :], op=ALU.mult, reads=[mm, mu_bc], writes=[mm])
                    P.op("dve", "tensor_tensor", out=mm[0:C, :], in0=mm[0:C, :], in1=u[0:C, :], op=ALU.add, reads=[mm, u], writes=[mm])

                def shared_b(ti, C, col0, ub):
                    mm = mmb[ub]
                    loT = loTb[ub]
                    P.op("act", "activation", out=lts[0:C, 0:64], in_=mm[0:C, 1536:1600], func=AF.Sigmoid, scale=2.0, reads=[mm], writes=[lts])
                    P.op("dve", "tensor_scalar", out=lo[0:C, 0:64], in0=lts[0:C, 0:64], scalar1=2.0, scalar2=-1.0, op0=ALU.mult, op1=ALU.add, reads=[lts], writes=[lo])
                    P.op("dve", "tensor_copy", out=lo[0:C, 64:128], in_=mm[0:C, 1600:1664], reads=[mm], writes=[lo])
                    P.op("act", "activation", out=lo[0:C, 128:256], in_=mm[0:C, 1664:1792], func=AF.Sigmoid, reads=[mm], writes=[lo])
                    for c in range(2):
                        P.op("pe", "transpose", out=S0b[:, c * 128:c * 128 + C], in_=lo[0:C, c * 128:(c + 1) * 128], identity=idb[0:C, 0:C], reads=[lo, idb], writes=[S0])
                    P.op("act", "copy", out=loT[:, :, 0:C], in_=v3(S0b[:, 0:256], 2)[:, :, 0:C], reads=[S0], writes=[loT])

                def lane(ti, C, col0, ub, hg, first, state_src, out_state):
                    Ln = lanes[hg]
                    Q0, Q1, Q2 = Ln.Q
                    Q0b, Q1b, Q2b = Ln.Qb
                    mm = mmb[ub]
                    loT = loTb[ub]
                    cs = slice(hg * 256, (hg + 1) * 256)
                    r_ = mm[0:C, hg * 256:(hg + 1) * 256]
                    kr_ = mm[0:C, 512 + hg * 256:512 + (hg + 1) * 256]
                    v_ = mm[0:C, 1024 + hg * 256:1024 + (hg + 1) * 256]
                    f_a, f_w, f_kk, f_k, f_t, f_g, f_ep, f_em, f_ex, sm = Ln.f_a, Ln.f_w, Ln.f_kk, Ln.f_k, Ln.f_t, Ln.f_g, Ln.f_ep, Ln.f_em, Ln.f_ex, Ln.sm
                    Rt, Kt, Bt, At, Vb = Ln.Rt, Ln.Kt, Ln.Bt, Ln.At, Ln.Vb
                    ARF, BFm, KFm, Mb, Mk, MbaT, Tm = Ln.ARF, Ln.BFm, Ln.KFm, Ln.Mb, Ln.Mk, Ln.MbaT, Ln.Tm
                    WlT, UlT, AFp, UT, ST, Sb, gC, sto, yb16 = Ln.WlT, Ln.UlT, Ln.AFp, Ln.UT, Ln.ST, Ln.Sb, Ln.gC, Ln.sto, Ln.yb16
                    yv = f_w
                    pb = lambda n: pbc[n][0:C, cs]
                    P.op("pe", "matmul", out=Q0[0:C, 0:256], lhsT=loT[0:64, 0, 0:C], rhs=w2a2[0:64, cs], start=True, stop=True, reads=[loT, w2a2], writes=[Q0])
                    P.op("pe", "matmul", out=Q1[0:C, 0:256], lhsT=loT[64:128, 0, 0:C], rhs=w2a2[64:128, cs], start=True, stop=True, reads=[loT, w2a2], writes=[Q1])
                    P.op("pe", "matmul", out=Q2[0:C, 0:256], lhsT=loT[:, 1, 0:C], rhs=g2b[:, cs], start=True, stop=True, reads=[loT, g2b], writes=[Q2])
                    P.op("dve", "tensor_tensor", out=f_w[0:C, :], in0=Q0[0:C, 0:256], in1=pb("w0"), op=ALU.add, reads=[Q0, pbc["w0"]], writes=[f_w])
                    P.op("dve", "tensor_tensor", out=f_a[0:C, :], in0=Q1[0:C, 0:256], in1=pb("a0"), op=ALU.add, reads=[Q1, pbc["a0"]], writes=[f_a])
                    P.op("act", "activation", out=f_w[0:C, :], in_=f_w[0:C, :], func=AF.Sigmoid, reads=[f_w], writes=[f_w])
                    P.op("act", "activation", out=f_a[0:C, :], in_=f_a[0:C, :], func=AF.Sigmoid, reads=[f_a], writes=[f_a])
                    P.op("act", "copy", out=f_g[0:C, :], in_=Q2[0:C, 0:256], reads=[Q2], writes=[f_g])
                    P.op("dve", "tensor_scalar", out=f_w[0:C, :], in0=f_w[0:C, :], scalar1=-0.6065306597126334, scalar2=None, op0=ALU.mult, reads=[f_w], writes=[f_w])
                    yield
                    P.op("pe", "matmul", out=Q0[0:C, 256:512], lhsT=tri[0:C, 0:C], rhs=f_w[0:C, :], start=True, stop=True, reads=[tri, f_w], writes=[Q0])
                    P.op("dve", "tensor_tensor", out=f_kk[0:C, :], in0=kr_, in1=pb("k_k"), op=ALU.mult, reads=[mm, pbc["k_k"]], writes=[f_kk])
                    P.op("dve", "tensor_tensor", out=f_t[0:C, :], in0=f_kk[0:C, :], in1=f_kk[0:C, :], op=ALU.mult, reads=[f_kk], writes=[f_t])
                    P.op("act", "activation", out=f_ep[0:C, :], in_=Q0[0:C, 256:512], func=AF.Exp, reads=[Q0], writes=[f_ep])
                    P.op("act", "activation", out=f_em[0:C, :], in_=Q0[0:C, 256:512], func=AF.Exp, scale=-1.0, reads=[Q0], writes=[f_em])
                    P.op("dve", "tensor_tensor", out=f_ex[0:C, :], in0=Q0[0:C, 256:512], in1=f_w[0:C, :], op=ALU.subtract, reads=[Q0, f_w], writes=[f_ex])
                    P.op("act", "activation", out=f_ex[0:C, :], in_=f_ex[0:C, :], func=AF.Exp, reads=[f_ex], writes=[f_ex])
                    yield
                    P.op("dve", "tensor_reduce", out=sm[0:C, 0:4], in_=v3(f_t[0:C, :], 4), axis=AX.X, op=ALU.add, reads=[f_t], writes=[sm])
                    P.op("dve", "tensor_scalar", out=sm[0:C, 0:4], in0=sm[0:C, 0:4], scalar1=1e-24, scalar2=None, op0=ALU.max, reads=[sm], writes=[sm])
                    P.op("act", "activation", out=sm[0:C, 0:4], in_=sm[0:C, 0:4], func=AF.Ln, reads=[sm], writes=[sm])
                    P.op("act", "activation", out=sm[0:C, 0:4], in_=sm[0:C, 0:4], func=AF.Exp, scale=-0.5, reads=[sm], writes=[sm])
                    P.op("dve", "tensor_tensor", out=v3(f_kk[0:C, :], 4), in0=v3(f_kk[0:C, :], 4), in1=sm[0:C, 0:4].unsqueeze(2).to_broadcast([C, 4, 64]), op=ALU.mult,
                         reads=[f_kk, sm], writes=[f_kk])
                    P.op("dve", "scalar_tensor_tensor", out=f_k[0:C, :], in0=f_a[0:C, :], scalar=-1.0, in1=pb("k_a"), op0=ALU.add, op1=ALU.mult,
                         reads=[f_a, pbc["k_a"]], writes=[f_k])
                    P.op("dve", "scalar_tensor_tensor", out=f_k[0:C, :], in0=f_k[0:C, :], scalar=1.0, in1=kr_, op0=ALU.add, op1=ALU.mult, reads=[f_k, mm], writes=[f_k])
                    yield
                    P.op("dve", "tensor_tensor", out=f_t[0:C, :], in0=r_, in1=f_k[0:C, :], op=ALU.mult, reads=[mm, f_k], writes=[f_t])
                    P.op("dve", "tensor_tensor", out=f_t[0:C, :], in0=f_t[0:C, :], in1=pb("r_k"), op=ALU.mult, reads=[f_t, pbc["r_k"]], writes=[f_t])
                    P.op("dve", "tensor_reduce", out=sm[0:C, 4:8], in_=v3(f_t[0:C, :], 4), axis=AX.X, op=ALU.add, reads=[f_t], writes=[sm])
                    P.op("dve", "tensor_tensor", out=Rt[0:C, :], in0=r_, in1=f_ep[0:C, :], op=ALU.mult, reads=[mm, f_ep], writes=[Rt])
                    P.op("dve", "tensor_tensor", out=Kt[0:C, :], in0=f_k[0:C, :], in1=f_em[0:C, :], op=ALU.mult, reads=[f_k, f_em], writes=[Kt])
                    P.op("dve", "tensor_tensor", out=f_t[0:C, :], in0=f_kk[0:C, :], in1=f_a[0:C, :], op=ALU.mult, reads=[f_kk, f_a], writes=[f_t])
                    P.op("dve", "tensor_tensor", out=Bt[0:C, :], in0=f_t[0:C, :], in1=f_em[0:C, :], op=ALU.mult, reads=[f_t, f_em], writes=[Bt])
                    P.op("dve", "scalar_tensor_tensor", out=At[0:C, :], in0=f_kk[0:C, :], scalar=-1.0, in1=f_ex[0:C, :], op0=ALU.mult, op1=ALU.mult, reads=[f_kk, f_ex], writes=[At])
                    P.op("act", "copy", out=Vb[0:C, :], in_=v_, reads=[mm], writes=[Vb])
                    yield
                    for h in range(4):
                        hc = slice(h * 64, (h + 1) * 64)
                        P.op("pe", "transpose", out=Q2b[0:64, h * 256:h * 256 + C], in_=At[0:C, hc], identity=idb[0:C, 0:C], reads=[At, idb], writes=[Q2])
                        P.op("pe", "transpose", out=Q2b[0:64, h * 256 + 128:h * 256 + 128 + C], in_=Rt[0:C, hc], identity=idb[0:C, 0:C], reads=[Rt, idb], writes=[Q2])
                        P.op("pe", "transpose", out=Q0b[0:64, h * 128:h * 128 + C], in_=Bt[0:C, hc], identity=idb[0:C, 0:C], reads=[Bt, idb], writes=[Q0])
                        P.op("pe", "transpose", out=Q0b[0:64, 512 + h * 128:512 + h * 128 + C], in_=Kt[0:C, hc], identity=idb[0:C, 0:C], reads=[Kt, idb], writes=[Q0])
                    P.op("act", "copy", out=ARF[:, :, :, 0:C], in_=Q2b[0:64, :].rearrange("p (h a t) -> p h a t", h=4, a=2)[:, :, :, 0:C], reads=[Q2], writes=[ARF])
                    P.op("dve", "tensor_copy", out=BFm[:, :, 0:C], in_=v3(Q0b[0:64, 0:512], 4)[:, :, 0:C], reads=[Q0], writes=[BFm])
                    P.op("dve", "tensor_copy", out=KFm[:, :, 0:C], in_=v3(Q0b[0:64, 512:1024], 4)[:, :, 0:C], reads=[Q0], writes=[KFm])
                    yield
                    bq = (Q1, Q2)
                    for h in range(4):
                        q = bq[h // 2]
                        P.op("pe", "matmul", out=q[0:C, (h % 2) * 256:(h % 2 + 1) * 256].rearrange("p (a t) -> p a t", a=2)[:, :, 0:C], lhsT=BFm[:, h, 0:C], rhs=ARF[:, h, :, 0:C],
                             start=True, stop=True, reads=[BFm, ARF], writes=[q])
                    mbc = msui[0:C, :, 0:C].unsqueeze(1).to_broadcast([C, 2, 2, C])
                    for j in range(2):
                        P.op("dve", "tensor_tensor", out=Mb[0:C, 2 * j:2 * j + 2, :, 0:C], in0=bq[j][0:C, :].rearrange("p (h a t) -> p h a t", h=2, a=2)[:, :, :, 0:C], in1=mbc, op=ALU.mult,
                             reads=[bq[j], msui], writes=[Mb])
                    yield
                    kq = (Q0, Q1)
                    for h in range(4):
                        q = kq[h // 2]
                        P.op("pe", "matmul", out=q[0:C, (h % 2) * 256:(h % 2 + 1) * 256].rearrange("p (a t) -> p a t", a=2)[:, :, 0:C], lhsT=KFm[:, h, 0:C], rhs=ARF[:, h, :, 0:C],
                             start=True, stop=True, reads=[KFm, ARF], writes=[q])
                    for h in range(4):
                        P.op("pe", "matmul", out=Q2[0:C, h * 128:h * 128 + C], lhsT=ARF[:, h, 0, 0:C], rhs=BFm[:, h, 0:C], start=True, stop=True, reads=[ARF, BFm], writes=[Q2])
                    for j in range(2):
                        P.op("dve", "tensor_tensor", out=Mk[0:C, 2 * j:2 * j + 2, :, 0:C], in0=kq[j][0:C, :].rearrange("p (h a t) -> p h a t", h=2, a=2)[:, :, :, 0:C], in1=mbc, op=ALU.mult,
                             reads=[kq[j], msui], writes=[Mk])
                    P.op("dve", "tensor_tensor", out=MbaT[0:C, :, 0:C], in0=v3(Q2[0:C, :], 4)[:, :, 0:C], in1=msl[0:C, 0:C].unsqueeze(1).to_broadcast([C, 4, C]), op=ALU.mult,
                         reads=[Q2, msl], writes=[MbaT])
                    P.op("dve", "tensor_tensor", out=Tm[0:C, :, 0:C], in0=Mb[0:C, :, 0, 0:C], in1=idb[0:C, 0:C].unsqueeze(1).to_broadcast([C, 4, C]), op=ALU.add,
                         reads=[Mb, idb], writes=[Tm])
                    yield
                    nlev = {128: 6, 32: 4}[C]

                    class View:
                        def __init__(self, t, f):
                            self.t, self.f = t, f

                    cur = (View(Mb, lambda h: Mb[0:C, h, 0, 0:C]), View(MbaT, lambda h: MbaT[0:C, h, 0:C]))
                    nxt = (View(Ln.A2, lambda h: Ln.A2[0:C, h, 0:C]), View(Ln.A2T, lambda h: Ln.A2T[0:C, h, 0:C]))
                    for lv in range(nlev):
                        Ac, ATc = cur
                        An, ATn = nxt
                        for h in range(4):
                            P.op("pe", "matmul", out=Q0[0:C, h * 128:h * 128 + C], lhsT=ATc.f(h), rhs=Ac.f(h), start=True, stop=True, reads=[ATc.t, Ac.t], writes=[Q0])
                        for h in range(4):
                            P.op("pe", "matmul", out=Q1[0:C, h * 128:h * 128 + C], lhsT=Ac.f(h), rhs=ATc.f(h), start=True, stop=True, reads=[ATc.t, Ac.t], writes=[Q1])
                        if An.t is Mb:
                            P.op("act", "copy", out=Mb[0:C, :, 0, 0:C], in_=v3(Q0[0:C, :], 4)[:, :, 0:C], reads=[Q0], writes=[Mb])
                        else:
                            P.op("act", "copy", out=An.t[0:C, :, 0:C], in_=v3(Q0[0:C, :], 4)[:, :, 0:C], reads=[Q0], writes=[An.t])
                        P.op("dve", "tensor_copy", out=ATn.t[0:C, :, 0:C], in_=v3(Q1[0:C, :], 4)[:, :, 0:C], reads=[Q1], writes=[ATn.t])
                        yield
                        for h in range(4):
                            P.op("pe", "matmul", out=Q2[0:C, h * 128:h * 128 + C], lhsT=ATn.f(h), rhs=Tm[0:C, h, 0:C], start=True, stop=True, reads=[ATn.t, Tm], writes=[Q2])
                        P.op("dve", "tensor_tensor", out=Tm[0:C, :, 0:C], in0=v3(Q2[0:C, :], 4)[:, :, 0:C], in1=Tm[0:C, :, 0:C], op=ALU.add, reads=[Q2, Tm], writes=[Tm])
                        cur, nxt = nxt, cur
                        yield
                    for h in range(4):
                        hc = slice(h * 64, (h + 1) * 64)
                        P.op("pe", "matmul", out=Q0[0:C, hc], lhsT=Mk[0:C, h, 0, 0:C], rhs=Vb[0:C, hc], start=True, stop=True, reads=[Mk, Vb], writes=[Q0])
                    P.op("act", "copy", out=WlT[0:C, :], in_=Q0[0:C, 0:256], reads=[Q0], writes=[WlT])
                    for h in range(4):
                        P.op("pe", "matmul", out=Q2[0:64, h * 128:h * 128 + C], lhsT=At[0:C, h * 64:(h + 1) * 64], rhs=Tm[0:C, h, 0:C], start=True, stop=True, reads=[At, Tm], writes=[Q2])
                    P.op("act", "copy", out=AFp[:, :, 0:C], in_=v3(Q2[0:64, :], 4)[:, :, 0:C], reads=[Q2], writes=[AFp])
                    ecol = 0 if C == 128 else 1
                    for h in range(4):
                        P.op("pe", "matmul", out=Q1[0:64, 256 + h * 2:256 + h * 2 + 2], lhsT=f_ep[0:C, h * 64:(h + 1) * 64], rhs=elast[0:C, 0:2], start=True, stop=True,
                             reads=[f_ep, elast], writes=[Q1])
                    yield
                    for h in range(4):
                        hc = slice(h * 64, (h + 1) * 64)
                        P.op("pe", "matmul", out=Q1[0:C, hc], lhsT=Tm[0:C, h, 0:C], rhs=WlT[0:C, hc], start=True, stop=True, reads=[Tm, WlT], writes=[Q1])
                    P.op("dve", "tensor_copy", out=gC[:, :], in_=Q1[0:64, 256:264].rearrange("p (h a) -> p h a", a=2)[:, :, ecol], reads=[Q1], writes=[gC])
                    P.op("act", "copy", out=UlT[0:C, :], in_=Q1[0:C, 0:256], reads=[Q1], writes=[UlT])
                    yield
                    if first:
                        if state_src is None:
                            P.op("dve", "memset", ap=ST[:], constant=0.0, writes=[ST])
                        else:
                            P.dma("sp", sto[:], state_src[hg * 4:(hg + 1) * 4].rearrange("h i j -> i h j"), writes=[sto])
                            for h in range(4):
                                P.op("pe", "transpose", out=Q0[0:64, h * 64:(h + 1) * 64], in_=sto[:, h, :], identity=idf[0:64, 0:64], reads=[sto, idf], writes=[Q0])
                            P.op("act", "copy", out=ST[:], in_=v3(Q0[0:64, 0:256], 4), reads=[Q0], writes=[ST])
                        P.op("act", "copy", out=Sb[:], in_=ST[:], reads=[ST], writes=[Sb])
                    for h in range(4):
                        P.op("pe", "matmul", out=Q0[0:C, h * 64:(h + 1) * 64], lhsT=AFp[:, h, 0:C], rhs=Sb[:, h, :], start=True, stop=True, reads=[AFp, Sb], writes=[Q0])
                    P.op("dve", "tensor_tensor", out=UT[0:C, :], in0=Q0[0:C, 0:256], in1=UlT[0:C, :], op=ALU.add, reads=[Q0, UlT], writes=[UT])
                    for h in range(4):
                        hc = slice(h * 64, (h + 1) * 64)
                        P.op("pe", "matmul", out=Q1[0:C, hc], lhsT=ARF[:, h, 1, 0:C], rhs=Sb[:, h, :], start=True, stop=False, reads=[ARF, Sb], writes=[Q1])
                        P.op("pe", "matmul", out=Q1[0:C, hc], lhsT=Mb[0:C, h, 1, 0:C], rhs=UT[0:C, hc], start=False, stop=False, reads=[Mb, UT], writes=[Q1])
                        P.op("pe", "matmul", out=Q1[0:C, hc], lhsT=Mk[0:C, h, 1, 0:C], rhs=Vb[0:C, hc], start=False, stop=True, reads=[Mk, Vb], writes=[Q1])
                    for h in range(4):
                        hc = slice(h * 64, (h + 1) * 64)
                        P.op("pe", "matmul", out=Q2[0:64, hc], lhsT=Bt[0:C, hc], rhs=UT[0:C, hc], start=True, stop=False, reads=[Bt, UT], writes=[Q2])
                        P.op("pe", "matmul", out=Q2[0:64, hc], lhsT=Kt[0:C, hc], rhs=Vb[0:C, hc], start=False, stop=True, reads=[Kt, Vb], writes=[Q2])
                    P.op("dve", "tensor_tensor", out=ST[:], in0=v3(Q2[0:64, 0:256], 4), in1=ST[:], op=ALU.add, reads=[Q2, ST], writes=[ST])
                    P.op("dve", "tensor_tensor", out=ST[:], in0=ST[:], in1=gC[:, :].unsqueeze(2).to_broadcast([64, 4, 64]), op=ALU.mult, reads=[ST, gC], writes=[ST])
                    P.op("act", "copy", out=Sb[:], in_=ST[:], reads=[ST], writes=[Sb])
                    P.op("act", "copy", out=yv[0:C, :], in_=Q1[0:C, 0:256], reads=[Q1], writes=[yv])
                    yield
                    if out_state is not None:
                        for h in range(4):
                            P.op("pe", "transpose", out=Q0[0:64, h * 64:(h + 1) * 64], in_=ST[:, h, :], identity=idf[0:64, 0:64], reads=[ST, idf], writes=[Q0])
                        P.op("act", "copy", out=sto[:], in_=v3(Q0[0:64, 0:256], 4), reads=[Q0], writes=[sto])
                        P.dma("sp", out_state[hg * 4:(hg + 1) * 4].rearrange("h i j -> i h j"), sto[:], reads=[sto])
                    y3 = v3(yv[0:C, :], 4)
                    P.op("dve", "tensor_reduce", out=sm[0:C, 8:12], in_=y3, axis=AX.X, op=ALU.add, reads=[yv], writes=[sm])
                    P.op("dve", "tensor_tensor", out=f_t[0:C, :], in0=yv[0:C, :], in1=yv[0:C, :], op=ALU.mult, reads=[yv], writes=[f_t])
                    P.op("dve", "tensor_reduce", out=sm[0:C, 12:16], in_=v3(f_t[0:C, :], 4), axis=AX.X, op=ALU.add, reads=[f_t], writes=[sm])
                    P.op("dve", "tensor_scalar", out=sm[0:C, 8:12], in0=sm[0:C, 8:12], scalar1=1.0 / 64, scalar2=None, op0=ALU.mult, reads=[sm], writes=[sm])
                    P.op("dve", "tensor_tensor", out=sm[0:C, 16:20], in0=sm[0:C, 8:12], in1=sm[0:C, 8:12], op=ALU.mult, reads=[sm], writes=[sm])
                    P.op("dve", "scalar_tensor_tensor", out=sm[0:C, 12:16], in0=sm[0:C, 12:16], scalar=1.0 / 64, in1=sm[0:C, 16:20], op0=ALU.mult, op1=ALU.subtract,
                         reads=[sm], writes=[sm])
                    P.op("dve", "tensor_scalar", out=sm[0:C, 12:16], in0=sm[0:C, 12:16], scalar1=64e-5, scalar2=None, op0=ALU.add, reads=[sm], writes=[sm])
                    P.op("act", "activation", out=sm[0:C, 12:16], in_=sm[0:C, 12:16], func=AF.Ln, reads=[sm], writes=[sm])
                    P.op("act", "activation", out=sm[0:C, 12:16], in_=sm[0:C, 12:16], func=AF.Exp, scale=-0.5, reads=[sm], writes=[sm])
                    yield
                    bc4 = lambda a: a.unsqueeze(2).to_broadcast([C, 4, 64])
                    P.op("dve", "tensor_tensor", out=y3, in0=y3, in1=bc4(sm[0:C, 8:12]), op=ALU.subtract, reads=[yv, sm], writes=[yv])
                    P.op("dve", "tensor_tensor", out=y3, in0=y3, in1=bc4(sm[0:C, 12:16]), op=ALU.mult, reads=[yv, sm], writes=[yv])
                    P.op("pool", "tensor_tensor", out=v3(f_t[0:C, :], 4), in0=v3(v_, 4), in1=bc4(sm[0:C, 4:8]), op=ALU.mult, reads=[mm, sm], writes=[f_t])
                    P.op("dve", "tensor_tensor", out=yv[0:C, :], in0=yv[0:C, :], in1=pb("gn_w"), op=ALU.mult, reads=[yv, pbc["gn_w"]], writes=[yv])
                    P.op("pool", "tensor_tensor", out=f_t[0:C, :], in0=f_t[0:C, :], in1=pb("gn_b"), op=ALU.add, reads=[f_t, pbc["gn_b"]], writes=[f_t])
                    P.op("dve", "tensor_tensor", out=yv[0:C, :], in0=yv[0:C, :], in1=f_t[0:C, :], op=ALU.add, reads=[yv, f_t], writes=[yv])
                    P.op("dve", "tensor_tensor", out=yb16[0:C, :], in0=yv[0:C, :], in1=f_g[0:C, :], op=ALU.mult, reads=[yv, f_g], writes=[yb16])
                    if "yB" in DBG:
                        P.op("dve", "tensor_tensor", out=yv[0:C, :], in0=yv[0:C, :], in1=f_g[0:C, :], op=ALU.mult, reads=[yv, f_g], writes=[yv])
                        P.dma("sp", DBG["yB"][col0:col0 + C, cs], yv[0:C, :], reads=[yv])
                    for c in range(2):
                        P.op("pe", "transpose", out=Q2b[:, c * 128:c * 128 + C], in_=yb16[0:C, c * 128:(c + 1) * 128], identity=idb[0:C, 0:C], reads=[yb16, idb], writes=[Q2])
                    P.op("act", "copy", out=yBT[:, 2 * hg:2 * hg + 2, col0:col0 + C], in_=v3(Q2b[:, 0:256], 2)[:, :, 0:C], reads=[Q2], writes=[yBT])

                seq = []
                for i in range(NPT):
                    seq.append(dict(ti=i, C=128, col0=i * 128, first=(i == 0), state_src=None, out_state=(O["wkv_p"] if i == NPT - 1 else None),
                                    out_shift=(O["shift_p"][0:1, :] if i == NPT - 1 else None), prev=("tile" if i > 0 else None)))
                for b in range(SB):
                    seq.append(dict(ti=16, C=32, col0=2048 + 32 * b, first=True, state_src=I["swkv"][b], out_state=O["wkv_s"][b],
                                    out_shift=O["shift_s"][b:b + 1, :], prev=("dram", b)))

                def do_shared_b(k):
                    d = seq[k]
                    shared_b(d["ti"], d["C"], d["col0"], k % 2)

                def do_shared(k):
                    d = seq[k]
                    if d["prev"] is None:
                        prev = None
                    elif d["prev"] == "tile":
                        prev = (uf[(k - 1) % 2][127:128, :], [uf[(k - 1) % 2]])
                    else:
                        prev = (I["sshift"][d["prev"][1]:d["prev"][1] + 1, :], [])
                    shared(d["ti"], d["C"], d["col0"], k % 2, prev, d["out_shift"])

                seq = seq[:int(os.environ.get("KSEQ", "99"))]
                do_shared(0)
                do_shared_b(0)
                RA = int(os.environ.get("KRA", "4"))
                RB = int(os.environ.get("KRB", "4"))
                for k in range(len(seq)):
                    d = seq[k]
                    gens = [lane(d["ti"], d["C"], d["col0"], k % 2, hg, d["first"], d["state_src"], d["out_state"]) for hg in range(2)]
                    rounds = 0
                    active = list(gens)
                    while active:
                        if rounds >= int(os.environ.get("KLST", "999")):
                            break
                        for g in list(active):
                            try:
                                next(g)
                            except StopIteration:
                                active.remove(g)
                        rounds += 1
                        if rounds == RA and k + 1 < len(seq):
                            do_shared(k + 1)
                        if rounds == RB and k + 1 < len(seq):
                            do_shared_b(k + 1)
                    if k + 1 < len(seq):
                        if rounds < RA:
                            do_shared(k + 1)
                        if rounds < RB:
                            do_shared_b(k + 1)
                P.barrier()
            ewu.close()
            if stop_after == "B":
                P.finish()
                return nc

            oAT = sb(e1, "oAT", [128, 4, TOK], BF16)
            with ExitStack() as ec:
                LP = 4224
                wq = sb(ec, "wq", [128, 8, 512], BF16)
                wiq = sb(ec, "wiq", [128, 8, 1024], BF16)
                o_q = IN_OFF["q"][0]
                o_iq = IN_OFF["iq"][0]
                load_w("pool", wiq, wiq[:, :, 0:512], wsl("iq", o_iq, o_iq + 512))
                load_w("pool", wiq, wiq[:, :, 512:1024], wsl("iq", o_iq + 512, o_iq + 1024))
                load_w("pool", wq, wq[:], wsl("q", o_q, o_q + 512))
                bsel = sb(ec, "bsel", [128, 4], F32)
                wtmp = sb(ec, "wtmp", [128, 16], F32)
                P.dma("sp", bsel[:], I["bsel"], writes=[bsel])
                qT4 = [sb(ec, f"qT4{i}", [128, 4, 512], BF16) for i in range(2)]
                iqT4 = sb(ec, "iqT4", [128, 8, 512], BF16)
                Dw = sb(ec, "Dw", [128, 16, 128], BF16)
                score = sb(ec, "score", [128, LP], F32)
                cx = {"qT": (qT4[0], 0), "iqT": (iqT4, 0), "Dw": Dw, "score": score}
                work = sb(ec, "work", [128, 512], F32)
                bz = sb(ec, "bz", [128, 4], F32)
                bzi = sb(ec, "bzi", [128, 2], mybir.dt.uint32)
                NBIS = int(os.environ.get("KNBIS", "18"))
                pw2 = sb(ec, "pw2", [128, 32], F32)
                hw = sb(ec, "hw", [128, 32], F32)
                P.dma("sp", pw2[:], I["pow2"], writes=[pw2])

                def score_bounds(L):
                    score = cx["score"]
                    P.op("dve", "tensor_reduce", out=bz[:, 0:1], in_=score[:, 0:L], axis=AX.X, op=ALU.min, reads=[score], writes=[bz])
                    P.op("dve", "tensor_reduce", out=bz[:, 1:2], in_=score[:, 0:L], axis=AX.X, op=ALU.max, reads=[score], writes=[bz])
                mask = sb(ec, "mask", [128, LP], BF16)
                maskT = sb(ec, "maskT", [128, 33, 128], BF16)
                rb = [sb(ec, f"rb{i}", [128, 512], BF16) for i in range(4)]
                Eb = [sb(ec, f"Eb{i}", [128, 8, 128], BF16) for i in range(3)]
                PTb = [sb(ec, f"PTb{i}", [128, 8, 128], BF16) for i in range(3)]
                rden = sb(ec, "rden", [128, 8], F32)
                o16 = sb(ec, "o16", [128, 512], BF16)
                X2 = [ps(ec, f"X2{i}", [128, 1024], F32) for i in range(2)]
                Y1 = [ps(ec, f"Y1{i}", [128, 512], F32) for i in range(4)]
                Y1b16 = [Y1[i].t[:].bitcast(BF16) for i in range(4)]

                def project_q(ti, col0, Pq):
                    qT, qo = cx["qT"]
                    for h in range(8):
                        for c in range(8):
                            P.op("pe", "matmul", out=Y1[0][64 * (h // 4):64 * (h // 4) + 64, (h % 4) * 128:(h % 4) * 128 + Pq], lhsT=wq[:, c, h * 64:(h + 1) * 64],
                                 rhs=xnT[:, c, col0:col0 + Pq], start=(c == 0), stop=(c == 7), reads=[wq, xnT_t[ti]], writes=[Y1[0]])
                    P.op("act", "copy", out=qT[:, :, qo:qo + Pq], in_=Y1[0][:, :].rearrange("p (h t) -> p h t", h=4)[:, :, 0:Pq], reads=[Y1[0]], writes=[qT])

                def project_iq(ti, col0, Pq):
                    iqT, io = cx["iqT"]
                    for h in range(16):
                        for c in range(8):
                            P.op("pe", "matmul", out=X2[0][64 * (h % 2):64 * (h % 2) + 64, (h // 2) * 128:(h // 2) * 128 + Pq], lhsT=wiq[:, c, h * 64:(h + 1) * 64],
                                 rhs=xnT[:, c, col0:col0 + Pq], start=(c == 0), stop=(c == 7), reads=[wiq, xnT_t[ti]], writes=[X2[0]])
                    P.op("act", "copy", out=iqT[:, :, io:io + Pq], in_=X2[0][:, :].rearrange("p (h t) -> p h t", h=8)[:, :, 0:Pq], reads=[X2[0]], writes=[iqT])

                def project_group(g, qbuf):
                    c0 = g * 512
                    rg = [xnT_t[4 * g + j] for j in range(4)]
                    slots = [(X2[0], 0), (X2[0], 512), (X2[1], 0), (X2[1], 512), (Y1[0], 0), (Y1[1], 0), (Y1[2], 0), (Y1[3], 0)]
                    for j in range(8):
                        t, off = slots[j]
                        for hh in range(2):
                            h = 2 * j + hh
                            for c in range(8):
                                P.op("pe", "matmul", out=t[64 * hh:64 * hh + 64, off:off + 512], lhsT=wiq[:, c, h * 64:(h + 1) * 64], rhs=xnT[:, c, c0:c0 + 512],
                                     start=(c == 0), stop=(c == 7), reads=[wiq] + rg, writes=[t])
                        P.op("act", "copy", out=iqT4[:, j, :], in_=t[:, off:off + 512], reads=[t], writes=[iqT4])
                    for j in range(4):
                        t = Y1[j]
                        for hh in range(2):
                            h = 4 * hh + j
                            for c in range(8):
                                P.op("pe", "matmul", out=t[64 * hh:64 * hh + 64, 0:512], lhsT=wq[:, c, h * 64:(h + 1) * 64], rhs=xnT[:, c, c0:c0 + 512],
                                     start=(c == 0), stop=(c == 7), reads=[wq] + rg, writes=[t])
                        P.op("act", "copy", out=qbuf[:, j, :], in_=t[:, 0:512], reads=[t], writes=[qbuf])

                def make_dw(ti, bcol):
                    Dw = cx["Dw"]
                    if bcol is None:
                        wsrc = iws[:, ti, :]
                        rd = [idb, iws]
                    else:
                        P.op("dve", "tensor_scalar", out=wtmp[:, :], in0=iws[:, ti, :], scalar1=bsel[:, bcol:bcol + 1], scalar2=None, op0=ALU.mult, reads=[iws, bsel], writes=[wtmp])
                        wsrc = wtmp[:, :]
                        rd = [idb, wtmp]
                    P.op("dve", "tensor_tensor", out=Dw[:, :, :], in0=idb[:, :].unsqueeze(1).to_broadcast([128, 16, 128]), in1=wsrc.unsqueeze(2).to_broadcast([128, 16, 128]), op=ALU.mult,
                         reads=rd, writes=[Dw])

                rbi = [0]

                def indexer(ikT_src, ik_off, L, accumulate):
                    nkb = (L + 511) // 512
                    pls = [Y1[0], Y1[1], Y1[2], X2[1]]
                    steps = [(kb, h) for kb in range(nkb) for h in range(16)]
                    iqT, io = cx["iqT"]
                    Dw = cx["Dw"]
                    score = cx["score"]

                    def mm1(si):
                        kb, h = steps[si]
                        n = min(512, L - kb * 512)
                        pl = pls[si % 4]
                        hp = slice(64 * (h % 2), 64 * (h % 2) + 64)
                        P.op("pe", "matmul", out=pl[:, 0:n], lhsT=iqT[hp, h // 2, io:io + 128], rhs=ikT_src[hp, ik_off + kb * 512:ik_off + kb * 512 + n], start=True, stop=True,
                             reads=[iqT, ikT_src], writes=[pl])
                        r = rb[si % 4]
                        if si % 2 == 0 or cx.get("act_only"):
                            P.op("act", "activation", out=r[:, 0:n], in_=pl[:, 0:n], func=AF.Relu, reads=[pl], writes=[r])
                        else:
                            P.op("dve", "tensor_scalar", out=r[:, 0:n], in0=pl[:, 0:n], scalar1=0.0, scalar2=None, op0=ALU.max, reads=[pl], writes=[r])

                    def mm2(si):
                        kb, h = steps[si]
                        n = min(512, L - kb * 512)
                        r = rb[si % 4]
                        P.op("pe", "matmul", out=Y1[3][:, 0:n], lhsT=Dw[:, h, :], rhs=r[:, 0:n], start=(h == 0), stop=(h == 15), reads=[Dw, r], writes=[Y1[3]])
                        if h == 15:
                            cs = slice(kb * 512, kb * 512 + n)
                            if accumulate:
                                P.op("dve", "tensor_tensor", out=score[:, cs], in0=Y1[3][:, 0:n], in1=score[:, cs], op=ALU.add, reads=[Y1[3], score], writes=[score])
                            else:
                                P.op("act", "copy", out=score[:, cs], in_=Y1[3][:, 0:n], reads=[Y1[3]], writes=[score])

                    LA = int(os.environ.get("KLA", "3"))
                    for si in range(len(steps) + LA):
                        if si < len(steps):
                            mm1(si)
                        if si >= LA:
                            mm2(si - LA)

                def topk_mask(L, do_topk):
                    score = cx["score"]
                    if do_topk:
                        P.op("dve", "tensor_tensor", out=bz[:, 1:2], in0=bz[:, 1:2], in1=bz[:, 0:1], op=ALU.subtract, reads=[bz], writes=[bz])
                        P.op("dve", "tensor_scalar", out=hw[:, :], in0=pw2[:, :], scalar1=bz[:, 1:2], scalar2=None, op0=ALU.mult, reads=[pw2, bz], writes=[hw])
                        P.op("dve", "tensor_tensor", out=bz[:, 2:3], in0=bz[:, 0:1], in1=hw[:, 0:1], op=ALU.add, reads=[bz, hw], writes=[bz])
                        for st in range(NBIS):
                            P.op("dve", "tensor_scalar", out=mask[:, 0:L], in0=score[:, 0:L], scalar1=bz[:, 2:3], scalar2=None, op0=ALU.is_ge, op1=ALU.add, accum_out=bz[:, 3:4],
                                 reads=[score, bz], writes=[mask, bz])
                            P.op("dve", "tensor_scalar", out=bz[:, 3:4], in0=bz[:, 3:4], scalar1=255.5, scalar2=hw[:, st:st + 1], op0=ALU.is_ge, op1=ALU.mult, reads=[bz, hw], writes=[bz])
                            P.op("dve", "scalar_tensor_tensor", out=bz[:, 2:3], in0=bz[:, 3:4], scalar=hw[:, st + 1:st + 2], in1=bz[:, 2:3], op0=ALU.subtract, op1=ALU.add,
                                 reads=[bz, hw], writes=[bz])
                        P.op("dve", "tensor_tensor", out=bz[:, 0:1], in0=bz[:, 2:3], in1=hw[:, NBIS:NBIS + 1], op=ALU.subtract, reads=[bz, hw], writes=[bz])
                        P.op("dve", "tensor_scalar", out=mask[:, 0:L], in0=score[:, 0:L], scalar1=bz[:, 0:1], scalar2=None, op0=ALU.is_ge, reads=[score, bz], writes=[mask])
                    else:
                        P.op("dve", "tensor_scalar", out=mask[:, 0:L], in0=score[:, 0:L], scalar1=-1.0e29, scalar2=None, op0=ALU.is_ge, reads=[score], writes=[mask])
                    nblk = (L + 127) // 128
                    for b0 in range(0, nblk, 8):
                        nb = min(8, nblk - b0)
                        for j in range(nb):
                            blk = b0 + j
                            ks = min(128, L - blk * 128)
                            P.op("pe", "transpose", out=Y1b16[0][0:ks, j * 128:(j + 1) * 128], in_=mask[:, blk * 128:blk * 128 + ks], identity=idb[:, :], reads=[mask, idb], writes=[Y1[0]])
                        P.op("act", "copy", out=maskT[:, b0:b0 + nb, :], in_=Y1b16[0][:, 0:nb * 128].rearrange("p (b t) -> p b t", b=nb), reads=[Y1[0]], writes=[maskT])

                ebi = [0]

                def attend(Pq, tq0, kT_src, va_src, L, ocol0):
                    nblk = (L + 127) // 128
                    base = ebi[0]
                    ebi[0] += nblk
                    qT, qo = cx["qT"]
                    tq0 = tq0 + qo
                    mq0 = tq0 - qo

                    def ksz(blk):
                        return min(128, L - blk * 128)

                    def pS_of(blk, kv):
                        bf = (base + blk) % 3
                        if bf < 2:
                            return X2[bf], X2[bf][0:ksz(blk), kv * 512:kv * 512 + 4 * Pq]
                        t = (Y1[0], Y1[3])[kv]
                        return t, t[0:ksz(blk), 0:4 * Pq]

                    def S(blk):
                        ks = ksz(blk)
                        bf = (base + blk) % 3
                        E = Eb[bf]
                        PT = PTb[bf]
                        for kv in range(2):
                            t, ap = pS_of(blk, kv)
                            P.op("pe", "matmul", out=ap.rearrange("p (g t) -> p g t", g=4), lhsT=kT_src[64 * kv:64 * kv + 64, blk * 128:blk * 128 + ks],
                                 rhs=qT[64 * kv:64 * kv + 64, :, tq0:tq0 + Pq], start=True, stop=True, reads=[kT_src, qT], writes=[t])
                        for kv in range(2):
                            t, ap = pS_of(blk, kv)
                            P.op("act", "activation", out=E[0:ks, 4 * kv:4 * kv + 4, 0:Pq], in_=ap.rearrange("p (g t) -> p g t", g=4), func=AF.Exp, scale=0.125, reads=[t], writes=[E])
                        eng = "pool" if blk % 3 == 0 else "dve"
                        P.op(eng, "tensor_tensor", out=PT[0:ks, :, 0:Pq], in0=E[0:ks, :, 0:Pq], in1=maskT[0:ks, blk, mq0:mq0 + Pq].unsqueeze(1).to_broadcast([ks, 8, Pq]), op=ALU.mult,
                             reads=[E, maskT], writes=[PT])

                    def PV(blk):
                        ks = ksz(blk)
                        PT = PTb[(base + blk) % 3]
                        for h in range(8):
                            kv, g = h // 4, h % 4
                            po = Y1[1 + kv]
                            P.op("pe", "matmul", out=po[0:Pq, 65 * g:65 * g + 65], lhsT=PT[0:ks, h, 0:Pq], rhs=va_src[0:ks, blk, kv, :], start=(blk == 0 and g == 0),
                                 stop=(blk == nblk - 1 and g == 3), skip_group_check=True, reads=[PT, va_src], writes=[po])

                    LA = 2
                    for bi in range(nblk + LA):
                        if bi < nblk:
                            S(bi)
                        if bi >= LA:
                            PV(bi - LA)
                    for kv in range(2):
                        po = Y1[1 + kv]
                        pov = po[0:Pq, 0:260].rearrange("p (g d) -> p g d", g=4)
                        P.op("dve", "reciprocal", out=rden[0:Pq, 4 * kv:4 * kv + 4], in_=pov[:, :, 64], reads=[po], writes=[rden])
                        P.op("dve", "tensor_tensor", out=o16[0:Pq, kv * 256:(kv + 1) * 256].rearrange("p (g d) -> p g d", g=4), in0=pov[:, :, 0:64],
                             in1=rden[0:Pq, 4 * kv:4 * kv + 4].unsqueeze(2).to_broadcast([Pq, 4, 64]), op=ALU.mult, reads=[po, rden], writes=[o16])
                    if "oA" in DBG:
                        P.op("act", "copy", out=work[0:Pq, 0:512], in_=o16[0:Pq, :], reads=[o16], writes=[work])
                        P.dma("sp", DBG["oA"][ocol0:ocol0 + Pq, :], work[0:Pq, 0:512], reads=[work])
                    for c in range(4):
                        P.op("pe", "transpose", out=Y1b16[3][:, c * 128:c * 128 + Pq], in_=o16[0:Pq, c * 128:(c + 1) * 128], identity=idb[0:Pq, 0:Pq], reads=[o16, idb], writes=[Y1[3]])
                    P.op("act", "copy", out=oAT[:, :, ocol0:ocol0 + Pq], in_=Y1b16[3][:, 0:512].rearrange("p (c t) -> p c t", c=4)[:, :, 0:Pq], reads=[Y1[3]], writes=[oAT])

                n_pt = int(os.environ.get("KNPT", NPT))
                with ExitStack() as ecp:
                    score2 = sb(ecp, "score2", [128, 2048], F32)
                    Dw2 = sb(ecp, "Dw2", [128, 16, 128], BF16)
                    scoreb = [score, score2]
                    Dwb = [Dw, Dw2]

                    def front(i):
                        g = i // 4
                        if i % 4 == 0:
                            project_group(g, qT4[g % 2])
                        cx.update({"qT": (qT4[g % 2], (i % 4) * 128), "iqT": (iqT4, (i % 4) * 128), "Dw": Dwb[i % 2], "score": scoreb[i % 2], "act_only": True})
                        make_dw(i, None)
                        indexer(ikT, 0, 128 * (i + 1), False)

                    def back(i):
                        g = i // 4
                        L = 128 * (i + 1)
                        cx.update({"qT": (qT4[g % 2], (i % 4) * 128), "score": scoreb[i % 2]})
                        sc = scoreb[i % 2]
                        if L > 256:
                            score_bounds(L)
                        P.op("dve", "memset", ap=sc[0:64, L - 64:L], constant=NEG, writes=[sc])
                        if "sc" in DBG and i == DBG_TILE:
                            P.dma("sp", DBG["sc"][0:128, 0:L], sc[:, 0:L], reads=[sc])
                        topk_mask(L, L > 256)
                        attend(128, 0, kT, vaug, L, i * 128)

                    front(0)
                    for i in range(n_pt):
                        if i + 1 < n_pt:
                            front(i + 1)
                        back(i)
                    P.barrier()
                cx.update({"qT": (qT4[0], 0), "iqT": (iqT4, 0), "Dw": Dw, "score": score, "act_only": False})

                if "s" not in SKIP:
                    ikTb = sb(ec, "ikTb", [128, LP], BF16)
                    kTb = sb(ec, "kTb", [128, LP], BF16)
                    vab = sb(ec, "vab", [128, 33, 2, 65], BF16)
                    stg = [sb(ec, "stg0", [128, 8, 128], BF16)] * 2
                    P.op("pool", "memset", ap=vab[:, :, :, 64:65], constant=1.0, writes=[vab])
                    project_q(16, 2048, 128)
                    project_iq(16, 2048, 128)
                    sgi = 0
                    sTm = wiq.t[:].rearrange("p c n -> p (c n)").bitcast(F32).rearrange("p (b t) -> p b t", t=128)
                    sTl = wq.t[:].rearrange("p c n -> p (c n)").bitcast(F32)
                    tAB = qT4[1].t[:].rearrange("p c n -> p (c n)").bitcast(F32)
                    wbc_ap = Dw.t[:].rearrange("p h t -> p (h t)").bitcast(F32)[:, 0:512].rearrange("p (h t) -> p h t", h=16)
                    P.op("dve", "memset", ap=work[:, 0:128], constant=1.0, writes=[work])

                    def sample_indexer_T(b):
                        P.op("dve", "tensor_tensor", out=tAB[:, 0:512].rearrange("p (h t) -> p h t", h=16), in0=idf[:, 32 * b:32 * b + 32].unsqueeze(1).to_broadcast([128, 16, 32]),
                             in1=iws[:, 16, :].unsqueeze(2).to_broadcast([128, 16, 32]), op=ALU.mult, reads=[idf, iws], writes=[qT4[1]])
                        P.op("pe", "matmul", out=Y1[3][:, 0:512], lhsT=work[:, 0:128], rhs=tAB[:, 0:512], start=True, stop=True, reads=[work, qT4[1]], writes=[Y1[3]])
                        P.op("act", "copy", out=wbc_ap, in_=Y1[3][:, 0:512].rearrange("p (h t) -> p h t", h=16), reads=[Y1[3]], writes=[Dw])
                        wv = wbc_ap.rearrange("p (j two) t -> p j two t", two=2)
                        npair = 17
                        banks = [(Y1[0], Y1[1]), (Y1[2], X2[1])]
                        for kp in range(npair):
                            nq = 2 if kp < 16 else 1
                            ks = 128 if kp < 16 else 32
                            bA, bB = banks[kp % 2]
                            rA = rb[(2 * kp) % 4]
                            rB = rb[(2 * kp + 1) % 4]
                            for par, bank in ((0, bA), (1, bB)):
                                for q in range(nq):
                                    blk = 2 * kp + q
                                    P.op("pe", "matmul", out=bank[0:ks, q * 256:(q + 1) * 256].rearrange("p (j t) -> p j t", j=8), lhsT=ikTb[64 * par:64 * par + 64, blk * 128:blk * 128 + ks],
                                         rhs=iqT4[64 * par:64 * par + 64, :, 32 * b:32 * b + 32], start=True, stop=True, reads=[ikTb, iqT4], writes=[bank])
                            n = nq * 256
                            P.op("act", "activation", out=rA[0:ks, 0:n], in_=bA[0:ks, 0:n], func=AF.Relu, reads=[bA], writes=[rA])
                            P.op("act", "activation", out=rB[0:ks, 0:n], in_=bB[0:ks, 0:n], func=AF.Relu, reads=[bB], writes=[rB])
                            v4 = lambda ap: ap.rearrange("p (q j t) -> p q j t", q=nq, j=8)
                            wA = wv[0:ks, :, 0, :].unsqueeze(1).to_broadcast([ks, nq, 8, 32])
                            wB = wv[0:ks, :, 1, :].unsqueeze(1).to_broadcast([ks, nq, 8, 32])
                            P.op("dve", "tensor_tensor", out=v4(tAB[0:ks, 0:n]), in0=v4(rA[0:ks, 0:n]), in1=wA, op=ALU.mult, reads=[rA, Dw], writes=[qT4[1]])
                            P.op("dve", "tensor_tensor", out=v4(tAB[0:ks, 512:512 + n]), in0=v4(rB[0:ks, 0:n]), in1=wB, op=ALU.mult, reads=[rB, Dw], writes=[qT4[1]])
                            P.op("dve", "tensor_tensor", out=tAB[0:ks, 0:n], in0=tAB[0:ks, 0:n], in1=tAB[0:ks, 512:512 + n], op=ALU.add, reads=[qT4[1]], writes=[qT4[1]])
                            src = tAB[0:ks, 0:n].rearrange("p (q j t) -> p q t j", q=nq, j=8)
                            if kp < 16:
                                P.op("dve", "tensor_reduce", out=sTm[:, 2 * kp:2 * kp + 2, 32 * b:32 * b + 32], in_=src, axis=AX.X, op=ALU.add, reads=[qT4[1]], writes=[wiq])
                            else:
                                P.op("dve", "tensor_reduce", out=sTl[0:32, 32 * b:32 * b + 32].unsqueeze(1), in_=src, axis=AX.X, op=ALU.add, reads=[qT4[1]], writes=[wq])

                    def sample_score_transpose():
                        for b0 in range(0, 33, 4):
                            nb = min(4, 33 - b0)
                            bank = Y1[(b0 // 4) % 2]
                            for j in range(nb):
                                blk = b0 + j
                                if blk < 32:
                                    P.op("pe", "transpose", out=bank[:, j * 128:(j + 1) * 128], in_=sTm[:, blk, :], identity=idf[:, :], reads=[wiq, idf], writes=[bank])
                                else:
                                    P.op("pe", "transpose", out=bank[:, j * 128:j * 128 + 32], in_=sTl[0:32, 0:128], identity=idf[0:32, 0:32], reads=[wq, idf], writes=[bank])
                            w_ = (nb - 1) * 128 + (128 if b0 + nb - 1 < 32 else 32)
                            P.op("act", "copy", out=score[:, b0 * 128:b0 * 128 + w_], in_=bank[:, 0:w_], reads=[bank], writes=[score])
                    for b in range(SB):
                        for pc in range(4):
                            sg = stg[sgi % 2]
                            sgi += 1
                            src = I["cik"][b, pc * 1024:(pc + 1) * 1024, :].rearrange("(blk s) d -> s blk d", s=128)
                            P.dma("pool", sg[:, :, 0:64], src, writes=[sg])
                            P.dma("pool", sg[:, :, 64:128], src, writes=[sg])
                            for j in range(8):
                                P.op("pe", "transpose", out=Y1b16[0][:, j * 128:(j + 1) * 128], in_=sg[:, j, :], identity=idb[:, :], reads=[sg, idb], writes=[Y1[0]])
                            P.op("act", "copy", out=ikTb[:, pc * 1024:(pc + 1) * 1024], in_=Y1b16[0][:, :], reads=[Y1[0]], writes=[ikTb])
                        P.op("act", "copy", out=ikTb[:, 4096:4128], in_=ikTs[:, 32 * b:32 * b + 32], reads=[ikTs], writes=[ikTb])
                        if "t" in SKIP:
                            make_dw(16, b)
                            indexer(ikTb, 0, LS, b > 0)
                        else:
                            sample_indexer_T(b)
                    if "t" not in SKIP:
                        sample_score_transpose()
                    if "sc" in DBG and DBG_TILE == 16:
                        P.dma("sp", DBG["sc"][0:128, 0:LS], score[:, 0:LS], reads=[score])
                    def build_kside(b):
                        nonlocal sgi
                        for pc in range(4):
                            sg = stg[sgi % 2]
                            sgi += 1
                            P.dma("pool", sg[:], I["ck"][b, pc * 1024:(pc + 1) * 1024, :].rearrange("(blk s) d -> s blk d", s=128), writes=[sg])
                            for j in range(8):
                                P.op("pe", "transpose", out=Y1b16[0][:, j * 128:(j + 1) * 128], in_=sg[:, j, :], identity=idb[:, :], reads=[sg, idb], writes=[Y1[0]])
                            P.op("act", "copy", out=kTb[:, pc * 1024:(pc + 1) * 1024], in_=Y1b16[0][:, :], reads=[Y1[0]], writes=[kTb])
                            for kv in range(2):
                                P.dma("pool", vab[:, pc * 8:(pc + 1) * 8, kv, 0:64], I["cv"][b, pc * 1024:(pc + 1) * 1024, kv * 64:(kv + 1) * 64].rearrange("(blk s) d -> s blk d", s=128), writes=[vab])
                        P.op("act", "copy", out=kTb[:, 4096:4128], in_=kTs[:, 32 * b:32 * b + 32], reads=[kTs], writes=[kTb])
                        P.op("pe", "transpose", out=Y1b16[0][0:32, 0:128], in_=vTs[:, 32 * b:32 * b + 32], identity=idb[:, :], reads=[vTs, idb], writes=[Y1[0]])
                        P.op("act", "copy", out=vab[0:32, 32, :, 0:64], in_=Y1b16[0][0:32, 0:128].rearrange("p (k d) -> p k d", k=2), reads=[Y1[0]], writes=[vab])

                    build_kside(0)
                    score_bounds(LS)
                    topk_mask(LS, True)
                    for b in range(SB):
                        if b > 0:
                            build_kside(b)
                        attend(32, 32 * b, kTb, vab, LS, 2048 + 32 * b)
                P.barrier()
            if stop_after == "C":
                P.finish()
                return nc

            Oy = [T(None, f"oy{i}") for i in range(NT)]
            with ExitStack() as ed:
                wga = sb(ed, "wga", [128, 8, 1024], BF16)
                wgb = sb(ed, "wgb", [128, 8, 1024], BF16)
                woa = sb(ed, "woa", [128, 4, 1024], BF16)
                wob = sb(ed, "wob", [128, 4, 1024], BF16)
                wout = sb(ed, "wout", [128, 8, 1024], BF16)
                o_ga = IN_OFF["ga"][0]
                o_gb = IN_OFF["gb"][0]
                for hf in range(2):
                    cs = slice(hf * 512, (hf + 1) * 512)
                    load_w("pool", wga, wga[:, :, cs], wsl("ga", o_ga + hf * 512, o_ga + (hf + 1) * 512))
                    load_w("pool", wgb, wgb[:, :, cs], wsl("gb", o_gb + hf * 512, o_gb + (hf + 1) * 512))
                    load_w("pool", woa, woa[:, :, cs], I["w_oa"][:, cs].rearrange("(c p) n -> p c n", p=128))
                    load_w("pool", wob, wob[:, :, cs], I["w_ob"][:, cs].rearrange("(c p) n -> p c n", p=128))
                for hf in range(2):
                    cs = slice(hf * 512, (hf + 1) * 512)
                    load_w("pool", wout, wout[:, :, cs], I["w_out"][:, cs].rearrange("(c p) n -> p c n", p=128))
                n2T = sb(ed, "n2T", [128, 8], F32)
                P.dma("sp", n2T[:], I["norm2"].rearrange("(c p) -> p c", p=128), writes=[n2T], allow_slow_non_contiguous=True)
                mTg = sb(ed, "mTg", [128, 8, 512], BF16)
                sga = sb(ed, "sga", [128, 512], F32)
                sgb = sb(ed, "sgb", [128, 512], F32)
                t1 = sb(ed, "t1", [128, 512], F32)
                t2 = sb(ed, "t2", [128, 512], F32)
                xt2 = [sb(ed, f"xt2{i}", [128, D], F32) for i in range(2)]
                ht = [sb(ed, f"ht{i}", [128, D], F32) for i in range(2)]
                hs = sb(ed, "hs", [128, D], BF16)
                jk = sb(ed, "jk", [128, D], BF16)
                ss2 = sb(ed, "ss2", [128, 2], F32)
                pG = [ps(ed, f"pG{i}", [128, 512], F32) for i in range(4)]
                pH = [ps(ed, f"pH{i}", [128, 512], F32) for i in range(2)]
                pT2 = ps(ed, "pT2", [128, 8, 128], BF16)
                groups = [(0, 4), (4, 4), (8, 4), (12, 4), (16, 1)]
                for (t0, nt) in groups:
                    c0 = t0 * 128
                    n = nt * 128
                    rg = [xnT_t[t0 + j] for j in range(nt)]
                    for fc in range(8):
                        fs = slice(fc * 128, (fc + 1) * 128)
                        for c in range(8):
                            P.op("pe", "matmul", out=pG[0][:, 0:n], lhsT=wga[:, c, fs], rhs=xnT[:, c, c0:c0 + n], start=(c == 0), stop=(c == 7), reads=[wga] + rg, writes=[pG[0]])
                        for c in range(8):
                            P.op("pe", "matmul", out=pG[1][:, 0:n], lhsT=wgb[:, c, fs], rhs=xnT[:, c, c0:c0 + n], start=(c == 0), stop=(c == 7), reads=[wgb] + rg, writes=[pG[1]])
                        for c in range(4):
                            P.op("pe", "matmul", out=pG[2][:, 0:n], lhsT=woa[:, c, fs], rhs=oAT[:, c, c0:c0 + n], start=(c == 0), stop=(c == 3), reads=[woa, oAT], writes=[pG[2]])
                        for c in range(4):
                            P.op("pe", "matmul", out=pG[3][:, 0:n], lhsT=wob[:, c, fs], rhs=yBT[:, c, c0:c0 + n], start=(c == 0), stop=(c == 3), reads=[wob, yBT], writes=[pG[3]])
                        P.op("act", "activation", out=sga[:, 0:n], in_=pG[0][:, 0:n], func=AF.Sigmoid, reads=[pG[0]], writes=[sga])
                        P.op("act", "activation", out=sgb[:, 0:n], in_=pG[1][:, 0:n], func=AF.Sigmoid, reads=[pG[1]], writes=[sgb])
                        P.op("dve", "tensor_tensor", out=t1[:, 0:n], in0=pG[2][:, 0:n], in1=sga[:, 0:n], op=ALU.mult, reads=[pG[2], sga], writes=[t1])
                        P.op("dve", "tensor_tensor", out=t2[:, 0:n], in0=pG[3][:, 0:n], in1=sgb[:, 0:n], op=ALU.mult, reads=[pG[3], sgb], writes=[t2])
                        P.op("pool", "tensor_tensor", out=mTg[:, fc, 0:n], in0=t1[:, 0:n], in1=t2[:, 0:n], op=ALU.add, reads=[t1, t2], writes=[mTg])
                    for j in range(nt):
                        ti = t0 + j
                        b = ti % 2
                        P.dma("sp", xt2[b][:], I["x"][ti * 128:(ti + 1) * 128, :], writes=[xt2[b]])
                        for hf in range(2):
                            for fc in range(8):
                                P.op("pe", "matmul", out=pH[hf][:, :], lhsT=mTg[:, fc, j * 128:(j + 1) * 128], rhs=wout[:, fc, hf * 512:(hf + 1) * 512], start=(fc == 0), stop=(fc == 7),
                                     reads=[mTg, wout], writes=[pH[hf]])
                            P.op("dve", "tensor_tensor", out=ht[b][:, hf * 512:(hf + 1) * 512], in0=pH[hf][:, :], in1=xt2[b][:, hf * 512:(hf + 1) * 512], op=ALU.add,
                                 reads=[pH[hf], xt2[b]], writes=[ht[b]])
                        P.dma("sp", O["y"][ti * 128:(ti + 1) * 128, :], ht[b][:], reads=[ht[b]], writes=[Oy[ti]])
                        if "h1" in DBG:
                            P.dma("sp", DBG["h1"][ti * 128:(ti + 1) * 128, :], ht[b][:], reads=[ht[b]])
                        P.op("act", "activation", out=jk[:], in_=ht[b][:], func=AF.Square, accum_out=ss2[:, 0:1], reads=[ht[b]], writes=[jk, ss2])
                        P.op("act", "activation", out=ss2[:, 1:2], in_=ss2[:, 0:1], func=AF.Sqrt, scale=1.0 / D, bias=1e-6, reads=[ss2], writes=[ss2])
                        P.op("dve", "reciprocal", out=ss2[:, 1:2], in_=ss2[:, 1:2], reads=[ss2], writes=[ss2])
                        P.op("dve", "tensor_scalar", out=hs[:], in0=ht[b][:], scalar1=ss2[:, 1:2], scalar2=None, op0=ALU.mult, reads=[ht[b], ss2], writes=[hs])
                        for c in range(8):
                            P.op("pe", "transpose", out=pT2[:, c, :], in_=hs[:, c * 128:(c + 1) * 128], identity=idb[:], reads=[hs, idb], writes=[pT2])
                        P.op("dve", "tensor_tensor", out=xnT[:, :, ti * 128:(ti + 1) * 128], in0=pT2[:], in1=n2T[:].unsqueeze(2).to_broadcast([128, 8, 128]), op=ALU.mult,
                             reads=[pT2, n2T], writes=[xnT_t[ti]])
                P.barrier()
            if stop_after == "D":
                P.finish()
                return nc
        with ExitStack() as ef:
            NFF = 22
            wg = sb(ef, "wg", [128, 8, 2816], BF16)
            wup = sb(ef, "wup", [128, 8, 2816], BF16)
            wd = sb(ef, "wd", [128, NFF, 1024], BF16)
            for q4 in range(4):
                cs = slice(q4 * 704, (q4 + 1) * 704)
                load_w("pool", wg, wg[:, :, cs], I["w_gate"][:, cs].rearrange("(c p) n -> p c n", p=128))
                load_w("pool", wup, wup[:, :, cs], I["w_up"][:, cs].rearrange("(c p) n -> p c n", p=128))
            for hf in range(2):
                cs = slice(hf * 512, (hf + 1) * 512)
                load_w("pool", wd, wd[:, :, cs], I["w_down"][:, cs].rearrange("(c p) n -> p c n", p=128))
            nfb = sb(ef, "nfb", [128, D], F32)
            P.dma("sp", nfb[:], I["norm_f"].partition_broadcast(128), writes=[nfb])
            actT = sb(ef, "actT", [128, NFF, 256], BF16)
            sg = [sb(ef, f"sg{i}", [128, 256], F32) for i in range(2)]
            hb = sb(ef, "hb", [128, D], F32)
            jk2 = sb(ef, "jk2", [128, D], BF16)
            ss3 = sb(ef, "ss3", [128, 2], F32)
            pF = [ps(ef, f"pF{i}", [128, 512], F32) for i in range(3)]
            pO = [ps(ef, f"pO{i}", [128, 512], F32) for i in range(2)]
            fgroups = [(t, 2) for t in range(0, 16, 2)] + [(16, 1)]
            k = 0
            for (t0, nt) in fgroups:
                c0 = t0 * 128
                n = nt * 128
                rg = [xnT_t[t0 + j] for j in range(nt)]
                for f in range(NFF):
                    fs = slice(f * 128, (f + 1) * 128)
                    pf = pF[k % 3]
                    s_ = sg[k % 2]
                    k += 1
                    for c in range(8):
                        P.op("pe", "matmul", out=pf[:, 0:n], lhsT=wg[:, c, fs], rhs=xnT[:, c, c0:c0 + n], start=(c == 0), stop=(c == 7), reads=[wg] + rg, writes=[pf])
                    for c in range(8):
                        P.op("pe", "matmul", out=pf[:, 256:256 + n], lhsT=wup[:, c, fs], rhs=xnT[:, c, c0:c0 + n], start=(c == 0), stop=(c == 7), reads=[wup] + rg, writes=[pf])
                    P.op("act", "activation", out=s_[:, 0:n], in_=pf[:, 0:n], func=AF.Silu, reads=[pf], writes=[s_])
                    P.op("dve", "tensor_tensor", out=actT[:, f, 0:n], in0=pf[:, 256:256 + n], in1=s_[:, 0:n], op=ALU.mult, reads=[pf, s_], writes=[actT])
                for j in range(nt):
                    ti = t0 + j
                    P.dma("sp", hb[:], O["y"][ti * 128:(ti + 1) * 128, :], reads=[Oy[ti]], writes=[hb])
                    for hf in range(2):
                        for f in range(NFF):
                            P.op("pe", "matmul", out=pO[hf][:, :], lhsT=actT[:, f, j * 128:(j + 1) * 128], rhs=wd[:, f, hf * 512:(hf + 1) * 512], start=(f == 0), stop=(f == NFF - 1),
                                 reads=[actT, wd], writes=[pO[hf]])
                        P.op("dve", "tensor_tensor", out=hb[:, hf * 512:(hf + 1) * 512], in0=pO[hf][:, :], in1=hb[:, hf * 512:(hf + 1) * 512], op=ALU.add, reads=[pO[hf], hb], writes=[hb])
                    P.op("act", "activation", out=jk2[:], in_=hb[:], func=AF.Square, accum_out=ss3[:, 0:1], reads=[hb], writes=[jk2, ss3])
                    P.op("act", "activation", out=ss3[:, 1:2], in_=ss3[:, 0:1], func=AF.Sqrt, scale=1.0 / D, bias=1e-6, reads=[ss3], writes=[ss3])
                    P.op("dve", "reciprocal", out=ss3[:, 1:2], in_=ss3[:, 1:2], reads=[ss3], writes=[ss3])
                    P.op("dve", "scalar_tensor_tensor", out=hb[:], in0=hb[:], scalar=ss3[:, 1:2], in1=nfb[:], op0=ALU.mult, op1=ALU.mult, reads=[hb, ss3, nfb], writes=[hb])
                    P.dma("sp", O["y"][ti * 128:(ti + 1) * 128, :], hb[:], reads=[hb], writes=[Oy[ti]])
            P.barrier()
        P.finish()
    return nc


def make_consts():
    c = {}
    c["ident_bf"] = np.eye(128, dtype=np.float32).astype(ml_dtypes.bfloat16)
    c["ident_f"] = np.eye(128, dtype=np.float32)
    s = np.arange(128)[:, None]
    t = np.arange(128)[None, :]
    c["tri"] = (s <= t).astype(np.float32)
    c["m_su"] = (s < t).astype(np.float32).astype(ml_dtypes.bfloat16)
    c["m_ui"] = (s <= t).astype(np.float32).astype(ml_dtypes.bfloat16)
    c["m_sl"] = (s > t).astype(np.float32).astype(ml_dtypes.bfloat16)
    el = np.zeros((128, 2), np.float32)
    el[127, 0] = 1.0
    el[31, 1] = 1.0
    c["elast"] = el
    bs = np.zeros((128, 4), np.float32)
    for b in range(4):
        bs[32 * b:32 * b + 32, b] = 1.0
    c["bsel"] = bs
    c["pow2"] = np.tile((0.5 ** np.arange(1, 33)).astype(np.float32)[None, :], (128, 1))
    return c


def core_inputs(inp, c):
    f = lambda a: np.ascontiguousarray(np.asarray(a, dtype=np.float32))
    m = {}
    m["x"] = np.concatenate([f(inp["x_prompt"][c]), f(inp["x_sample"][4 * c:4 * c + 4]).reshape(128, D)], axis=0)
    m["ck"] = f(inp["cache_k"][0, 4 * c:4 * c + 4]).reshape(SB, PAST, 128)
    m["cv"] = f(inp["cache_v"][0, 4 * c:4 * c + 4]).reshape(SB, PAST, 128)
    m["cik"] = f(inp["cache_idx_k"][0, 4 * c:4 * c + 4])
    m["swkv"] = f(inp["state_wkv"][0, 4 * c:4 * c + 4])
    m["sshift"] = f(inp["state_shift"][0, 4 * c:4 * c + 4]).reshape(SB, U_W)
    for n in ("norm1", "w_in", "idx_k_g", "idx_k_b", "shift_mu", "w0", "w2", "a0", "a2", "g2", "k_k", "k_a", "r_k", "gn_w", "gn_b",
              "w_oa", "w_ob", "w_out", "norm2", "w_gate", "w_up", "w_down"):
        m[n] = f(inp[n][0])
    m["norm_f"] = f(inp["norm_f"])
    m.update(make_consts())
    return m


def assemble(res):
    n = len(res)
    B = 8
    y_p = np.stack([r["y"][0:2048] for r in res])
    y_s = np.concatenate([r["y"][2048:].reshape(4, 32, D) for r in res])
    k_p = np.stack([r["ko"][0:2048].reshape(2048, 2, 64) for r in res])[None]
    v_p = np.stack([r["vo"][0:2048].reshape(2048, 2, 64) for r in res])[None]
    ik_p = np.stack([r["iko"][0:2048] for r in res])[None]
    wkv_p = np.stack([r["wkv_p"] for r in res])[None]
    sh_p = np.stack([r["shift_p"] for r in res])[None]
    k_s = np.concatenate([r["ko"][2048:].reshape(4, 32, 2, 64) for r in res])[None]
    v_s = np.concatenate([r["vo"][2048:].reshape(4, 32, 2, 64) for r in res])[None]
    ik_s = np.concatenate([r["iko"][2048:].reshape(4, 32, 64) for r in res])[None]
    wkv_s = np.concatenate([r["wkv_s"] for r in res])[None]
    sh_s = np.concatenate([r["shift_s"].reshape(4, 1, U_W) for r in res])[None]
    outs = (y_p, y_s, k_p, v_p, ik_p, wkv_p, sh_p, k_s, v_s, ik_s, wkv_s, sh_s)
    return tuple(np.ascontiguousarray(o.astype(np.float32)) for o in outs)


def kernel(**inputs):
    nc = build()
    in_maps = [core_inputs(inputs, c) for c in range(8)]
    res = run_bass_kernel_spmd(nc, in_maps, core_ids=list(range(8)))
    return assemble(res.results)
```
